# Optimizing a Trainium2 kernel written in Bass

```python
import jax
import jax.numpy as jnp
from jax import lax
import numpy as np

D_MODEL = 1024
BATCH = 16
SEQ = 2048
DEPTH = 4

CTX_LEN = 256
GRID_W = 64
N_HEADS = 16
HEAD_DIM = D_MODEL // N_HEADS
D_FF = 2816
N_MIXERS = 2
NA_KH_MAX = 8
NA_KW = 16
NA_QC = 16
NA_KBW = 2 * NA_KW
CONV_W = 3
ROPE_BASE = 10000.0
N_MOD = 9
N_NA_LAYERS = (DEPTH + N_MIXERS - 1) // N_MIXERS
N_CONV_LAYERS = DEPTH // N_MIXERS
DN_ALPHA = (2.0 * DEPTH) ** 0.25
DN_BETA = (8.0 * DEPTH) ** -0.25
LN_EPS = 1e-6

kernel_name = "hybrid_natten_shortconv_deepnorm_dit"


def layer_norm(x, g, b):
    xf = x.astype(jnp.float32)
    mu = jnp.mean(xf, axis=-1, keepdims=True)
    var = jnp.mean(jnp.square(xf - mu), axis=-1, keepdims=True)
    return ((xf - mu) * lax.rsqrt(var + LN_EPS)).astype(x.dtype) * g + b


def modulate(x, shift, scale):
    return x * (1 + scale) + shift


def post_norm_update(x, y, gate, res_w, g, b):
    return layer_norm(DN_ALPHA * x + res_w * gate * y, g, b)


def swiglu_ffn(h, w_in, w_out):
    a, u = jnp.split(h @ w_in, 2, axis=-1)
    return (jax.nn.silu(a) * u) @ w_out


def half_step_ffn(x, shift, scale, gate, w_in, w_out, g, b):
    y = swiglu_ffn(modulate(x, shift, scale), w_in, w_out)
    return post_norm_update(x, y, gate, 0.5, g, b)


def axial_rope_tables(n_tok):
    t = np.arange(n_tok)
    n_freq = HEAD_DIM // 4
    inv_freq = ROPE_BASE ** (-np.arange(n_freq) / n_freq)
    ang_row = (t // GRID_W)[:, None] * inv_freq[None]
    ang_col = (t % GRID_W)[:, None] * inv_freq[None]
    return (jnp.asarray(np.cos(ang_row), jnp.float32), jnp.asarray(np.sin(ang_row), jnp.float32),
            jnp.asarray(np.cos(ang_col), jnp.float32), jnp.asarray(np.sin(ang_col), jnp.float32))


def _rotate(x, cos, sin):
    x1, x2 = jnp.split(x, 2, axis=-1)
    cos = cos[None, :, None, :].astype(x.dtype)
    sin = sin[None, :, None, :].astype(x.dtype)
    return jnp.concatenate([x1 * cos - x2 * sin, x2 * cos + x1 * sin], axis=-1)


def apply_axial_rope(x, tables):
    cos_r, sin_r, cos_c, sin_c = tables
    half = HEAD_DIM // 2
    return jnp.concatenate([_rotate(x[..., :half], cos_r, sin_r),
                            _rotate(x[..., half:], cos_c, sin_c)], axis=-1)


def neighbourhood_attention(hx, hc, w_qkv, w_out, rpb, rope, with_ctx_queries):
    B, S, D = hx.shape
    rows = S // GRID_W
    kh = min(NA_KH_MAX, rows)
    ncb = GRID_W // NA_QC
    nk = kh * NA_KBW
    scale = HEAD_DIM ** -0.5

    qkv = (hx @ w_qkv).reshape(B, S, 3, N_HEADS, HEAD_DIM)
    q = apply_axial_rope(qkv[:, :, 0], rope)
    k = apply_axial_rope(qkv[:, :, 1], rope)
    v = qkv[:, :, 2]
    qkv_c = (hc @ w_qkv).reshape(B, hc.shape[1], 3, N_HEADS, HEAD_DIM)
    q_c, k_c, v_c = qkv_c[:, :, 0], qkv_c[:, :, 1], qkv_c[:, :, 2]

    q_col = np.arange(ncb)[:, None] * NA_QC + np.arange(NA_QC)[None]
    win_c0 = np.clip(q_col - NA_KW // 2, 0, GRID_W - NA_KW)
    blk_c0 = np.clip(np.arange(ncb) * NA_QC - NA_KW // 2, 0, GRID_W - NA_KBW)
    key_col = blk_c0[:, None] + np.arange(NA_KBW)[None]
    col_ok = (key_col[:, None, :] >= win_c0[..., None]) & (key_col[:, None, :] < win_c0[..., None] + NA_KW)
    mask = jnp.asarray(np.broadcast_to(col_ok[:, :, None, :], (ncb, NA_QC, kh, NA_KBW)).reshape(ncb, NA_QC, nk))
    dx_idx = np.clip(key_col[:, None, :] - q_col[..., None] + NA_KW - 1, 0, 2 * NA_KW - 2)
    dx_b = np.broadcast_to(dx_idx[:, :, None, :], (ncb, NA_QC, kh, NA_KBW))

    k_grid = k.reshape(B, rows, GRID_W, N_HEADS, HEAD_DIM)
    v_grid = v.reshape(B, rows, GRID_W, N_HEADS, HEAD_DIM)
    q_rows = jnp.moveaxis(q.reshape(B, rows, ncb, NA_QC, N_HEADS, HEAD_DIM), 1, 0)

    def row_block(args):
        r, q_r = args
        rs = jnp.clip(r - kh // 2, 0, rows - kh)
        k_rows = lax.dynamic_slice_in_dim(k_grid, rs, kh, axis=1)
        v_rows = lax.dynamic_slice_in_dim(v_grid, rs, kh, axis=1)
        k_blk = jnp.moveaxis(k_rows[:, :, key_col], 2, 1).reshape(B, ncb, nk, N_HEADS, HEAD_DIM)
        v_blk = jnp.moveaxis(v_rows[:, :, key_col], 2, 1).reshape(B, ncb, nk, N_HEADS, HEAD_DIM)
        dy_idx = rs + jnp.arange(kh) - r + NA_KH_MAX - 1
        bias = rpb[:, dy_idx[None, None, :, None], dx_b].reshape(N_HEADS, ncb, NA_QC, nk)
        s_win = jnp.einsum('bjqhd,bjkhd->bhjqk', q_r, k_blk).astype(jnp.float32) * scale
        s_win = jnp.where(mask, s_win + bias.astype(jnp.float32)[None], -jnp.inf)
        s_ctx = jnp.einsum('bjqhd,bchd->bhjqc', q_r, k_c).astype(jnp.float32) * scale
        p = jax.nn.softmax(jnp.concatenate([s_win, s_ctx], axis=-1), axis=-1).astype(v.dtype)
        return (jnp.einsum('bhjqk,bjkhd->bjqhd', p[..., :nk], v_blk)
                + jnp.einsum('bhjqc,bchd->bjqhd', p[..., nk:], v_c))

    o = lax.map(row_block, (jnp.arange(rows), q_rows))
    y_lat = jnp.moveaxis(o, 0, 1).reshape(B, S, D) @ w_out

    y_ctx = None
    if with_ctx_queries:
        s = jnp.einsum('bqhd,bkhd->bhqk', q_c, k_c).astype(jnp.float32) * scale
        p = jax.nn.softmax(s, axis=-1).astype(v_c.dtype)
        y_ctx = jnp.einsum('bhqk,bkhd->bqhd', p, v_c).reshape(B, hc.shape[1], D) @ w_out
    return y_lat, y_ctx


def short_conv_mixer(h, w_in, conv_w, w_out):
    b_gate, c_gate, hv = jnp.split(h @ w_in, 3, axis=-1)
    u = c_gate * hv
    n = u.shape[1]
    pad = CONV_W // 2
    up = jnp.pad(u, ((0, 0), (pad, pad), (0, 0)))
    y = conv_w[0] * up[:, 0:n]
    for j in range(1, CONV_W):
        y = y + conv_w[j] * up[:, j:j + n]
    return (b_gate * y) @ w_out


def setup_inputs(seed: int = 0) -> dict:
    key = jax.random.key(seed)
    ks = jax.random.split(key, 16)
    D, F = D_MODEL, D_FF

    def nrm(k, shape, s):
        return jax.random.normal(k, shape, jnp.float32) * s

    return {
        "x": nrm(ks[0], (BATCH, SEQ, D), 1.0),
        "c": nrm(ks[1], (BATCH, D), 1.0),
        "ctx": nrm(ks[2], (BATCH, CTX_LEN, D), 1.0),
        "c_ctx": nrm(ks[3], (D,), 1.0),
        "ada_w": nrm(ks[4], (DEPTH, D, N_MOD * D), 0.5 * D ** -0.5),
        "ada_b": nrm(ks[5], (DEPTH, N_MOD * D), 0.02),
        "ln_g": 1.0 + nrm(ks[6], (DEPTH, 3, D), 0.02),
        "ln_b": nrm(ks[7], (DEPTH, 3, D), 0.02),
        "ffn_w_in": nrm(ks[8], (DEPTH, 2, D, 2 * F), D ** -0.5),
        "ffn_w_out": nrm(ks[9], (DEPTH, 2, F, D), DN_BETA * F ** -0.5),
        "na_w_qkv": nrm(ks[10], (N_NA_LAYERS, D, 3 * D), D ** -0.5),
        "na_w_out": nrm(ks[11], (N_NA_LAYERS, D, D), DN_BETA * D ** -0.5),
        "na_rpb": nrm(ks[12], (N_NA_LAYERS, N_HEADS, 2 * NA_KH_MAX - 1, 2 * NA_KW - 1), 0.1),
        "sc_w_in": nrm(ks[13], (N_CONV_LAYERS, D, 3 * D), D ** -0.5),
        "sc_conv": nrm(ks[14], (N_CONV_LAYERS, CONV_W, D), CONV_W ** -0.5),
        "sc_w_out": nrm(ks[15], (N_CONV_LAYERS, D, D), DN_BETA * D ** -0.5),
    }


def reference(x, c, ctx, c_ctx, ada_w, ada_b, ln_g, ln_b, ffn_w_in, ffn_w_out,
              na_w_qkv, na_w_out, na_rpb, sc_w_in, sc_conv, sc_w_out):
    B, S, D = x.shape
    rope = axial_rope_tables(S)
    last_na = max(range(0, DEPTH, N_MIXERS))
    s_c = jax.nn.silu(c)
    s_cc = jax.nn.silu(c_ctx)
    for i in range(DEPTH):
        ctx_in = i <= last_na
        ctx_out = i < last_na
        j = i // N_MIXERS
        m = (s_c @ ada_w[i] + ada_b[i]).reshape(B, N_MOD, 1, D)
        mc = (s_cc @ ada_w[i] + ada_b[i]).reshape(N_MOD, D)

        x = half_step_ffn(x, m[:, 0], m[:, 1], m[:, 2], ffn_w_in[i, 0], ffn_w_out[i, 0], ln_g[i, 0], ln_b[i, 0])
        if ctx_in:
            ctx = half_step_ffn(ctx, mc[0], mc[1], mc[2], ffn_w_in[i, 0], ffn_w_out[i, 0], ln_g[i, 0], ln_b[i, 0])

        hx = modulate(x, m[:, 3], m[:, 4])
        y_ctx = None
        if i % N_MIXERS == 0:
            hc = modulate(ctx, mc[3], mc[4])
            y_lat, y_ctx = neighbourhood_attention(hx, hc, na_w_qkv[j], na_w_out[j], na_rpb[j], rope, ctx_out)
        else:
            y_lat = short_conv_mixer(hx, sc_w_in[j], sc_conv[j], sc_w_out[j])
            if ctx_out:
                y_ctx = short_conv_mixer(modulate(ctx, mc[3], mc[4]), sc_w_in[j], sc_conv[j], sc_w_out[j])
        x = post_norm_update(x, y_lat, m[:, 5], 1.0, ln_g[i, 1], ln_b[i, 1])
        if ctx_out:
            ctx = post_norm_update(ctx, y_ctx, mc[5], 1.0, ln_g[i, 1], ln_b[i, 1])

        x = half_step_ffn(x, m[:, 6], m[:, 7], m[:, 8], ffn_w_in[i, 1], ffn_w_out[i, 1], ln_g[i, 2], ln_b[i, 2])
        if ctx_out:
            ctx = half_step_ffn(ctx, mc[6], mc[7], mc[8], ffn_w_in[i, 1], ffn_w_out[i, 1], ln_g[i, 2], ln_b[i, 2])
    return x
```

```python
import numpy as np
import concourse.bass as bass
import concourse.mybir as mybir
from concourse.bass_utils import run_bass_kernel_spmd

F32 = mybir.dt.float32
BF16 = mybir.dt.bfloat16
AF = mybir.ActivationFunctionType
ALU = mybir.AluOpType

D = 1024
NCH = 8
TL = 2048
TCX = 256
T = TL + TCX
FF = 2816
BLK = [2, 4, 4, 4, 4, 4]
BOFF = [0, 2, 6, 10, 14, 18]
NFB = len(BLK)
DEPTH = 4
ALPHA = float((2.0 * DEPTH) ** 0.25)
EPS = 1e-6
NEG = -30000.0
GRID_W = 64
TILES = [(0, 512), (512, 512), (1024, 512), (1536, 512), (2048, 256)]
TBW = 640 + 4 * 512
ARENA = 46592


class Op:
    __slots__ = ("eng", "fn", "deps", "dma", "idx", "waits", "sig", "sigval")


class Prog:
    ENGS = ["pe", "act", "dve", "pool", "sp"]

    def __init__(self):
        self.ops = []
        self.lastw = {}
        self.readers = {}
        self.pending_bar = {}
        self.last_of = {}
        self.dmas_since_bar = []
        self.ranges = {}
        self.stale = []
        self.stale_done = set()
        self.phase_no = 0

    def add(self, eng, fn, r=(), w=(), dma=None):
        op = Op()
        op.eng, op.fn, op.dma, op.idx = eng, fn, dma, len(self.ops)
        deps = {}
        for t in r:
            x = self.lastw.get(t)
            if x is not None:
                deps[x] = "raw"
        for t in w:
            x = self.lastw.get(t)
            if x is not None and x not in deps:
                deps[x] = "waw"
            for y in self.readers.get(t, ()):
                if y not in deps:
                    deps[y] = "war"
        if eng in self.pending_bar:
            for x in self.pending_bar.pop(eng):
                deps[x] = "bar" if deps.get(x) != "raw" else "raw"
        if self.stale:
            for t in list(r) + list(w):
                rg = self.ranges.get(t[0])
                if rg is None or (eng, t[0]) in self.stale_done:
                    continue
                self.stale_done.add((eng, t[0]))
                for (_, a, b_, opset) in self.stale:
                    if a < rg[1] and rg[0] < b_:
                        for x in opset:
                            if x not in deps:
                                deps[x] = "war"
        op.deps = deps
        for t in r:
            self.readers.setdefault(t, []).append(op.idx)
        for t in w:
            self.lastw[t] = op.idx
            self.readers[t] = []
        self.ops.append(op)
        if dma is None:
            self.last_of[eng] = op.idx
        else:
            self.dmas_since_bar.append(op.idx)
        return op.idx

    def new_phase(self):
        self.phase_no += 1
        byname = {}
        for t in list(self.lastw.keys()):
            if t[0] in self.ranges:
                byname.setdefault(t[0], set()).add(self.lastw.pop(t))
        for t in list(self.readers.keys()):
            if t[0] in self.ranges:
                byname.setdefault(t[0], set()).update(self.readers.pop(t))
        for nm, opset in byname.items():
            a, b_ = self.ranges[nm]
            self.stale.append((self.phase_no, a, b_, opset))
        self.stale = [x for x in self.stale if x[0] >= self.phase_no - 2]
        self.ranges = {}
        self.stale_done = set()

    def barrier(self):
        deps = set(self.last_of.values()) | set(self.dmas_since_bar)
        self.pending_bar = {e: set(deps) for e in self.ENGS}
        self.dmas_since_bar = []
        self.lastw = {}
        self.readers = {}

    def emit(self, nc, block, sems, dma_sems):
        ops = self.ops
        need = set()
        for op in ops:
            waits = []
            for d, kind in op.deps.items():
                dop = ops[d]
                if dop.dma is not None:
                    waits.append(d)
                elif dop.eng == op.eng:
                    if op.dma is not None:
                        waits.append(d)
                    elif op.eng == "pe":
                        continue
                    elif kind == "raw":
                        waits.append(d)
                else:
                    waits.append(d)
            op.waits = waits
            for d in waits:
                if ops[d].dma is None:
                    need.add(d)
        cnt = {e: 0 for e in self.ENGS}
        dcnt = {}
        for op in ops:
            if op.dma is not None:
                dcnt[op.dma] = dcnt.get(op.dma, 0) + 16
                op.sigval = dcnt[op.dma]
                op.sig = True
            elif op.idx in need:
                cnt[op.eng] += 1
                op.sigval = cnt[op.eng]
                op.sig = True
            else:
                op.sig = False
                op.sigval = 0
        self.final_counts = (cnt, dcnt)

        def run_engine(engname, eng):
            waited = {}
            for op in ops:
                if op.eng != engname:
                    continue
                for d in op.waits:
                    dop = ops[d]
                    key = ("d", dop.dma) if dop.dma is not None else ("e", dop.eng)
                    if waited.get(key, 0) < dop.sigval:
                        s = dma_sems[dop.dma] if dop.dma is not None else sems[dop.eng]
                        eng.wait_ge(s, dop.sigval)
                        waited[key] = dop.sigval
                ins = op.fn(eng)
                if op.sig:
                    if op.dma is not None:
                        ins.then_inc(dma_sems[op.dma], 16)
                    else:
                        ins.then_inc(sems[op.eng], 1)
            for e in self.ENGS:
                if cnt[e] > 0 and waited.get(("e", e), 0) < cnt[e] and e != engname:
                    pass
            if engname == "sp":
                for k, v in dcnt.items():
                    if k.startswith("out"):
                        eng.wait_ge(dma_sems[k], v)

        @block.tensor
        def _(e):
            run_engine("pe", e)

        @block.scalar
        def _(e):
            run_engine("act", e)

        @block.vector
        def _(e):
            run_engine("dve", e)

        @block.gpsimd
        def _(e):
            run_engine("pool", e)

        @block.sync
        def _(e):
            run_engine("sp", e)


def sublayer_tiles(l, s, what):
    if l <= 1:
        return 5
    if l == 2:
        if s == 0:
            return 5
        if s == 1:
            return 5 if what == "in" else 4
        return 4
    return 4


def build_program(stop=None, nb=2, debug=False):
    nc = bass.Bass("TRN2", target_bir_lowering=False)

    def dram(name, shape, kind="ExternalInput", dt=F32):
        return nc.dram_tensor(name, list(shape), dt, kind=kind).ap()

    xin = dram("xin", [2, 128, NCH, T])
    cv = dram("cv", [128, NCH, 3])
    adaw = dram("adaw", [DEPTH, 18, 128, NCH * 512])
    adab = dram("adab", [128, DEPTH, 72])
    lng = dram("lng", [128, DEPTH, 3, NCH])
    lnb = dram("lnb", [128, DEPTH, 3, NCH])
    win = dram("win", [DEPTH, 2, NFB, 128, NCH * 1024])
    wout = dram("wout", [DEPTH, 2, NFB, 128, 4 * 1024])
    wqkv = dram("wqkv", [2, 8, 128, NCH * 384])
    wo = dram("wo", [2, 4, 128, 2 * 1024])
    tbias = dram("tbias", [2, 8, 128, 2 * TBW])
    wcin = dram("wcin", [2, 8, 128, NCH * 384])
    wco = dram("wco", [2, 4, 128, 2 * 1024])
    convw = dram("convw", [128, 2, 3, NCH])
    ropec = dram("ropec", [128, TL])
    ropes = dram("ropes", [128, TL])
    ident = dram("ident", [128, 128])
    out = dram("out", [2, 128, NCH, TL], kind="ExternalOutput")
    if debug:
        dbg_qt = dram("dbg_qt", [128, T], kind="ExternalOutput", dt=BF16)
        dbg_kt = dram("dbg_kt", [128, T], kind="ExternalOutput", dt=BF16)
        dbg_vc = dram("dbg_vc", [128, 18 * 192], kind="ExternalOutput", dt=BF16)
        dbg_ot = dram("dbg_ot", [128, T], kind="ExternalOutput", dt=BF16)
        dbg_xm = dram("dbg_xm", [128, NCH, T], kind="ExternalOutput", dt=BF16)
        dbg_x = dram("dbg_x", [128, NCH, T], kind="ExternalOutput")
        dbg_mod = dram("dbg_mod", [128, DEPTH, 72, 3], kind="ExternalOutput")

    P = Prog()
    ctxs = []

    def sb(name, shape, dt):
        cm = nc.sbuf_tensor(name, list(shape), dt)
        t = cm.__enter__()
        ctxs.append(cm)
        return t

    X = sb("X", [128, NCH, T], F32)
    XM = sb("XM", [128, NCH, T], BF16)
    AR = sb("AR", [128, ARENA], BF16)
    MOD = sb("MOD", [128, DEPTH, 72, 3], F32)
    ADAB = sb("ADAB", [128, DEPTH, 72], F32)
    LNG = sb("LNG", [128, DEPTH, 3, NCH], F32)
    LNB = sb("LNB", [128, DEPTH, 3, NCH], F32)
    CONVW = sb("CONVW", [128, 2, 3, NCH], F32)
    IDB = sb("IDB", [128, 128], BF16)
    IDF = sb("IDF", [128, 128], F32)
    ONESD = sb("ONESD", [128, 128], BF16)
    MHALF = sb("MHALF", [128, 1], F32)
    SVS = [sb("SV%d" % i, [128, 2, 6, NCH], F32) for i in range(2)]
    SVI = sb("SVI", [128, 2, NCH], F32)
    cur = {"k": 0}
    SC = sb("SC", [128, NCH, 3], F32)
    CVT = sb("CVT", [128, NCH, 3], F32)
    cm = nc.psum_tensor("PS", [128, 8, 512], F32)
    PS = cm.__enter__()
    ctxs.append(cm)

    K_GT, K_GA, K_BA, K_GP, K_BP, K_TMP = range(6)

    class Carver:
        def __init__(self, off=0):
            self.off = off

        def bf(self, n, name=None):
            a = AR[:, self.off:self.off + n]
            if name is not None:
                P.ranges[name] = (self.off, self.off + n)
            self.off += n
            assert self.off <= ARENA, self.off
            return a

        def f32(self, n, name=None):
            if self.off % 2:
                self.off += 1
            a = AR[:, self.off:self.off + 2 * n].bitcast(F32)
            if name is not None:
                P.ranges[name] = (self.off, self.off + 2 * n)
            self.off += 2 * n
            assert self.off <= ARENA, self.off
            return a

    A_LOW = 0
    A_W = 19584
    A_S = 33152

    def bank(i):
        return PS[:, i, :]

    def pst(i):
        return ("ps", i)

    P.add("sp", lambda e: e.dma_start(out=CVT[:], in_=cv), w=[("CVT",)], dma="misc_cvt")
    P.add("sp", lambda e: e.dma_start(out=ADAB[:], in_=adab), w=[("ADAB",)], dma="misc_adab")
    P.add("sp", lambda e: e.dma_start(out=LNG[:], in_=lng), w=[("LNG",)], dma="misc_lng")
    P.add("sp", lambda e: e.dma_start(out=LNB[:], in_=lnb), w=[("LNB",)], dma="misc_lnb")
    P.add("sp", lambda e: e.dma_start(out=CONVW[:], in_=convw), w=[("CONVW",)], dma="misc_convw")
    P.add("pool", lambda e: e.dma_start(out=IDB[:], in_=ident), w=[("IDB",)], dma="misc2")
    P.add("sp", lambda e: e.dma_start(out=IDF[:], in_=ident), w=[("IDF",)], dma="misc_idf")
    P.add("dve", lambda e: e.memset(ONESD[:], 1.0 / 1024.0), w=[("ONESD",)])
    P.add("dve", lambda e: e.memset(MHALF[:], -0.5), w=[("MHALF",)])
    P.add("act", lambda e: e.activation(out=SC[:], in_=CVT[:], func=AF.Silu), r=[("CVT",)], w=[("SC",)])

    cz = Carver()
    AW = [cz.f32(NCH * 512, "aw%d" % i).rearrange("p (k n) -> p k n", k=NCH) for i in range(2)]
    n_ada = 0
    MT = [cz.f32(512, "mt%d" % i) for i in range(2)]
    layers_needed = DEPTH if stop is None else stop[0] + 2
    for l in range(min(DEPTH, layers_needed)):
        for jb in range(18):
            slot = n_ada % 2
            P.add("sp", lambda e, l=l, jb=jb, slot=slot: e.dma_start(
                out=AW[slot], in_=adaw[l, jb].rearrange("p (k n) -> p k n", k=NCH)),
                w=[("aw%d" % slot,)], dma="aw%d" % slot)
            bk = (2 * n_ada) % 8
            bk2 = bk + 1

            def mm(e, slot=slot, bk=bk):
                ins = None
                for k in range(NCH):
                    ins = e.matmul(PS[0:3, bk, :], lhsT=SC[:, k, :], rhs=AW[slot][:, k, :],
                                   start=(k == 0), stop=(k == NCH - 1))
                return ins
            P.add("pe", mm, r=[("aw%d" % slot,), ("SC",)], w=[pst(bk)])
            P.add("act", lambda e, slot=slot, bk=bk: e.activation(out=MT[slot][0:3, :], in_=PS[0:3, bk, :], func=AF.Copy),
                  w=[pst(bk), ("mt%d" % slot,)])

            def tr(e, slot=slot, bk2=bk2):
                ins = None
                for fcl in range(4):
                    ins = e.transpose(PS[:, bk2, fcl * 3:(fcl + 1) * 3], MT[slot][0:3, fcl * 128:(fcl + 1) * 128],
                                      IDF[0:3, 0:3])
                return ins
            P.add("pe", tr, r=[("mt%d" % slot,), ("IDF",)], w=[pst(bk2)])
            P.add("dve", lambda e, l=l, jb=jb, bk2=bk2: e.tensor_tensor(
                MOD[:, l, jb * 4:(jb + 1) * 4, :], PS[:, bk2, 0:12].rearrange("p (f c) -> p f c", c=3),
                ADAB[:, l, jb * 4:(jb + 1) * 4].unsqueeze(2).to_broadcast([128, 4, 3]), op=ALU.add),
                r=[("ADAB",)], w=[pst(bk2), ("MOD",)])
            n_ada += 1
    P.barrier()

    def modv(l, mv, col):
        return MOD[:, l, mv * 8:(mv + 1) * 8, col]

    def next_sub(l, s):
        if s < 2:
            return (l, s + 1)
        if l + 1 < DEPTH:
            return (l + 1, 0)
        return None

    def emit_sv(b, l, s, last):
        w = 0.5 if s != 1 else 1.0
        nxt = None if last else next_sub(l, s)
        cur["k"] += 1
        par = cur["k"] % 2
        SV = SVS[par]
        cur["sv"] = SV
        cur["tag"] = "SVR%d" % par
        tg = lambda x: ("%s_%d" % (x, par),)
        for ci, col in enumerate((b, 2)):
            P.add("dve", lambda e, ci=ci, col=col: e.tensor_scalar(
                SV[:, ci, K_GT, :], modv(l, 3 * s + 2, col), w, None, op0=ALU.mult), w=[tg("SV"), (cur["tag"],)])
            al = ALPHA if nxt is not None else 1.0
            P.add("dve", lambda e, ci=ci, al=al: e.tensor_scalar(
                SV[:, ci, K_GA, :], LNG[:, l, s, :], al, None, op0=ALU.mult), w=[tg("SV")])
            P.add("dve", lambda e, ci=ci, al=al: e.tensor_scalar(
                SV[:, ci, K_BA, :], LNB[:, l, s, :], al, None, op0=ALU.mult), w=[tg("SV")])
            if nxt is not None:
                nl, ns = nxt
                P.add("dve", lambda e, ci=ci, col=col, nl=nl, ns=ns: e.tensor_scalar(
                    SV[:, ci, K_TMP, :], modv(nl, 3 * ns + 1, col), 1.0, None, op0=ALU.add), w=[tg("SV")])
                P.add("dve", lambda e, ci=ci: e.tensor_tensor(
                    SV[:, ci, K_GP, :], LNG[:, l, s, :], SV[:, ci, K_TMP, :], op=ALU.mult),
                    r=[tg("SV")], w=[tg("SV2")])
                P.add("dve", lambda e, ci=ci: e.tensor_tensor(
                    SV[:, ci, K_BP, :], LNB[:, l, s, :], SV[:, ci, K_TMP, :], op=ALU.mult),
                    r=[tg("SV")], w=[tg("SV3")])
                P.add("dve", lambda e, ci=ci, col=col, nl=nl, ns=ns: e.tensor_tensor(
                    SV[:, ci, K_BP, :], SV[:, ci, K_BP, :], modv(nl, 3 * ns, col), op=ALU.add),
                    r=[tg("SV3")], w=[tg("SV4")])
        tag = cur["tag"]
        P.add("dve", lambda e: e.memset(SV[:, 0, K_TMP, 0:1], 0.0),
              r=[tg("SV"), tg("SV2"), tg("SV3"), tg("SV4")], w=[(tag,)])
        return (SV, tag)

    def svv(sv, tt, kind, c):
        ci = 1 if tt == 4 else 0
        return sv[0][:, ci, kind, c:c + 1]

    def xtok(tt):
        return [("X", c, tt) for c in range(NCH)]

    def xmtok(tt):
        return [("XM", c, tt) for c in range(NCH)]

    def emit_ln_front_a(tt, lns):
        t0, n = TILES[tt]
        ZB, SQ, MEAN, MSQ, VARE, RSTD = lns
        P.add("dve", lambda e: e.tensor_copy(ZB[:, :, :n], X[:, :, t0:t0 + n]), r=xtok(tt), w=[("ZB",)])
        P.add("act", lambda e: e.activation(out=SQ[:, :, :n], in_=X[:, :, t0:t0 + n], func=AF.Square),
              r=xtok(tt), w=[("SQ",)])

    def emit_ln_front(tt, lns):
        emit_ln_front_a(tt, lns)
        emit_ln_front_b(tt, lns)

    def emit_ln_front_b(tt, lns):
        t0, n = TILES[tt]
        ZB, SQ, MEAN, MSQ, VARE, RSTD = lns

        def st(e, src, bk):
            ins = None
            for c in range(NCH):
                ins = e.matmul(PS[:, bk, :n], lhsT=ONESD[:], rhs=src[:, c, :n], start=(c == 0), stop=(c == NCH - 1))
            return ins
        P.add("pe", lambda e: st(e, ZB, 4), r=[("ZB",), ("ONESD",)], w=[pst(4)])
        P.add("pe", lambda e: st(e, SQ, 5), r=[("SQ",), ("ONESD",)], w=[pst(5)])

        def a1(e):
            e.activation(out=MEAN[:, :n], in_=PS[:, 4, :n], func=AF.Copy)
            return e.activation(out=MSQ[:, :n], in_=PS[:, 4, :n], func=AF.Square)
        P.add("act", a1, w=[pst(4), ("MEAN",), ("MSQ",)])
        P.add("dve", lambda e: e.scalar_tensor_tensor(
            out=VARE[:, :n], in0=PS[:, 5, :n], scalar=EPS, in1=MSQ[:, :n], op0=ALU.add, op1=ALU.subtract),
            r=[("MSQ",)], w=[pst(5), ("VARE",)])

        P.add("act", lambda e: e.activation(out=VARE[:, :n], in_=VARE[:, :n], func=AF.Ln),
              r=[("VARE",)], w=[("SD",)])
        P.add("act", lambda e: e.activation(out=RSTD[:, :n], in_=VARE[:, :n], func=AF.Exp, scale=-0.5),
              r=[("SD",)], w=[("RSTD",), ("VARE",)])

        for eng, c0, c1 in (("dve", 0, 5), ("pool", 5, 8)):
            toks = [("X", c, tt) for c in range(c0, c1)]
            P.add(eng, lambda e, c0=c0, c1=c1: e.tensor_tensor(
                X[:, c0:c1, t0:t0 + n], X[:, c0:c1, t0:t0 + n],
                MEAN[:, None, :n].to_broadcast([128, c1 - c0, n]), op=ALU.subtract),
                r=[("MEAN",)] + toks, w=toks)
            P.add(eng, lambda e, c0=c0, c1=c1: e.tensor_tensor(
                X[:, c0:c1, t0:t0 + n], X[:, c0:c1, t0:t0 + n],
                RSTD[:, None, :n].to_broadcast([128, c1 - c0, n]), op=ALU.mult),
                r=[("RSTD",)] + toks, w=toks)

    def emit_ln_apply(tt, sv, has_next):
        t0, n = TILES[tt]
        svr = (sv[1],)
        for c in range(NCH):
            gp, bp, ga, ba = svv(sv, tt, K_GP, c), svv(sv, tt, K_BP, c), svv(sv, tt, K_GA, c), svv(sv, tt, K_BA, c)

            if has_next:
                P.add("act", lambda e, c=c, gp=gp, bp=bp: e.activation(
                    out=XM[:, c, t0:t0 + n], in_=X[:, c, t0:t0 + n], func=AF.Identity, scale=gp, bias=bp),
                    r=[svr, ("X", c, tt)], w=[("XM", c, tt)])
            P.add("pool", lambda e, c=c, ga=ga, ba=ba: e.tensor_scalar(
                X[:, c, t0:t0 + n], X[:, c, t0:t0 + n], ga, ba, op0=ALU.mult, op1=ALU.add),
                r=[svr], w=[("X", c, tt)])

    def carve_lns(cz):
        ZB = cz.bf(NCH * 512, "ZB").rearrange("p (c t) -> p c t", c=NCH)
        SQ = cz.bf(NCH * 512, "SQ").rearrange("p (c t) -> p c t", c=NCH)
        MEAN = cz.f32(512, "MEAN")
        MSQ = cz.f32(512, "MSQ")
        VARE = cz.f32(512, "VARE")
        RSTD = cz.f32(512, "RSTD")
        P.ranges["SD"] = P.ranges["VARE"]
        return (ZB, SQ, MEAN, MSQ, VARE, RSTD)

    def emit_rmw(sv, tt, dc, bk):
        t0, n = TILES[tt]
        gt = svv(sv, tt, K_GT, dc)
        P.add("dve", lambda e: e.scalar_tensor_tensor(
            out=X[:, dc, t0:t0 + n], in0=PS[:, bk, :n], scalar=gt, in1=X[:, dc, t0:t0 + n],
            op0=ALU.mult, op1=ALU.add), r=[(sv[1],)], w=[pst(bk), ("X", dc, tt)])

    def emit_ffn(b, l, s, last):
        fi = 0 if s == 0 else 1
        nt_in = sublayer_tiles(l, s, "in")
        nt_out = sublayer_tiles(l, s, "out")
        nt = min(nt_in, nt_out)
        P.new_phase()
        lns = carve_lns(Carver(A_LOW))
        WIN, WOUT = [None, None], [None, None]
        cz = Carver(A_W)
        WIN[0] = cz.bf(NCH * 1024, "win0").rearrange("p (k n) -> p k n", k=NCH)
        WOUT[0] = cz.bf(4 * 1024, "wout0").rearrange("p (k n) -> p k n", k=4)
        assert cz.off <= A_S
        cz = Carver(A_S)
        WIN[1] = cz.bf(NCH * 1024, "win1").rearrange("p (k n) -> p k n", k=NCH)
        WOUT[1] = cz.bf(4 * 1024, "wout1").rearrange("p (k n) -> p k n", k=4)
        cz = Carver(12288)
        G = [cz.bf(4 * 512, "G%d" % i).rearrange("p (k n) -> p k n", k=4) for i in range(2)]
        SIL = [cz.f32(512, "SIL%d" % i) for i in range(2)]
        assert cz.off <= A_W
        sv = emit_sv(b, l, s, last)

        def load(j):
            slot = j % 2
            nbk = BLK[j]
            P.add("pool", lambda e: e.dma_start(
                out=WIN[slot][:, :, 0:nbk * 256],
                in_=win[l, fi, j].rearrange("p (k n) -> p k n", k=NCH)[:, :, 0:nbk * 256]),
                w=[("win%d" % slot,)], dma="win%d" % slot)
            P.add("pool", lambda e: e.dma_start(
                out=WOUT[slot][:, 0:nbk, :], in_=wout[l, fi, j].rearrange("p (k n) -> p k n", k=4)[:, 0:nbk, :]),
                w=[("wout%d" % slot,)], dma="wout%d" % slot)

        def emit_h(j, tt, gs):
            slot = j % 2
            t0, n = TILES[tt]
            nbk = BLK[j]
            for i in range(nbk):
                for half in range(2):
                    bk = 2 * (i % 2) + half
                    co = half * nbk * 128 + i * 128

                    def mm(e, bk=bk, co=co):
                        ins = None
                        for k in range(NCH):
                            ins = e.matmul(PS[:, bk, :n], lhsT=WIN[slot][:, k, co:co + 128],
                                           rhs=XM[:, k, t0:t0 + n], start=(k == 0), stop=(k == NCH - 1))
                        return ins
                    P.add("pe", mm, r=[("win%d" % slot,)] + xmtok(tt), w=[pst(bk)])
                ip = i % 2
                P.add("act", lambda e, ip=ip: e.activation(out=SIL[ip][:, :n], in_=PS[:, 2 * ip, :n], func=AF.Silu),
                      w=[pst(2 * ip), ("SIL%d" % ip,)])
                P.add("dve", lambda e, i=i, ip=ip: e.tensor_tensor(
                    G[gs][:, i, :n], PS[:, 2 * ip + 1, :n], SIL[ip][:, :n], op=ALU.mult),
                    r=[("SIL%d" % ip,)], w=[pst(2 * ip + 1), ("G%d" % gs, i)])

        def emit_y(j, tt, gs):
            slot = j % 2
            t0, n = TILES[tt]
            for dc in range(NCH):
                bk = (6 + dc % 2) if j >= NFB - 1 else (4 + dc % 4)
                nbk = BLK[j]

                def mm(e, dc=dc, bk=bk):
                    ins = None
                    for i in range(nbk):
                        ins = e.matmul(PS[:, bk, :n], lhsT=WOUT[slot][:, i, dc * 128:(dc + 1) * 128],
                                       rhs=G[gs][:, i, :n], start=(i == 0), stop=(i == nbk - 1))
                    return ins
                P.add("pe", mm, r=[("wout%d" % slot,)] + [("G%d" % gs, i) for i in range(nbk)], w=[pst(bk)])
                emit_rmw(sv, tt, dc, bk)

        load(0)
        yield "ready"
        pending = None
        cnt = 0
        qb, qc = [], []
        for j in range(NFB):
            for tt in range(nt):
                gs = cnt % 2
                cnt += 1
                emit_h(j, tt, gs)
                fb = qb.pop(0) if qb else None
                if fb is not None:
                    emit_ln_front_b(fb, lns)
                if pending is not None:
                    emit_y(*pending)
                    if qc:
                        emit_ln_apply(qc.pop(0), sv, not last)
                    if fb is not None:
                        qc.append(fb)
                    if pending[0] == NFB - 1:
                        emit_ln_front_a(pending[1], lns)
                        qb.append(pending[1])
                pending = (j, tt, gs)
                if tt == 0 and j + 1 < NFB:
                    load(j + 1)
        emit_y(*pending)
        fb = qb.pop(0) if qb else None
        if fb is not None:
            emit_ln_front_b(fb, lns)
        if qc:
            emit_ln_apply(qc.pop(0), sv, not last)
        if fb is not None:
            qc.append(fb)
        emit_ln_front_a(pending[1], lns)
        yield "hook"
        emit_ln_front_b(pending[1], lns)
        while qc:
            emit_ln_apply(qc.pop(0), sv, not last)
        emit_ln_apply(pending[1], sv, not last)

    def emit_outproj(WOg, OT, tiles_out, lns, last, final_group, sv):
        prev = None
        for tt in range(tiles_out):
            t0, n = TILES[tt]
            for dc in range(NCH):
                bk = 6 + dc % 2

                def mm(e, dc=dc, bk=bk, t0=t0, n=n):
                    ins = None
                    for i in range(2):
                        ins = e.matmul(PS[:, bk, :n], lhsT=WOg[:, i, dc * 128:(dc + 1) * 128],
                                       rhs=OT[:, i, t0:t0 + n], start=(i == 0), stop=(i == 1))
                    return ins
                P.add("pe", mm, r=[("wo",), ("OT", 0, tt), ("OT", 1, tt)], w=[pst(bk)])
                emit_rmw(sv, tt, dc, bk)
            if final_group:
                if tt >= 1:
                    emit_ln_front_b(tt - 1, lns)
                if tt >= 2:
                    emit_ln_apply(tt - 2, sv, not last)
                if tt == tiles_out - 1:
                    yield "hook"
                emit_ln_front_a(tt, lns)
        if final_group:
            lt = tiles_out - 1
            emit_ln_front_b(lt, lns)
            if lt >= 1:
                emit_ln_apply(lt - 1, sv, not last)
            emit_ln_apply(lt, sv, not last)

    def emit_na(b, l, last):
        jn = l // 2
        nt_in = 5
        nt_out = sublayer_tiles(l, 1, "out")
        ctxq = (l == 0)
        P.new_phase()
        lns = carve_lns(Carver(A_LOW))
        cz = Carver(A_LOW)
        QT = cz.bf(2 * T, "QT").rearrange("p (h t) -> p h t", h=2)
        P.ranges["QTZ"] = P.ranges["QT"]
        KT = cz.bf(T, "KT")
        VC = cz.bf(18 * 192, "VC").rearrange("p (c n) -> p c n", c=18)
        P.ranges["VC1"] = P.ranges["VC"]
        PT = [cz.bf(8 * 128, "PT%d" % i) for i in range(4)]
        T1 = [cz.f32(512, "T1%d" % i) for i in range(2)]
        T2 = [cz.f32(512, "T2%d" % i) for i in range(2)]
        REC = [cz.f32(256, "REC%d" % i) for i in range(2)]
        assert cz.off <= A_W
        cz = Carver(A_W)
        WQ = [cz.bf(NCH * 384, "wq%d" % i).rearrange("p (k n) -> p k n", k=NCH) for i in range(2)]
        WOg = cz.bf(2 * 1024, "wo").rearrange("p (k n) -> p k n", k=2)
        TB = cz.bf(2 * TBW, "TB").rearrange("p (h n) -> p h n", h=2)
        assert cz.off <= A_S
        cz = Carver(A_S)
        OT = cz.bf(2 * T, "OT").rearrange("p (k n) -> p k n", k=2)
        RC = cz.f32(TL, "RC")
        RS = cz.f32(TL, "RS")
        sv = emit_sv(b, l, 1, last)
        yield "ready"
        P.add("sp", lambda e: e.dma_start(out=RC, in_=ropec), w=[("RC",)], dma="ropec")
        P.add("sp", lambda e: e.dma_start(out=RS, in_=ropes), w=[("RS",)], dma="ropes")
        P.add("pool", lambda e: e.memset(VC[:, :, 64:128], 1.0), w=[("VC1",)])
        P.add("pool", lambda e: e.memset(QT[64:128, 0, :], 0.0), w=[("QTZ",)])
        P.add("pool", lambda e: e.memset(QT[0:64, 1, :], 0.0), w=[("QTZ",)])

        def rope(tt, bk, dst, scale, which):
            t0, n = TILES[tt]
            a, bq = T1[which], T2[which]

            def d1(e):
                e.scalar_tensor_tensor(out=a[:, :n], in0=PS[:, bk, :n], scalar=scale, in1=RC[:, t0:t0 + n],
                                       op0=ALU.mult, op1=ALU.mult)
                ins = None
                for q in range(4):
                    isl = slice(32 * (q ^ 1), 32 * (q ^ 1) + 32)
                    osl = slice(32 * q, 32 * q + 32)
                    ins = e.scalar_tensor_tensor(out=bq[osl, :n], in0=PS[isl, bk, :n], scalar=scale,
                                                 in1=RS[isl, t0:t0 + n], op0=ALU.mult, op1=ALU.mult)
                return ins
            P.add("dve", d1, r=[("RC",), ("RS",)], w=[pst(bk), ("T1%d" % which,), ("T2%d" % which,)])
            if which == 1:
                P.add("pool", lambda e: e.tensor_tensor(dst[:, t0:t0 + n], a[:, :n], bq[:, :n], op=ALU.add),
                      r=[("T1%d" % which,), ("T2%d" % which,)], w=[("KT", tt)])
            else:
                def qadd(e):
                    e.tensor_tensor(QT[0:64, 0, t0:t0 + n], a[0:64, :n], bq[0:64, :n], op=ALU.add)
                    return e.tensor_tensor(QT[64:128, 1, t0:t0 + n], a[64:128, :n], bq[64:128, :n], op=ALU.add)
                P.add("pool", qadd, r=[("T1%d" % which,), ("T2%d" % which,)], w=[("QT", tt)])

        ucount = [0]

        def emit_st(u):
            kind = u["kind"]
            if kind == "ctx":
                nq, q0 = 256, TL
                tcs, boff = [], None
            else:
                p = kind
                nq, q0 = 128, 128 * p
                if 2 <= p <= 13:
                    tcs, boff = list(range(p - 2, p + 3)), 0
                elif p < 2:
                    tcs, boff = [0, 1, 2, 3], 640 + 512 * p
                else:
                    tcs, boff = [12, 13, 14, 15], 640 + 512 * (p - 12)
            allc = tcs + [16, 17]
            u["allc"], u["nq"], u["q0"] = allc, nq, q0
            w2 = 2 * nq
            per = 1024 // w2
            halves = [list(range(i, min(i + per, len(allc)))) for i in range(0, len(allc), per)]
            u["pts"] = {}
            for hi, idxs in enumerate(halves):
                buf = ucount[0] % 2
                ucount[0] += 1
                ptb = (u["n"] % 2) * 2 + hi
                stv = PS[:, 2 * buf:2 * buf + 2, :].rearrange("p b (s q) -> p (b s) q", q=w2)
                ptv = PT[ptb].rearrange("p (s q) -> p s q", q=w2)

                def mm(e, idxs=idxs, stv=stv):
                    ins = None
                    for s_, i in enumerate(idxs):
                        tc = allc[i]
                        hasb = i < len(tcs)
                        ins = e.matmul(stv[:, s_, :], lhsT=KT[:, tc * 128:(tc + 1) * 128],
                                       rhs=QT[:, :, q0:q0 + nq], start=True, stop=not hasb)
                        if hasb:
                            for hh in range(2):
                                ins = e.matmul(stv[:, s_, hh * nq:(hh + 1) * nq],
                                               lhsT=TB[:, hh, boff + i * 128:boff + (i + 1) * 128],
                                               rhs=IDB[:, :], start=False, stop=(hh == 1))
                    return ins
                banks = [pst(2 * buf), pst(2 * buf + 1)]
                P.add("pe", mm, r=[("QT", t) for t in range(5)] + [("KT", t) for t in range(5)]
                      + [("TB",), ("IDB",), ("QTZ",)], w=banks)
                ns = len(idxs)
                P.add("act", lambda e, ns=ns, stv=stv, ptv=ptv: e.activation(
                    out=ptv[:, 0:ns, :], in_=stv[:, 0:ns, :], func=AF.Exp), w=banks + [("PT%d" % ptb,)])
                for s_, i in enumerate(idxs):
                    u["pts"][i] = (ptb, ptv, s_)

        def emit_pv(u, ci):
            kind = u["kind"]
            allc, nq, q0 = u["allc"], u["nq"], u["q0"]
            tt = 4 if kind == "ctx" else kind // 4
            ptbs = sorted({v[0] for v in u["pts"].values()})
            for hh in range(2):
                hoff = 64 * hh
                osl = slice(64 * hh, 64 * hh + 64)
                rsl = slice(64 * (1 - hh), 64 * (1 - hh) + 64)
                bk = 4 + 2 * (u["n"] % 2) + hh

                def mm(e, hh=hh, hoff=hoff, bk=bk):
                    ins = None
                    for i, tc in enumerate(allc):
                        _, ptv, s_ = u["pts"][i]
                        ins = e.matmul(PS[:, bk, :nq], lhsT=VC[:, tc, hoff:hoff + 128],
                                       rhs=ptv[:, s_, hh * nq:(hh + 1) * nq],
                                       start=(i == 0), stop=(i == len(allc) - 1))
                    return ins
                P.add("pe", mm, r=[("PT%d" % x,) for x in ptbs] + [("VC1",)] + [("VC", t) for t in range(5)],
                      w=[pst(bk)])
                rb = REC[hh]

                P.add("dve", lambda e, osl=osl, rsl=rsl, bk=bk, rb=rb: e.reciprocal(rb[osl, :nq], PS[rsl, bk, :nq]),
                      w=[pst(bk), ("REC%d" % hh,)])
                P.add("dve", lambda e, osl=osl, bk=bk, rb=rb: e.tensor_tensor(
                    OT[osl, ci, q0:q0 + nq], PS[osl, bk, :nq], rb[osl, :nq], op=ALU.mult),
                    r=[("REC%d" % hh,)], w=[pst(bk), ("OT", ci, tt)])

        def load_wq(c):
            P.add("pool", lambda e: e.dma_start(
                out=WQ[c % 2], in_=wqkv[jn, c].rearrange("p (k n) -> p k n", k=NCH)),
                w=[("wq%d" % (c % 2),)], dma="wq%d" % (c % 2))
        load_wq(0)
        for c in range(8):
            slot = c % 2
            ci = c % 2
            if c + 1 < 8:
                load_wq(c + 1)
            P.add("pool", lambda e, c=c: e.dma_start(
                out=TB, in_=tbias[jn, c].rearrange("p (h n) -> p h n", h=2)), w=[("TB",)], dma="tb")
            if ci == 0:
                P.add("pool", lambda e, c=c: e.dma_start(
                    out=WOg, in_=wo[jn, c // 2].rearrange("p (k n) -> p k n", k=2)), w=[("wo",)], dma="wo")
            for tt in range(nt_in):
                t0, n = TILES[tt]
                for which in range(2):
                    def mm(e, which=which, t0=t0, n=n, slot=slot):
                        ins = None
                        for k in range(NCH):
                            ins = e.matmul(PS[:, which, :n], lhsT=WQ[slot][:, k, which * 128:(which + 1) * 128],
                                           rhs=XM[:, k, t0:t0 + n], start=(k == 0), stop=(k == NCH - 1))
                        return ins
                    P.add("pe", mm, r=[("wq%d" % slot,)] + xmtok(tt), w=[pst(which)])
                    dst = QT if which == 0 else KT
                    scale = 0.125 if which == 0 else 1.0
                    if tt < 4:
                        rope(tt, which, dst, scale, which)
                    elif which == 1:
                        P.add("act", lambda e, which=which, dst=dst, scale=scale, t0=t0, n=n: e.activation(
                            out=dst[:, t0:t0 + n], in_=PS[:, which, :n], func=AF.Copy, scale=scale),
                            w=[pst(which), ("KT", tt)])
                    else:
                        def qcp(e, t0=t0, n=n, scale=scale):
                            e.activation(out=QT[0:64, 0, t0:t0 + n], in_=PS[0:64, 0, :n], func=AF.Copy, scale=scale)
                            return e.activation(out=QT[64:128, 1, t0:t0 + n], in_=PS[64:128, 0, :n],
                                                func=AF.Copy, scale=scale)
                        P.add("act", qcp, w=[pst(0), ("QT", tt)])
                ntc = n // 128

                def mmv(e, t0=t0, ntc=ntc, slot=slot):
                    ins = None
                    for tcl in range(ntc):
                        for k in range(NCH):
                            ins = e.matmul(PS[:, 2, tcl * 128:(tcl + 1) * 128],
                                           lhsT=XM[:, k, t0 + tcl * 128:t0 + (tcl + 1) * 128],
                                           rhs=WQ[slot][:, k, 256:384], start=(k == 0), stop=(k == NCH - 1))
                    return ins
                P.add("pe", mmv, r=[("wq%d" % slot,)] + xmtok(tt), w=[pst(2)])
                tc0 = t0 // 128

                def ev(e, tc0=tc0, ntc=ntc):
                    src = PS[:, 2, 0:ntc * 128].rearrange("p (c n) -> p c n", n=128)
                    e.activation(out=VC[:, tc0:tc0 + ntc, 0:64], in_=src[:, :, 0:64], func=AF.Copy)
                    return e.activation(out=VC[:, tc0:tc0 + ntc, 128:192], in_=src[:, :, 64:128], func=AF.Copy)
                P.add("act", ev, w=[pst(2), ("VC", tt)])
            if debug and c == 0 and b == 0:
                P.add("sp", lambda e: e.dma_start(out=dbg_qt, in_=QT[:, 0, :]), r=[("QT", t) for t in range(5)], dma="dbg1")
                P.add("sp", lambda e: e.dma_start(out=dbg_kt, in_=KT), r=[("KT", t) for t in range(5)], dma="dbg2")
                P.add("sp", lambda e: e.dma_start(out=dbg_vc, in_=VC.rearrange("p c n -> p (c n)")),
                      r=[("VC", t) for t in range(5)] + [("VC1",)], dma="dbg3")
                P.add("sp", lambda e: e.dma_start(out=dbg_xm, in_=XM[:]),
                      r=[t for tt in range(5) for t in xmtok(tt)], dma="dbg4")
            units = [{"kind": p} for p in range(16)]
            if ctxq:
                units.append({"kind": "ctx"})
            prev = None
            for un, u in enumerate(units):
                u["n"] = un
                emit_st(u)
                if prev is not None:
                    emit_pv(prev, ci)
                prev = u
            emit_pv(prev, ci)
            if debug and c == 0 and b == 0:
                P.add("sp", lambda e: e.dma_start(out=dbg_ot, in_=OT[:, 0, :]),
                      r=[("OT", 0, t) for t in range(5)], dma="dbg5")
            if ci == 1:
                yield from emit_outproj(WOg, OT, nt_out, lns, last, c == 7, sv)

    def emit_conv(b, l, last):
        jn = l // 2
        nt = sublayer_tiles(l, 1, "out")
        P.new_phase()
        lns = carve_lns(Carver(A_LOW))
        cz = Carver(A_W)
        WC = [cz.bf(NCH * 384, "wc%d" % i).rearrange("p (k n) -> p k n", k=NCH) for i in range(2)]
        WOg = cz.bf(2 * 1024, "wo").rearrange("p (k n) -> p k n", k=2)
        cz = Carver(A_S)
        BB = cz.bf(T, "BB")
        OT = cz.bf(2 * T, "OT").rearrange("p (k n) -> p k n", k=2)
        CG = [cz.f32(512, "CG%d" % i) for i in range(2)]
        ACC = [cz.f32(512, "ACC%d" % i) for i in range(2)]
        cz = Carver(12288)
        U = cz.f32(T + 8, "U")
        assert cz.off <= A_W
        P.ranges["Upad"] = P.ranges["U"]
        sv = emit_sv(b, l, 1, last)
        P.add("pool", lambda e: e.memset(U[:, 0:1], 0.0), w=[("Upad",)])
        P.add("pool", lambda e: e.memset(U[:, 2049:2051], 0.0), w=[("Upad",)])
        P.add("pool", lambda e: e.memset(U[:, 2307:2308], 0.0), w=[("Upad",)])

        def upos(t):
            return t + 1 if t < TL else t + 3

        def load_wc(c):
            P.add("pool", lambda e: e.dma_start(
                out=WC[c % 2], in_=wcin[jn, c].rearrange("p (k n) -> p k n", k=NCH)),
                w=[("wc%d" % (c % 2),)], dma="wq%d" % (c % 2))
        load_wc(0)
        yield "ready"
        for c in range(8):
            slot = c % 2
            ci = c % 2
            if c + 1 < 8:
                load_wc(c + 1)
            if ci == 0:
                P.add("pool", lambda e, c=c: e.dma_start(
                    out=WOg, in_=wco[jn, c // 2].rearrange("p (k n) -> p k n", k=2)), w=[("wo",)], dma="wo")
            for tt in range(nt):
                t0, n = TILES[tt]
                for which in range(3):
                    def mm(e, which=which, t0=t0, n=n, slot=slot):
                        ins = None
                        for k in range(NCH):
                            ins = e.matmul(PS[:, which, :n], lhsT=WC[slot][:, k, which * 128:(which + 1) * 128],
                                           rhs=XM[:, k, t0:t0 + n], start=(k == 0), stop=(k == NCH - 1))
                        return ins
                    P.add("pe", mm, r=[("wc%d" % slot,)] + xmtok(tt), w=[pst(which)])
                P.add("act", lambda e, t0=t0, n=n: e.activation(out=BB[:, t0:t0 + n], in_=PS[:, 0, :n], func=AF.Copy),
                      w=[pst(0), ("BB", tt)])
                cg = CG[tt % 2]
                P.add("act", lambda e, n=n, cg=cg: e.activation(out=cg[:, :n], in_=PS[:, 1, :n], func=AF.Copy),
                      w=[pst(1), ("CG%d" % (tt % 2),)])
                u0 = upos(t0)
                P.add("dve", lambda e, n=n, cg=cg, u0=u0: e.tensor_tensor(
                    U[:, u0:u0 + n], PS[:, 2, :n], cg[:, :n], op=ALU.mult),
                    r=[("CG%d" % (tt % 2),), ("Upad",)], w=[pst(2), ("U", tt)])
            for tt in range(nt):
                t0, n = TILES[tt]
                u0 = upos(t0)
                acc = ACC[tt % 2]
                if tt == 4:
                    rd = [("U", 4)]
                else:
                    rd = [("U", x) for x in range(max(tt - 1, 0), min(tt + 1, 3) + 1)]

                atok = ("ACC%d" % (tt % 2),)
                P.add("dve", lambda e, n=n, u0=u0, acc=acc, c=c: e.tensor_scalar(
                    acc[:, :n], U[:, u0 - 1:u0 - 1 + n], CONVW[:, jn, 0, c:c + 1], None, op0=ALU.mult),
                    r=rd + [("Upad",)], w=[atok])
                for tap in (1, 2):
                    P.add("dve", lambda e, n=n, u0=u0, acc=acc, c=c, tap=tap: e.scalar_tensor_tensor(
                        out=acc[:, :n], in0=U[:, u0 - 1 + tap:u0 - 1 + tap + n], scalar=CONVW[:, jn, tap, c:c + 1],
                        in1=acc[:, :n], op0=ALU.mult, op1=ALU.add), r=rd + [("Upad",), atok], w=[atok])
                P.add("pool", lambda e, t0=t0, n=n, acc=acc, ci=ci: e.tensor_tensor(
                    OT[:, ci, t0:t0 + n], acc[:, :n], BB[:, t0:t0 + n], op=ALU.mult),
                    r=[("ACC%d" % (tt % 2),), ("BB", tt)], w=[("OT", ci, tt)])
            if ci == 1:
                yield from emit_outproj(WOg, OT, nt, lns, last, c == 7, sv)

    subs = [(l, s) for l in range(DEPTH) for s in range(3)]
    if stop is not None:
        subs = subs[:subs.index(tuple(stop)) + 1]

    for b in range(nb):
        for tt in range(5):
            t0, n = TILES[tt]
            P.add("sp", lambda e, b=b, t0=t0, n=n: e.dma_start(out=X[:, :, t0:t0 + n], in_=xin[b][:, :, t0:t0 + n]),
                  w=xtok(tt), dma="xin%d" % tt)
        for ci, col in enumerate((b, 2)):
            P.add("dve", lambda e, ci=ci, col=col: e.tensor_scalar(
                SVI[:, ci, :], modv(0, 1, col), 1.0, None, op0=ALU.add), w=[("SVI",)])
        for tt in range(5):
            t0, n = TILES[tt]
            ci = 1 if tt == 4 else 0
            col = 2 if tt == 4 else b
            for c in range(NCH):
                P.add("pool", lambda e, c=c, t0=t0, n=n, ci=ci, col=col: e.tensor_scalar(
                    XM[:, c, t0:t0 + n], X[:, c, t0:t0 + n], SVI[:, ci, c:c + 1], MOD[:, 0, c, col:col + 1],
                    op0=ALU.mult, op1=ALU.add), r=[("X", c, tt), ("SVI",)], w=[("XM", c, tt)])
                P.add("act", lambda e, c=c, t0=t0, n=n: e.activation(
                    out=X[:, c, t0:t0 + n], in_=X[:, c, t0:t0 + n], func=AF.Copy, scale=ALPHA),
                    w=[("X", c, tt)])
        gens = []
        for (l, s) in subs:
            last = (l, s) == subs[-1]
            if s != 1:
                gens.append(("ffn", emit_ffn(b, l, s, last)))
            elif l % 2 == 0:
                gens.append(("na", emit_na(b, l, last)))
            else:
                gens.append(("conv", emit_conv(b, l, last)))
        assert next(gens[0][1]) == "ready"
        for gi, (kind, g) in enumerate(gens):
            assert next(g) == "hook"
            nxt = gens[gi + 1] if gi + 1 < len(gens) else None
            hoist = nxt is not None and nxt[0] != "na"
            if hoist:
                assert next(nxt[1]) == "ready"
            for _ in g:
                raise AssertionError("unexpected extra yield")
            if nxt is not None and not hoist:
                assert next(nxt[1]) == "ready"
        for tt in range(4):
            t0, n = TILES[tt]
            P.add("sp", lambda e, b=b, t0=t0, n=n: e.dma_start(out=out[b][:, :, t0:t0 + n], in_=X[:, :, t0:t0 + n]),
                  r=xtok(tt), dma="out%d_%d" % (b, tt))
        if debug and b == 0:
            P.add("sp", lambda e: e.dma_start(out=dbg_x, in_=X[:]), dma="outdbg1")
            P.add("sp", lambda e: e.dma_start(out=dbg_mod, in_=MOD[:]), dma="outdbg2")

    dma_keys = sorted({op.dma for op in P.ops if op.dma is not None})
    sems = {}
    dsems = {}
    for e in Prog.ENGS:
        cm = nc.semaphore("s_" + e)
        sems[e] = cm.__enter__()
        ctxs.append(cm)
    for k in dma_keys:
        cm = nc.semaphore("d_" + k)
        dsems[k] = cm.__enter__()
        ctxs.append(cm)
    with nc.Block() as block:
        P.emit(nc, block, sems, dsems)
    return nc, P


def _fm(a2d):
    n = a2d.shape[1]
    return np.ascontiguousarray(a2d.reshape(NCH, 128, n).transpose(1, 0, 2))


def _bias_tables(rpb):
    out = np.full((16, 128, TBW), NEG, np.float32)
    qc = np.arange(64)
    wc0 = np.clip(qc - 8, 0, 48)
    kc = np.arange(64)
    colok = (kc[None, :] >= wc0[:, None]) & (kc[None, :] < wc0[:, None] + 16)
    dx = np.clip(kc[None, :] - qc[:, None] + 15, 0, 30)

    def fill(dst, r0, kbase, nrows, rs_of):
        for qi in range(2):
            qr = r0 + qi
            rs = rs_of(qr)
            for e in range(nrows):
                kr = kbase + e
                if rs <= kr < rs + 8:
                    dy = kr - qr + 7
                    vals = rpb[:, dy, :][:, dx]
                    blk = np.where(colok[None], vals, NEG)
                    dst[:, qi * 64:(qi + 1) * 64, e * 64:(e + 1) * 64] = blk

    rs_of = lambda r: int(np.clip(r - 4, 0, 24))
    fill(out[:, :, 0:640], 12, 8, 10, rs_of)
    fill(out[:, :, 640:1152], 0, 0, 8, rs_of)
    fill(out[:, :, 1152:1664], 2, 0, 8, rs_of)
    fill(out[:, :, 1664:2176], 28, 24, 8, rs_of)
    fill(out[:, :, 2176:2688], 30, 24, 8, rs_of)
    return np.ascontiguousarray(out.reshape(8, 2, 128, TBW).transpose(0, 2, 1, 3)).reshape(8, 128, 2 * TBW)


def _rope_tables():
    t = np.arange(TL)
    nf = 16
    inv = 10000.0 ** (-np.arange(nf) / nf)
    ang_r = (t // GRID_W)[:, None] * inv[None]
    ang_c = (t % GRID_W)[:, None] * inv[None]
    cr, sr = np.cos(ang_r).astype(np.float32), np.sin(ang_r).astype(np.float32)
    cc, sc = np.cos(ang_c).astype(np.float32), np.sin(ang_c).astype(np.float32)
    C = np.zeros((128, TL), np.float32)
    S = np.zeros((128, TL), np.float32)
    for p in range(128):
        n = p % 64
        m = n % 32
        cs, sn = (cr[:, m], sr[:, m]) if m < 16 else (cc[:, m - 16], sc[:, m - 16])
        C[p] = cs
        S[p] = sn if n < 32 else -sn
    return C, S


_PERM = np.concatenate([np.arange(0, 16), np.arange(32, 48), np.arange(16, 32), np.arange(48, 64)])


def prepare_inputs(x, c, ctx, c_ctx, ada_w, ada_b, ln_g, ln_b, ffn_w_in, ffn_w_out,
                   na_w_qkv, na_w_out, na_rpb, sc_w_in, sc_conv, sc_w_out):
    f = lambda a: np.asarray(a, np.float32)
    x, c, ctx, c_ctx = f(x), f(c), f(ctx), f(c_ctx)
    ada_w, ada_b, ln_g, ln_b = f(ada_w), f(ada_b), f(ln_g), f(ln_b)
    ffn_w_in, ffn_w_out = f(ffn_w_in), f(ffn_w_out)
    na_w_qkv, na_w_out, na_rpb = f(na_w_qkv), f(na_w_out), f(na_rpb)
    sc_w_in, sc_conv, sc_w_out = f(sc_w_in), f(sc_conv), f(sc_w_out)
    sh = {}
    a = ada_w.reshape(DEPTH, NCH, 128, 18, 512).transpose(0, 3, 2, 1, 4)
    sh["adaw"] = np.ascontiguousarray(a).reshape(DEPTH, 18, 128, NCH * 512)
    sh["adab"] = np.ascontiguousarray(ada_b.reshape(DEPTH, 72, 128).transpose(2, 0, 1))
    sh["lng"] = np.ascontiguousarray(ln_g.reshape(DEPTH, 3, NCH, 128).transpose(3, 0, 1, 2))
    sh["lnb"] = np.ascontiguousarray(ln_b.reshape(DEPTH, 3, NCH, 128).transpose(3, 0, 1, 2))
    winb = np.zeros((DEPTH, 2, NFB, 128, NCH, 1024), np.float32)
    woutb = np.zeros((DEPTH, 2, NFB, 128, 4, 1024), np.float32)
    wi5 = ffn_w_in.reshape(DEPTH, 2, NCH, 128, 2 * FF)
    wo5 = ffn_w_out.reshape(DEPTH, 2, FF // 128, 128, D)
    for j in range(NFB):
        nbk, o = BLK[j], BOFF[j] * 128
        a = wi5[..., o:o + nbk * 128]
        u = wi5[..., FF + o:FF + o + nbk * 128]
        blk = np.concatenate([a, u], axis=-1)
        winb[:, :, j, :, :, 0:2 * nbk * 128] = blk.transpose(0, 1, 3, 2, 4)
        woutb[:, :, j, :, 0:nbk, :] = wo5[:, :, BOFF[j]:BOFF[j] + nbk].transpose(0, 1, 3, 2, 4)
    sh["win"] = winb.reshape(DEPTH, 2, NFB, 128, NCH * 1024)
    sh["wout"] = woutb.reshape(DEPTH, 2, NFB, 128, 4 * 1024)
    cols = []
    for cch in range(8):
        qcols = np.concatenate([(2 * cch) * 64 + _PERM, (2 * cch + 1) * 64 + _PERM])
        cols.append(np.concatenate([qcols, 1024 + qcols, 2048 + cch * 128 + np.arange(128)]))
    cols = np.stack(cols)
    wq = na_w_qkv[:, :, cols]
    wq = wq.reshape(2, NCH, 128, 8, 384).transpose(0, 3, 2, 1, 4)
    sh["wqkv"] = np.ascontiguousarray(wq).reshape(2, 8, 128, NCH * 384)
    w2 = na_w_out.reshape(2, 4, 2, 128, D).transpose(0, 1, 3, 2, 4)
    sh["wo"] = np.ascontiguousarray(w2).reshape(2, 4, 128, 2 * 1024)
    sh["tbias"] = np.stack([_bias_tables(na_rpb[j]) for j in range(2)])
    ccols = np.stack([np.concatenate([cch * 128 + np.arange(128), 1024 + cch * 128 + np.arange(128),
                                      2048 + cch * 128 + np.arange(128)]) for cch in range(8)])
    wc = sc_w_in[:, :, ccols].reshape(2, NCH, 128, 8, 384).transpose(0, 3, 2, 1, 4)
    sh["wcin"] = np.ascontiguousarray(wc).reshape(2, 8, 128, NCH * 384)
    w3 = sc_w_out.reshape(2, 4, 2, 128, D).transpose(0, 1, 3, 2, 4)
    sh["wco"] = np.ascontiguousarray(w3).reshape(2, 4, 128, 2 * 1024)
    sh["convw"] = np.ascontiguousarray(sc_conv.reshape(2, 3, NCH, 128).transpose(3, 0, 1, 2))
    C, S = _rope_tables()
    sh["ropec"], sh["ropes"] = C, S
    sh["ident"] = np.eye(128, dtype=np.float32)
    in_maps = []
    for i in range(8):
        m = dict(sh)
        xs = []
        for bb in range(2):
            full = np.concatenate([x[2 * i + bb], ctx[2 * i + bb]], axis=0)
            xs.append(_fm(np.ascontiguousarray(full.T)))
        m["xin"] = np.stack(xs)
        cvv = np.stack([c[2 * i], c[2 * i + 1], c_ctx], axis=1)
        m["cv"] = _fm(cvv)
        in_maps.append(m)
    return in_maps


def assemble_output(results):
    outs = []
    for i in range(8):
        o = np.asarray(results[i]["out"])
        for bb in range(2):
            outs.append(o[bb].transpose(2, 1, 0).reshape(TL, D))
    return np.ascontiguousarray(np.stack(outs)).astype(np.float32)


_CACHE = {}


def kernel(x, c, ctx, c_ctx, ada_w, ada_b, ln_g, ln_b, ffn_w_in, ffn_w_out,
           na_w_qkv, na_w_out, na_rpb, sc_w_in, sc_conv, sc_w_out):
    in_maps = prepare_inputs(x, c, ctx, c_ctx, ada_w, ada_b, ln_g, ln_b, ffn_w_in, ffn_w_out,
                             na_w_qkv, na_w_out, na_rpb, sc_w_in, sc_conv, sc_w_out)
    if "nc" not in _CACHE:
        _CACHE["nc"] = build_program()[0]
    res = run_bass_kernel_spmd(_CACHE["nc"], in_maps, core_ids=list(range(8)))
    return assemble_output(res.results)
```

```python
import numpy as np
import concourse.bass as bass
import concourse.mybir as mybir
from concourse.bass_utils import run_bass_kernel_spmd

F32 = mybir.dt.float32
BF16 = mybir.dt.bfloat16
AF = mybir.ActivationFunctionType
ALU = mybir.AluOpType

D = 1024
NCH = 8
TL = 2048
TCX = 256
T = TL + TCX
FF = 2816
BLK = [2, 4, 4, 4, 4, 4]
BOFF = [0, 2, 6, 10, 14, 18]
NFB = len(BLK)
DEPTH = 4
ALPHA = float((2.0 * DEPTH) ** 0.25)
EPS = 1e-6
NEG = -30000.0
GRID_W = 64
TILES = [(0, 512), (512, 512), (1024, 512), (1536, 512), (2048, 256)]
TBW = 640 + 4 * 512
ARENA = 46592


class Op:
    __slots__ = ("eng", "fn", "deps", "dma", "idx", "waits", "sig", "sigval")


class Prog:
    ENGS = ["pe", "act", "dve", "pool", "sp"]

    def __init__(self):
        self.ops = []
        self.lastw = {}
        self.readers = {}
        self.pending_bar = {}
        self.last_of = {}
        self.dmas_since_bar = []
        self.ranges = {}
        self.stale = []
        self.stale_done = set()
        self.phase_no = 0

    def add(self, eng, fn, r=(), w=(), dma=None):
        op = Op()
        op.eng, op.fn, op.dma, op.idx = eng, fn, dma, len(self.ops)
        deps = {}
        for t in r:
            x = self.lastw.get(t)
            if x is not None:
                deps[x] = "raw"
        for t in w:
            x = self.lastw.get(t)
            if x is not None and x not in deps:
                deps[x] = "waw"
            for y in self.readers.get(t, ()):
                if y not in deps:
                    deps[y] = "war"
        if eng in self.pending_bar:
            for x in self.pending_bar.pop(eng):
                deps[x] = "bar" if deps.get(x) != "raw" else "raw"
        if self.stale:
            for t in list(r) + list(w):
                rg = self.ranges.get(t[0])
                if rg is None or (eng, t[0]) in self.stale_done:
                    continue
                self.stale_done.add((eng, t[0]))
                for (_, a, b_, opset) in self.stale:
                    if a < rg[1] and rg[0] < b_:
                        for x in opset:
                            if x not in deps:
                                deps[x] = "war"
        op.deps = deps
        for t in r:
            self.readers.setdefault(t, []).append(op.idx)
        for t in w:
            self.lastw[t] = op.idx
            self.readers[t] = []
        self.ops.append(op)
        if dma is None:
            self.last_of[eng] = op.idx
        else:
            self.dmas_since_bar.append(op.idx)
        return op.idx

    def new_phase(self):
        self.phase_no += 1
        byname = {}
        for t in list(self.lastw.keys()):
            if t[0] in self.ranges:
                byname.setdefault(t[0], set()).add(self.lastw.pop(t))
        for t in list(self.readers.keys()):
            if t[0] in self.ranges:
                byname.setdefault(t[0], set()).update(self.readers.pop(t))
        for nm, opset in byname.items():
            a, b_ = self.ranges[nm]
            self.stale.append((self.phase_no, a, b_, opset))
        self.stale = [x for x in self.stale if x[0] >= self.phase_no - 2]
        self.ranges = {}
        self.stale_done = set()

    def barrier(self):
        deps = set(self.last_of.values()) | set(self.dmas_since_bar)
        self.pending_bar = {e: set(deps) for e in self.ENGS}
        self.dmas_since_bar = []
        self.lastw = {}
        self.readers = {}

    def emit(self, nc, block, sems, dma_sems):
        ops = self.ops
        need = set()
        for op in ops:
            waits = []
            for d, kind in op.deps.items():
                dop = ops[d]
                if dop.dma is not None:
                    waits.append(d)
                elif dop.eng == op.eng:
                    if op.dma is not None:
                        waits.append(d)
                    elif op.eng == "pe":
                        continue
                    elif kind == "raw":
                        waits.append(d)
                else:
                    waits.append(d)
            op.waits = waits
            for d in waits:
                if ops[d].dma is None:
                    need.add(d)
        cnt = {e: 0 for e in self.ENGS}
        dcnt = {}
        for op in ops:
            if op.dma is not None:
                dcnt[op.dma] = dcnt.get(op.dma, 0) + 16
                op.sigval = dcnt[op.dma]
                op.sig = True
            elif op.idx in need:
                cnt[op.eng] += 1
                op.sigval = cnt[op.eng]
                op.sig = True
            else:
                op.sig = False
                op.sigval = 0
        self.final_counts = (cnt, dcnt)

        def run_engine(engname, eng):
            waited = {}
            for op in ops:
                if op.eng != engname:
                    continue
                for d in op.waits:
                    dop = ops[d]
                    key = ("d", dop.dma) if dop.dma is not None else ("e", dop.eng)
                    if waited.get(key, 0) < dop.sigval:
                        s = dma_sems[dop.dma] if dop.dma is not None else sems[dop.eng]
                        eng.wait_ge(s, dop.sigval)
                        waited[key] = dop.sigval
                ins = op.fn(eng)
                if op.sig:
                    if op.dma is not None:
                        ins.then_inc(dma_sems[op.dma], 16)
                    else:
                        ins.then_inc(sems[op.eng], 1)
            for e in self.ENGS:
                if cnt[e] > 0 and waited.get(("e", e), 0) < cnt[e] and e != engname:
                    pass
            if engname == "sp":
                for k, v in dcnt.items():
                    if k.startswith("out"):
                        eng.wait_ge(dma_sems[k], v)

        @block.tensor
        def _(e):
            run_engine("pe", e)

        @block.scalar
        def _(e):
            run_engine("act", e)

        @block.vector
        def _(e):
            run_engine("dve", e)

        @block.gpsimd
        def _(e):
            run_engine("pool", e)

        @block.sync
        def _(e):
            run_engine("sp", e)


def sublayer_tiles(l, s, what):
    if l <= 1:
        return 5
    if l == 2:
        if s == 0:
            return 5
        if s == 1:
            return 5 if what == "in" else 4
        return 4
    return 4


def build_program(stop=None, nb=2, debug=False):
    nc = bass.Bass("TRN2", target_bir_lowering=False)

    def dram(name, shape, kind="ExternalInput", dt=F32):
        return nc.dram_tensor(name, list(shape), dt, kind=kind).ap()

    xin = dram("xin", [2, 128, NCH, T])
    cv = dram("cv", [128, NCH, 3])
    adaw = dram("adaw", [DEPTH, 18, 128, NCH * 512])
    adab = dram("adab", [128, DEPTH, 72])
    lng = dram("lng", [128, DEPTH, 3, NCH])
    lnb = dram("lnb", [128, DEPTH, 3, NCH])
    win = dram("win", [DEPTH, 2, NFB, 128, NCH * 1024])
    wout = dram("wout", [DEPTH, 2, NFB, 128, 4 * 1024])
    wqkv = dram("wqkv", [2, 8, 128, NCH * 384])
    wo = dram("wo", [2, 4, 128, 2 * 1024])
    tbias = dram("tbias", [2, 8, 128, 2 * TBW])
    wcin = dram("wcin", [2, 8, 128, NCH * 384])
    wco = dram("wco", [2, 4, 128, 2 * 1024])
    convw = dram("convw", [128, 2, 3, NCH])
    ropec = dram("ropec", [128, TL])
    ropes = dram("ropes", [128, TL])
    ident = dram("ident", [128, 128])
    out = dram("out", [2, 128, NCH, TL], kind="ExternalOutput")
    if debug:
        dbg_qt = dram("dbg_qt", [128, T], kind="ExternalOutput", dt=BF16)
        dbg_kt = dram("dbg_kt", [128, T], kind="ExternalOutput", dt=BF16)
        dbg_vc = dram("dbg_vc", [128, 18 * 192], kind="ExternalOutput", dt=BF16)
        dbg_ot = dram("dbg_ot", [128, T], kind="ExternalOutput", dt=BF16)
        dbg_xm = dram("dbg_xm", [128, NCH, T], kind="ExternalOutput", dt=BF16)
        dbg_x = dram("dbg_x", [128, NCH, T], kind="ExternalOutput")
        dbg_mod = dram("dbg_mod", [128, DEPTH, 72, 3], kind="ExternalOutput")

    P = Prog()
    ctxs = []

    def sb(name, shape, dt):
        cm = nc.sbuf_tensor(name, list(shape), dt)
        t = cm.__enter__()
        ctxs.append(cm)
        return t

    X = sb("X", [128, NCH, T], F32)
    XM = sb("XM", [128, NCH, T], BF16)
    AR = sb("AR", [128, ARENA], BF16)
    MOD = sb("MOD", [128, DEPTH, 72, 3], F32)
    ADAB = sb("ADAB", [128, DEPTH, 72], F32)
    LNG = sb("LNG", [128, DEPTH, 3, NCH], F32)
    LNB = sb("LNB", [128, DEPTH, 3, NCH], F32)
    CONVW = sb("CONVW", [128, 2, 3, NCH], F32)
    IDB = sb("IDB", [128, 128], BF16)
    IDF = sb("IDF", [128, 128], F32)
    ONESD = sb("ONESD", [128, 128], BF16)
    MHALF = sb("MHALF", [128, 1], F32)
    SVS = [sb("SV%d" % i, [128, 2, 6, NCH], F32) for i in range(2)]
    SVI = sb("SVI", [128, 2, NCH], F32)
    cur = {"k": 0}
    SC = sb("SC", [128, NCH, 3], F32)
    CVT = sb("CVT", [128, NCH, 3], F32)
    cm = nc.psum_tensor("PS", [128, 8, 512], F32)
    PS = cm.__enter__()
    ctxs.append(cm)

    K_GT, K_GA, K_BA, K_GP, K_BP, K_TMP = range(6)

    class Carver:
        def __init__(self, off=0):
            self.off = off

        def bf(self, n, name=None):
            a = AR[:, self.off:self.off + n]
            if name is not None:
                P.ranges[name] = (self.off, self.off + n)
            self.off += n
            assert self.off <= ARENA, self.off
            return a

        def f32(self, n, name=None):
            if self.off % 2:
                self.off += 1
            a = AR[:, self.off:self.off + 2 * n].bitcast(F32)
            if name is not None:
                P.ranges[name] = (self.off, self.off + 2 * n)
            self.off += 2 * n
            assert self.off <= ARENA, self.off
            return a

    A_LOW = 0
    A_W = 19584
    A_S = 33152

    def bank(i):
        return PS[:, i, :]

    def pst(i):
        return ("ps", i)

    P.add("sp", lambda e: e.dma_start(out=CVT[:], in_=cv), w=[("CVT",)], dma="misc_cvt")
    P.add("sp", lambda e: e.dma_start(out=ADAB[:], in_=adab), w=[("ADAB",)], dma="misc_adab")
    P.add("sp", lambda e: e.dma_start(out=LNG[:], in_=lng), w=[("LNG",)], dma="misc_lng")
    P.add("sp", lambda e: e.dma_start(out=LNB[:], in_=lnb), w=[("LNB",)], dma="misc_lnb")
    P.add("sp", lambda e: e.dma_start(out=CONVW[:], in_=convw), w=[("CONVW",)], dma="misc_convw")
    P.add("pool", lambda e: e.dma_start(out=IDB[:], in_=ident), w=[("IDB",)], dma="misc2")
    P.add("sp", lambda e: e.dma_start(out=IDF[:], in_=ident), w=[("IDF",)], dma="misc_idf")
    P.add("dve", lambda e: e.memset(ONESD[:], 1.0 / 1024.0), w=[("ONESD",)])
    P.add("dve", lambda e: e.memset(MHALF[:], -0.5), w=[("MHALF",)])
    P.add("act", lambda e: e.activation(out=SC[:], in_=CVT[:], func=AF.Silu), r=[("CVT",)], w=[("SC",)])

    cz = Carver()
    AW = [cz.f32(NCH * 512, "aw%d" % i).rearrange("p (k n) -> p k n", k=NCH) for i in range(2)]
    n_ada = 0
    MT = [cz.f32(512, "mt%d" % i) for i in range(2)]
    layers_needed = DEPTH if stop is None else stop[0] + 2
    for l in range(min(DEPTH, layers_needed)):
        for jb in range(18):
            slot = n_ada % 2
            P.add("sp", lambda e, l=l, jb=jb, slot=slot: e.dma_start(
                out=AW[slot], in_=adaw[l, jb].rearrange("p (k n) -> p k n", k=NCH)),
                w=[("aw%d" % slot,)], dma="aw%d" % slot)
            bk = (2 * n_ada) % 8
            bk2 = bk + 1

            def mm(e, slot=slot, bk=bk):
                ins = None
                for k in range(NCH):
                    ins = e.matmul(PS[0:3, bk, :], lhsT=SC[:, k, :], rhs=AW[slot][:, k, :],
                                   start=(k == 0), stop=(k == NCH - 1))
                return ins
            P.add("pe", mm, r=[("aw%d" % slot,), ("SC",)], w=[pst(bk)])
            P.add("act", lambda e, slot=slot, bk=bk: e.activation(out=MT[slot][0:3, :], in_=PS[0:3, bk, :], func=AF.Copy),
                  w=[pst(bk), ("mt%d" % slot,)])

            def tr(e, slot=slot, bk2=bk2):
                ins = None
                for fcl in range(4):
                    ins = e.transpose(PS[:, bk2, fcl * 3:(fcl + 1) * 3], MT[slot][0:3, fcl * 128:(fcl + 1) * 128],
                                      IDF[0:3, 0:3])
                return ins
            P.add("pe", tr, r=[("mt%d" % slot,), ("IDF",)], w=[pst(bk2)])
            P.add("dve", lambda e, l=l, jb=jb, bk2=bk2: e.tensor_tensor(
                MOD[:, l, jb * 4:(jb + 1) * 4, :], PS[:, bk2, 0:12].rearrange("p (f c) -> p f c", c=3),
                ADAB[:, l, jb * 4:(jb + 1) * 4].unsqueeze(2).to_broadcast([128, 4, 3]), op=ALU.add),
                r=[("ADAB",)], w=[pst(bk2), ("MOD",)])
            n_ada += 1
    P.barrier()

    def modv(l, mv, col):
        return MOD[:, l, mv * 8:(mv + 1) * 8, col]

    def next_sub(l, s):
        if s < 2:
            return (l, s + 1)
        if l + 1 < DEPTH:
            return (l + 1, 0)
        return None

    def emit_sv(b, l, s, last):
        w = 0.5 if s != 1 else 1.0
        nxt = None if last else next_sub(l, s)
        cur["k"] += 1
        par = cur["k"] % 2
        SV = SVS[par]
        cur["sv"] = SV
        cur["tag"] = "SVR%d" % par
        tg = lambda x: ("%s_%d" % (x, par),)
        for ci, col in enumerate((b, 2)):
            P.add("dve", lambda e, ci=ci, col=col: e.tensor_scalar(
                SV[:, ci, K_GT, :], modv(l, 3 * s + 2, col), w, None, op0=ALU.mult), w=[tg("SV"), (cur["tag"],)])
            al = ALPHA if nxt is not None else 1.0
            P.add("dve", lambda e, ci=ci, al=al: e.tensor_scalar(
                SV[:, ci, K_GA, :], LNG[:, l, s, :], al, None, op0=ALU.mult), w=[tg("SV")])
            P.add("dve", lambda e, ci=ci, al=al: e.tensor_scalar(
                SV[:, ci, K_BA, :], LNB[:, l, s, :], al, None, op0=ALU.mult), w=[tg("SV")])
            if nxt is not None:
                nl, ns = nxt
                P.add("dve", lambda e, ci=ci, col=col, nl=nl, ns=ns: e.tensor_scalar(
                    SV[:, ci, K_TMP, :], modv(nl, 3 * ns + 1, col), 1.0, None, op0=ALU.add), w=[tg("SV")])
                P.add("dve", lambda e, ci=ci: e.tensor_tensor(
                    SV[:, ci, K_GP, :], LNG[:, l, s, :], SV[:, ci, K_TMP, :], op=ALU.mult),
                    r=[tg("SV")], w=[tg("SV2")])
                P.add("dve", lambda e, ci=ci: e.tensor_tensor(
                    SV[:, ci, K_BP, :], LNB[:, l, s, :], SV[:, ci, K_TMP, :], op=ALU.mult),
                    r=[tg("SV")], w=[tg("SV3")])
                P.add("dve", lambda e, ci=ci, col=col, nl=nl, ns=ns: e.tensor_tensor(
                    SV[:, ci, K_BP, :], SV[:, ci, K_BP, :], modv(nl, 3 * ns, col), op=ALU.add),
                    r=[tg("SV3")], w=[tg("SV4")])
        tag = cur["tag"]
        P.add("dve", lambda e: e.memset(SV[:, 0, K_TMP, 0:1], 0.0),
              r=[tg("SV"), tg("SV2"), tg("SV3"), tg("SV4")], w=[(tag,)])
        return (SV, tag)

    def svv(sv, tt, kind, c):
        ci = 1 if tt == 4 else 0
        return sv[0][:, ci, kind, c:c + 1]

    def xtok(tt):
        return [("X", c, tt) for c in range(NCH)]

    def xmtok(tt):
        return [("XM", c, tt) for c in range(NCH)]

    def emit_ln_front_a(tt, lns):
        t0, n = TILES[tt]
        ZB, SQ, MEAN, MSQ, VARE, RSTD = lns
        P.add("dve", lambda e: e.tensor_copy(ZB[:, :, :n], X[:, :, t0:t0 + n]), r=xtok(tt), w=[("ZB",)])
        P.add("act", lambda e: e.activation(out=SQ[:, :, :n], in_=X[:, :, t0:t0 + n], func=AF.Square),
              r=xtok(tt), w=[("SQ",)])

    def emit_ln_front(tt, lns):
        emit_ln_front_a(tt, lns)
        emit_ln_front_b(tt, lns)

    def emit_ln_front_b(tt, lns, passes=True):
        t0, n = TILES[tt]
        ZB, SQ, MEAN, MSQ, VARE, RSTD = lns

        def st(e, src, bk):
            ins = None
            for c in range(NCH):
                ins = e.matmul(PS[:, bk, :n], lhsT=ONESD[:], rhs=src[:, c, :n], start=(c == 0), stop=(c == NCH - 1))
            return ins
        P.add("pe", lambda e: st(e, ZB, 4), r=[("ZB",), ("ONESD",)], w=[pst(4)])
        P.add("pe", lambda e: st(e, SQ, 5), r=[("SQ",), ("ONESD",)], w=[pst(5)])

        def a1(e):
            e.activation(out=MEAN[:, :n], in_=PS[:, 4, :n], func=AF.Copy)
            return e.activation(out=MSQ[:, :n], in_=PS[:, 4, :n], func=AF.Square)
        P.add("act", a1, w=[pst(4), ("MEAN",), ("MSQ",)])
        P.add("dve", lambda e: e.scalar_tensor_tensor(
            out=VARE[:, :n], in0=PS[:, 5, :n], scalar=EPS, in1=MSQ[:, :n], op0=ALU.add, op1=ALU.subtract),
            r=[("MSQ",)], w=[pst(5), ("VARE",)])

        P.add("act", lambda e: e.activation(out=VARE[:, :n], in_=VARE[:, :n], func=AF.Ln),
              r=[("VARE",)], w=[("SD",)])
        P.add("act", lambda e: e.activation(out=RSTD[:, :n], in_=VARE[:, :n], func=AF.Exp, scale=-0.5),
              r=[("SD",)], w=[("RSTD",), ("VARE",)])

        if not passes:
            return
        emit_ln_front_b2(tt, lns)

    def emit_ln_front_b2(tt, lns):
        t0, n = TILES[tt]
        ZB, SQ, MEAN, MSQ, VARE, RSTD = lns
        for eng, c0, c1 in (("dve", 0, 5), ("pool", 5, 8)):
            toks = [("X", c, tt) for c in range(c0, c1)]
            P.add(eng, lambda e, c0=c0, c1=c1: e.tensor_tensor(
                X[:, c0:c1, t0:t0 + n], X[:, c0:c1, t0:t0 + n],
                MEAN[:, None, :n].to_broadcast([128, c1 - c0, n]), op=ALU.subtract),
                r=[("MEAN",)] + toks, w=toks)
            P.add(eng, lambda e, c0=c0, c1=c1: e.tensor_tensor(
                X[:, c0:c1, t0:t0 + n], X[:, c0:c1, t0:t0 + n],
                RSTD[:, None, :n].to_broadcast([128, c1 - c0, n]), op=ALU.mult),
                r=[("RSTD",)] + toks, w=toks)

    def emit_ln_apply(tt, sv, has_next):
        t0, n = TILES[tt]
        svr = (sv[1],)
        for c in range(NCH):
            gp, bp, ga, ba = svv(sv, tt, K_GP, c), svv(sv, tt, K_BP, c), svv(sv, tt, K_GA, c), svv(sv, tt, K_BA, c)

            if has_next:
                P.add("act", lambda e, c=c, gp=gp, bp=bp: e.activation(
                    out=XM[:, c, t0:t0 + n], in_=X[:, c, t0:t0 + n], func=AF.Identity, scale=gp, bias=bp),
                    r=[svr, ("X", c, tt)], w=[("XM", c, tt)])
            P.add("pool", lambda e, c=c, ga=ga, ba=ba: e.tensor_scalar(
                X[:, c, t0:t0 + n], X[:, c, t0:t0 + n], ga, ba, op0=ALU.mult, op1=ALU.add),
                r=[svr], w=[("X", c, tt)])

    def carve_lns(cz):
        ZB = cz.bf(NCH * 512, "ZB").rearrange("p (c t) -> p c t", c=NCH)
        SQ = cz.bf(NCH * 512, "SQ").rearrange("p (c t) -> p c t", c=NCH)
        MEAN = cz.f32(512, "MEAN")
        MSQ = cz.f32(512, "MSQ")
        VARE = cz.f32(512, "VARE")
        RSTD = cz.f32(512, "RSTD")
        P.ranges["SD"] = P.ranges["VARE"]
        return (ZB, SQ, MEAN, MSQ, VARE, RSTD)

    def emit_rmw(sv, tt, dc, bk):
        t0, n = TILES[tt]
        gt = svv(sv, tt, K_GT, dc)
        P.add("dve", lambda e: e.scalar_tensor_tensor(
            out=X[:, dc, t0:t0 + n], in0=PS[:, bk, :n], scalar=gt, in1=X[:, dc, t0:t0 + n],
            op0=ALU.mult, op1=ALU.add), r=[(sv[1],)], w=[pst(bk), ("X", dc, tt)])

    def emit_ffn(b, l, s, last):
        fi = 0 if s == 0 else 1
        nt_in = sublayer_tiles(l, s, "in")
        nt_out = sublayer_tiles(l, s, "out")
        nt = min(nt_in, nt_out)
        P.new_phase()
        lns = carve_lns(Carver(A_LOW))
        WIN, WOUT = [None, None], [None, None]
        cz = Carver(A_W)
        WIN[0] = cz.bf(NCH * 1024, "win0").rearrange("p (k n) -> p k n", k=NCH)
        WOUT[0] = cz.bf(4 * 1024, "wout0").rearrange("p (k n) -> p k n", k=4)
        assert cz.off <= A_S
        cz = Carver(A_S)
        WIN[1] = cz.bf(NCH * 1024, "win1").rearrange("p (k n) -> p k n", k=NCH)
        WOUT[1] = cz.bf(4 * 1024, "wout1").rearrange("p (k n) -> p k n", k=4)
        cz = Carver(12288)
        G = [cz.bf(4 * 512, "G%d" % i).rearrange("p (k n) -> p k n", k=4) for i in range(2)]
        SIL = [cz.f32(512, "SIL%d" % i) for i in range(2)]
        assert cz.off <= A_W
        sv = emit_sv(b, l, s, last)

        def load(j):
            slot = j % 2
            nbk = BLK[j]
            P.add("pool", lambda e: e.dma_start(
                out=WIN[slot][:, :, 0:nbk * 256],
                in_=win[l, fi, j].rearrange("p (k n) -> p k n", k=NCH)[:, :, 0:nbk * 256]),
                w=[("win%d" % slot,)], dma="win%d" % slot)
            P.add("pool", lambda e: e.dma_start(
                out=WOUT[slot][:, 0:nbk, :], in_=wout[l, fi, j].rearrange("p (k n) -> p k n", k=4)[:, 0:nbk, :]),
                w=[("wout%d" % slot,)], dma="wout%d" % slot)

        def emit_h(j, tt, gs):
            slot = j % 2
            t0, n = TILES[tt]
            nbk = BLK[j]
            for i in range(nbk):
                for half in range(2):
                    bk = 2 * (i % 2) + half
                    co = half * nbk * 128 + i * 128

                    def mm(e, bk=bk, co=co):
                        ins = None
                        for k in range(NCH):
                            ins = e.matmul(PS[:, bk, :n], lhsT=WIN[slot][:, k, co:co + 128],
                                           rhs=XM[:, k, t0:t0 + n], start=(k == 0), stop=(k == NCH - 1))
                        return ins
                    P.add("pe", mm, r=[("win%d" % slot,)] + xmtok(tt), w=[pst(bk)])
                ip = i % 2
                P.add("act", lambda e, ip=ip: e.activation(out=SIL[ip][:, :n], in_=PS[:, 2 * ip, :n], func=AF.Silu),
                      w=[pst(2 * ip), ("SIL%d" % ip,)])
                P.add("dve", lambda e, i=i, ip=ip: e.tensor_tensor(
                    G[gs][:, i, :n], PS[:, 2 * ip + 1, :n], SIL[ip][:, :n], op=ALU.mult),
                    r=[("SIL%d" % ip,)], w=[pst(2 * ip + 1), ("G%d" % gs, i)])

        def emit_y(j, tt, gs):
            slot = j % 2
            t0, n = TILES[tt]
            for dc in range(NCH):
                bk = (6 + dc % 2) if j >= NFB - 1 else (4 + dc % 4)
                nbk = BLK[j]

                def mm(e, dc=dc, bk=bk):
                    ins = None
                    for i in range(nbk):
                        ins = e.matmul(PS[:, bk, :n], lhsT=WOUT[slot][:, i, dc * 128:(dc + 1) * 128],
                                       rhs=G[gs][:, i, :n], start=(i == 0), stop=(i == nbk - 1))
                    return ins
                P.add("pe", mm, r=[("wout%d" % slot,)] + [("G%d" % gs, i) for i in range(nbk)], w=[pst(bk)])
                emit_rmw(sv, tt, dc, bk)

        load(0)
        yield "ready"
        pending = None
        cnt = 0
        qb, qc = [], []
        for j in range(NFB):
            for tt in range(nt):
                gs = cnt % 2
                cnt += 1
                emit_h(j, tt, gs)
                fb = qb.pop(0) if qb else None
                if fb is not None:
                    emit_ln_front_b(fb, lns, passes=False)
                if pending is not None:
                    emit_y(*pending)
                    if fb is not None:
                        emit_ln_front_b2(fb, lns)
                    if qc:
                        emit_ln_apply(qc.pop(0), sv, not last)
                    if fb is not None:
                        qc.append(fb)
                    if pending[0] == NFB - 1:
                        emit_ln_front_a(pending[1], lns)
                        qb.append(pending[1])
                pending = (j, tt, gs)
                if tt == 0 and j + 1 < NFB:
                    load(j + 1)
        fb = qb.pop(0) if qb else None
        if fb is not None:
            emit_ln_front_b(fb, lns, passes=False)
        emit_y(*pending)
        if fb is not None:
            emit_ln_front_b2(fb, lns)
        if qc:
            emit_ln_apply(qc.pop(0), sv, not last)
        if fb is not None:
            qc.append(fb)
        emit_ln_front_a(pending[1], lns)
        yield "hook"
        emit_ln_front_b(pending[1], lns)
        while qc:
            emit_ln_apply(qc.pop(0), sv, not last)
        emit_ln_apply(pending[1], sv, not last)

    def emit_outproj(WOg, OT, tiles_out, lns, last, final_group, sv):
        prev = None
        for tt in range(tiles_out):
            t0, n = TILES[tt]
            for dc in range(NCH):
                bk = 6 + dc % 2

                def mm(e, dc=dc, bk=bk, t0=t0, n=n):
                    ins = None
                    for i in range(2):
                        ins = e.matmul(PS[:, bk, :n], lhsT=WOg[:, i, dc * 128:(dc + 1) * 128],
                                       rhs=OT[:, i, t0:t0 + n], start=(i == 0), stop=(i == 1))
                    return ins
                P.add("pe", mm, r=[("wo",), ("OT", 0, tt), ("OT", 1, tt)], w=[pst(bk)])
                emit_rmw(sv, tt, dc, bk)
            if final_group:
                if tt >= 1:
                    emit_ln_front_b(tt - 1, lns)
                if tt >= 2:
                    emit_ln_apply(tt - 2, sv, not last)
                if tt == tiles_out - 1:
                    yield "hook"
                emit_ln_front_a(tt, lns)
        if final_group:
            lt = tiles_out - 1
            emit_ln_front_b(lt, lns)
            if lt >= 1:
                emit_ln_apply(lt - 1, sv, not last)
            emit_ln_apply(lt, sv, not last)

    def emit_na(b, l, last):
        jn = l // 2
        nt_in = 5
        nt_out = sublayer_tiles(l, 1, "out")
        ctxq = (l == 0)
        P.new_phase()
        lns = carve_lns(Carver(A_LOW))
        cz = Carver(A_LOW)
        QT = cz.bf(2 * T, "QT").rearrange("p (h t) -> p h t", h=2)
        P.ranges["QTZ"] = P.ranges["QT"]
        KT = cz.bf(T, "KT")
        VC = cz.bf(18 * 192, "VC").rearrange("p (c n) -> p c n", c=18)
        P.ranges["VC1"] = P.ranges["VC"]
        PT = [cz.bf(8 * 128, "PT%d" % i) for i in range(4)]
        T1 = [cz.f32(512, "T1%d" % i) for i in range(2)]
        T2 = [cz.f32(512, "T2%d" % i) for i in range(2)]
        REC = [cz.f32(256, "REC%d" % i) for i in range(2)]
        assert cz.off <= A_W
        cz = Carver(A_W)
        WQ = [cz.bf(NCH * 384, "wq%d" % i).rearrange("p (k n) -> p k n", k=NCH) for i in range(2)]
        WOg = cz.bf(2 * 1024, "wo").rearrange("p (k n) -> p k n", k=2)
        TB = cz.bf(2 * TBW, "TB").rearrange("p (h n) -> p h n", h=2)
        assert cz.off <= A_S
        cz = Carver(A_S)
        OT = cz.bf(2 * T, "OT").rearrange("p (k n) -> p k n", k=2)
        RC = cz.f32(TL, "RC")
        RS = cz.f32(TL, "RS")
        sv = emit_sv(b, l, 1, last)
        yield "ready"
        P.add("sp", lambda e: e.dma_start(out=RC, in_=ropec), w=[("RC",)], dma="ropec")
        P.add("sp", lambda e: e.dma_start(out=RS, in_=ropes), w=[("RS",)], dma="ropes")
        P.add("pool", lambda e: e.memset(VC[:, :, 64:128], 1.0), w=[("VC1",)])
        P.add("pool", lambda e: e.memset(QT[64:128, 0, :], 0.0), w=[("QTZ",)])
        P.add("pool", lambda e: e.memset(QT[0:64, 1, :], 0.0), w=[("QTZ",)])

        def rope(tt, bk, dst, scale, which):
            t0, n = TILES[tt]
            a, bq = T1[which], T2[which]

            def d1(e):
                e.scalar_tensor_tensor(out=a[:, :n], in0=PS[:, bk, :n], scalar=scale, in1=RC[:, t0:t0 + n],
                                       op0=ALU.mult, op1=ALU.mult)
                ins = None
                for q in range(4):
                    isl = slice(32 * (q ^ 1), 32 * (q ^ 1) + 32)
                    osl = slice(32 * q, 32 * q + 32)
                    ins = e.scalar_tensor_tensor(out=bq[osl, :n], in0=PS[isl, bk, :n], scalar=scale,
                                                 in1=RS[isl, t0:t0 + n], op0=ALU.mult, op1=ALU.mult)
                return ins
            P.add("dve", d1, r=[("RC",), ("RS",)], w=[pst(bk), ("T1%d" % which,), ("T2%d" % which,)])
            if which == 1:
                P.add("pool", lambda e: e.tensor_tensor(dst[:, t0:t0 + n], a[:, :n], bq[:, :n], op=ALU.add),
                      r=[("T1%d" % which,), ("T2%d" % which,)], w=[("KT", tt)])
            else:
                def qadd(e):
                    e.tensor_tensor(QT[0:64, 0, t0:t0 + n], a[0:64, :n], bq[0:64, :n], op=ALU.add)
                    return e.tensor_tensor(QT[64:128, 1, t0:t0 + n], a[64:128, :n], bq[64:128, :n], op=ALU.add)
                P.add("pool", qadd, r=[("T1%d" % which,), ("T2%d" % which,)], w=[("QT", tt)])

        ucount = [0]

        def emit_st(u):
            kind = u["kind"]
            if kind == "ctx":
                nq, q0 = 256, TL
                tcs, boff = [], None
            else:
                p = kind
                nq, q0 = 128, 128 * p
                if 2 <= p <= 13:
                    tcs, boff = list(range(p - 2, p + 3)), 0
                elif p < 2:
                    tcs, boff = [0, 1, 2, 3], 640 + 512 * p
                else:
                    tcs, boff = [12, 13, 14, 15], 640 + 512 * (p - 12)
            allc = tcs + [16, 17]
            u["allc"], u["nq"], u["q0"] = allc, nq, q0
            w2 = 2 * nq
            per = 1024 // w2
            halves = [list(range(i, min(i + per, len(allc)))) for i in range(0, len(allc), per)]
            u["pts"] = {}
            for hi, idxs in enumerate(halves):
                buf = ucount[0] % 2
                ucount[0] += 1
                ptb = (u["n"] % 2) * 2 + hi
                stv = PS[:, 2 * buf:2 * buf + 2, :].rearrange("p b (s q) -> p (b s) q", q=w2)
                ptv = PT[ptb].rearrange("p (s q) -> p s q", q=w2)

                def mm(e, idxs=idxs, stv=stv):
                    ins = None
                    for s_, i in enumerate(idxs):
                        tc = allc[i]
                        hasb = i < len(tcs)
                        ins = e.matmul(stv[:, s_, :], lhsT=KT[:, tc * 128:(tc + 1) * 128],
                                       rhs=QT[:, :, q0:q0 + nq], start=True, stop=not hasb)
                        if hasb:
                            for hh in range(2):
                                ins = e.matmul(stv[:, s_, hh * nq:(hh + 1) * nq],
                                               lhsT=TB[:, hh, boff + i * 128:boff + (i + 1) * 128],
                                               rhs=IDB[:, :], start=False, stop=(hh == 1))
                    return ins
                banks = [pst(2 * buf), pst(2 * buf + 1)]
                P.add("pe", mm, r=[("QT", t) for t in range(5)] + [("KT", t) for t in range(5)]
                      + [("TB",), ("IDB",), ("QTZ",)], w=banks)
                ns = len(idxs)
                P.add("act", lambda e, ns=ns, stv=stv, ptv=ptv: e.activation(
                    out=ptv[:, 0:ns, :], in_=stv[:, 0:ns, :], func=AF.Exp), w=banks + [("PT%d" % ptb,)])
                for s_, i in enumerate(idxs):
                    u["pts"][i] = (ptb, ptv, s_)

        def emit_pv(u, ci):
            kind = u["kind"]
            allc, nq, q0 = u["allc"], u["nq"], u["q0"]
            tt = 4 if kind == "ctx" else kind // 4
            ptbs = sorted({v[0] for v in u["pts"].values()})
            for hh in range(2):
                hoff = 64 * hh
                osl = slice(64 * hh, 64 * hh + 64)
                rsl = slice(64 * (1 - hh), 64 * (1 - hh) + 64)
                bk = 4 + 2 * (u["n"] % 2) + hh

                def mm(e, hh=hh, hoff=hoff, bk=bk):
                    ins = None
                    for i, tc in enumerate(allc):
                        _, ptv, s_ = u["pts"][i]
                        ins = e.matmul(PS[:, bk, :nq], lhsT=VC[:, tc, hoff:hoff + 128],
                                       rhs=ptv[:, s_, hh * nq:(hh + 1) * nq],
                                       start=(i == 0), stop=(i == len(allc) - 1))
                    return ins
                P.add("pe", mm, r=[("PT%d" % x,) for x in ptbs] + [("VC1",)] + [("VC", t) for t in range(5)],
                      w=[pst(bk)])
                rb = REC[hh]

                P.add("dve", lambda e, osl=osl, rsl=rsl, bk=bk, rb=rb: e.reciprocal(rb[osl, :nq], PS[rsl, bk, :nq]),
                      w=[pst(bk), ("REC%d" % hh,)])
                P.add("dve", lambda e, osl=osl, bk=bk, rb=rb: e.tensor_tensor(
                    OT[osl, ci, q0:q0 + nq], PS[osl, bk, :nq], rb[osl, :nq], op=ALU.mult),
                    r=[("REC%d" % hh,)], w=[pst(bk), ("OT", ci, tt)])

        def load_wq(c):
            P.add("pool", lambda e: e.dma_start(
                out=WQ[c % 2], in_=wqkv[jn, c].rearrange("p (k n) -> p k n", k=NCH)),
                w=[("wq%d" % (c % 2),)], dma="wq%d" % (c % 2))
        load_wq(0)
        for c in range(8):
            slot = c % 2
            ci = c % 2
            if c + 1 < 8:
                load_wq(c + 1)
            P.add("pool", lambda e, c=c: e.dma_start(
                out=TB, in_=tbias[jn, c].rearrange("p (h n) -> p h n", h=2)), w=[("TB",)], dma="tb")
            if ci == 0:
                P.add("pool", lambda e, c=c: e.dma_start(
                    out=WOg, in_=wo[jn, c // 2].rearrange("p (k n) -> p k n", k=2)), w=[("wo",)], dma="wo")
            for tt in range(nt_in):
                t0, n = TILES[tt]
                for which in range(2):
                    def mm(e, which=which, t0=t0, n=n, slot=slot):
                        ins = None
                        for k in range(NCH):
                            ins = e.matmul(PS[:, which, :n], lhsT=WQ[slot][:, k, which * 128:(which + 1) * 128],
                                           rhs=XM[:, k, t0:t0 + n], start=(k == 0), stop=(k == NCH - 1))
                        return ins
                    P.add("pe", mm, r=[("wq%d" % slot,)] + xmtok(tt), w=[pst(which)])
                    dst = QT if which == 0 else KT
                    scale = 0.125 if which == 0 else 1.0
                    if tt < 4:
                        rope(tt, which, dst, scale, which)
                    elif which == 1:
                        P.add("act", lambda e, which=which, dst=dst, scale=scale, t0=t0, n=n: e.activation(
                            out=dst[:, t0:t0 + n], in_=PS[:, which, :n], func=AF.Copy, scale=scale),
                            w=[pst(which), ("KT", tt)])
                    else:
                        def qcp(e, t0=t0, n=n, scale=scale):
                            e.activation(out=QT[0:64, 0, t0:t0 + n], in_=PS[0:64, 0, :n], func=AF.Copy, scale=scale)
                            return e.activation(out=QT[64:128, 1, t0:t0 + n], in_=PS[64:128, 0, :n],
                                                func=AF.Copy, scale=scale)
                        P.add("act", qcp, w=[pst(0), ("QT", tt)])
                ntc = n // 128

                def mmv(e, t0=t0, ntc=ntc, slot=slot):
                    ins = None
                    for tcl in range(ntc):
                        for k in range(NCH):
                            ins = e.matmul(PS[:, 2, tcl * 128:(tcl + 1) * 128],
                                           lhsT=XM[:, k, t0 + tcl * 128:t0 + (tcl + 1) * 128],
                                           rhs=WQ[slot][:, k, 256:384], start=(k == 0), stop=(k == NCH - 1))
                    return ins
                P.add("pe", mmv, r=[("wq%d" % slot,)] + xmtok(tt), w=[pst(2)])
                tc0 = t0 // 128

                def ev(e, tc0=tc0, ntc=ntc):
                    src = PS[:, 2, 0:ntc * 128].rearrange("p (c n) -> p c n", n=128)
                    e.activation(out=VC[:, tc0:tc0 + ntc, 0:64], in_=src[:, :, 0:64], func=AF.Copy)
                    return e.activation(out=VC[:, tc0:tc0 + ntc, 128:192], in_=src[:, :, 64:128], func=AF.Copy)
                P.add("act", ev, w=[pst(2), ("VC", tt)])
            if debug and c == 0 and b == 0:
                P.add("sp", lambda e: e.dma_start(out=dbg_qt, in_=QT[:, 0, :]), r=[("QT", t) for t in range(5)], dma="dbg1")
                P.add("sp", lambda e: e.dma_start(out=dbg_kt, in_=KT), r=[("KT", t) for t in range(5)], dma="dbg2")
                P.add("sp", lambda e: e.dma_start(out=dbg_vc, in_=VC.rearrange("p c n -> p (c n)")),
                      r=[("VC", t) for t in range(5)] + [("VC1",)], dma="dbg3")
                P.add("sp", lambda e: e.dma_start(out=dbg_xm, in_=XM[:]),
                      r=[t for tt in range(5) for t in xmtok(tt)], dma="dbg4")
            units = [{"kind": p} for p in range(16)]
            if ctxq:
                units.append({"kind": "ctx"})
            prev = None
            for un, u in enumerate(units):
                u["n"] = un
                emit_st(u)
                if prev is not None:
                    emit_pv(prev, ci)
                prev = u
            emit_pv(prev, ci)
            if debug and c == 0 and b == 0:
                P.add("sp", lambda e: e.dma_start(out=dbg_ot, in_=OT[:, 0, :]),
                      r=[("OT", 0, t) for t in range(5)], dma="dbg5")
            if ci == 1:
                yield from emit_outproj(WOg, OT, nt_out, lns, last, c == 7, sv)

    def emit_conv(b, l, last):
        jn = l // 2
        nt = sublayer_tiles(l, 1, "out")
        P.new_phase()
        lns = carve_lns(Carver(A_LOW))
        cz = Carver(A_W)
        WC = [cz.bf(NCH * 384, "wc%d" % i).rearrange("p (k n) -> p k n", k=NCH) for i in range(2)]
        WOg = cz.bf(2 * 1024, "wo").rearrange("p (k n) -> p k n", k=2)
        cz = Carver(A_S)
        BB = cz.bf(T, "BB")
        OT = cz.bf(2 * T, "OT").rearrange("p (k n) -> p k n", k=2)
        CG = [cz.f32(512, "CG%d" % i) for i in range(2)]
        ACC = [cz.f32(512, "ACC%d" % i) for i in range(2)]
        cz = Carver(12288)
        U = cz.f32(T + 8, "U")
        assert cz.off <= A_W
        P.ranges["Upad"] = P.ranges["U"]
        sv = emit_sv(b, l, 1, last)
        P.add("pool", lambda e: e.memset(U[:, 0:1], 0.0), w=[("Upad",)])
        P.add("pool", lambda e: e.memset(U[:, 2049:2051], 0.0), w=[("Upad",)])
        P.add("pool", lambda e: e.memset(U[:, 2307:2308], 0.0), w=[("Upad",)])

        def upos(t):
            return t + 1 if t < TL else t + 3

        def load_wc(c):
            P.add("pool", lambda e: e.dma_start(
                out=WC[c % 2], in_=wcin[jn, c].rearrange("p (k n) -> p k n", k=NCH)),
                w=[("wc%d" % (c % 2),)], dma="wq%d" % (c % 2))
        load_wc(0)
        yield "ready"
        for c in range(8):
            slot = c % 2
            ci = c % 2
            if c + 1 < 8:
                load_wc(c + 1)
            if ci == 0:
                P.add("pool", lambda e, c=c: e.dma_start(
                    out=WOg, in_=wco[jn, c // 2].rearrange("p (k n) -> p k n", k=2)), w=[("wo",)], dma="wo")
            for tt in range(nt):
                t0, n = TILES[tt]
                for which in range(3):
                    def mm(e, which=which, t0=t0, n=n, slot=slot):
                        ins = None
                        for k in range(NCH):
                            ins = e.matmul(PS[:, which, :n], lhsT=WC[slot][:, k, which * 128:(which + 1) * 128],
                                           rhs=XM[:, k, t0:t0 + n], start=(k == 0), stop=(k == NCH - 1))
                        return ins
                    P.add("pe", mm, r=[("wc%d" % slot,)] + xmtok(tt), w=[pst(which)])
                P.add("act", lambda e, t0=t0, n=n: e.activation(out=BB[:, t0:t0 + n], in_=PS[:, 0, :n], func=AF.Copy),
                      w=[pst(0), ("BB", tt)])
                cg = CG[tt % 2]
                P.add("act", lambda e, n=n, cg=cg: e.activation(out=cg[:, :n], in_=PS[:, 1, :n], func=AF.Copy),
                      w=[pst(1), ("CG%d" % (tt % 2),)])
                u0 = upos(t0)
                P.add("dve", lambda e, n=n, cg=cg, u0=u0: e.tensor_tensor(
                    U[:, u0:u0 + n], PS[:, 2, :n], cg[:, :n], op=ALU.mult),
                    r=[("CG%d" % (tt % 2),), ("Upad",)], w=[pst(2), ("U", tt)])
            for tt in range(nt):
                t0, n = TILES[tt]
                u0 = upos(t0)
                acc = ACC[tt % 2]
                if tt == 4:
                    rd = [("U", 4)]
                else:
                    rd = [("U", x) for x in range(max(tt - 1, 0), min(tt + 1, 3) + 1)]

                atok = ("ACC%d" % (tt % 2),)
                P.add("dve", lambda e, n=n, u0=u0, acc=acc, c=c: e.tensor_scalar(
                    acc[:, :n], U[:, u0 - 1:u0 - 1 + n], CONVW[:, jn, 0, c:c + 1], None, op0=ALU.mult),
                    r=rd + [("Upad",)], w=[atok])
                for tap in (1, 2):
                    P.add("dve", lambda e, n=n, u0=u0, acc=acc, c=c, tap=tap: e.scalar_tensor_tensor(
                        out=acc[:, :n], in0=U[:, u0 - 1 + tap:u0 - 1 + tap + n], scalar=CONVW[:, jn, tap, c:c + 1],
                        in1=acc[:, :n], op0=ALU.mult, op1=ALU.add), r=rd + [("Upad",), atok], w=[atok])
                P.add("pool", lambda e, t0=t0, n=n, acc=acc, ci=ci: e.tensor_tensor(
                    OT[:, ci, t0:t0 + n], acc[:, :n], BB[:, t0:t0 + n], op=ALU.mult),
                    r=[("ACC%d" % (tt % 2),), ("BB", tt)], w=[("OT", ci, tt)])
            if ci == 1:
                yield from emit_outproj(WOg, OT, nt, lns, last, c == 7, sv)

    subs = [(l, s) for l in range(DEPTH) for s in range(3)]
    if stop is not None:
        subs = subs[:subs.index(tuple(stop)) + 1]

    for b in range(nb):
        for tt in range(5):
            t0, n = TILES[tt]
            P.add("sp", lambda e, b=b, t0=t0, n=n: e.dma_start(out=X[:, :, t0:t0 + n], in_=xin[b][:, :, t0:t0 + n]),
                  w=xtok(tt), dma="xin%d" % tt)
        for ci, col in enumerate((b, 2)):
            P.add("dve", lambda e, ci=ci, col=col: e.tensor_scalar(
                SVI[:, ci, :], modv(0, 1, col), 1.0, None, op0=ALU.add), w=[("SVI",)])
        for tt in range(5):
            t0, n = TILES[tt]
            ci = 1 if tt == 4 else 0
            col = 2 if tt == 4 else b
            for c in range(NCH):
                P.add("pool", lambda e, c=c, t0=t0, n=n, ci=ci, col=col: e.tensor_scalar(
                    XM[:, c, t0:t0 + n], X[:, c, t0:t0 + n], SVI[:, ci, c:c + 1], MOD[:, 0, c, col:col + 1],
                    op0=ALU.mult, op1=ALU.add), r=[("X", c, tt), ("SVI",)], w=[("XM", c, tt)])
                P.add("act", lambda e, c=c, t0=t0, n=n: e.activation(
                    out=X[:, c, t0:t0 + n], in_=X[:, c, t0:t0 + n], func=AF.Copy, scale=ALPHA),
                    w=[("X", c, tt)])
        gens = []
        for (l, s) in subs:
            last = (l, s) == subs[-1]
            if s != 1:
                gens.append(("ffn", emit_ffn(b, l, s, last)))
            elif l % 2 == 0:
                gens.append(("na", emit_na(b, l, last)))
            else:
                gens.append(("conv", emit_conv(b, l, last)))
        assert next(gens[0][1]) == "ready"
        for gi, (kind, g) in enumerate(gens):
            assert next(g) == "hook"
            nxt = gens[gi + 1] if gi + 1 < len(gens) else None
            hoist = nxt is not None and nxt[0] != "na"
            if hoist:
                assert next(nxt[1]) == "ready"
            for _ in g:
                raise AssertionError("unexpected extra yield")
            if nxt is not None and not hoist:
                assert next(nxt[1]) == "ready"
        for tt in range(4):
            t0, n = TILES[tt]
            P.add("sp", lambda e, b=b, t0=t0, n=n: e.dma_start(out=out[b][:, :, t0:t0 + n], in_=X[:, :, t0:t0 + n]),
                  r=xtok(tt), dma="out%d_%d" % (b, tt))
        if debug and b == 0:
            P.add("sp", lambda e: e.dma_start(out=dbg_x, in_=X[:]), dma="outdbg1")
            P.add("sp", lambda e: e.dma_start(out=dbg_mod, in_=MOD[:]), dma="outdbg2")

    dma_keys = sorted({op.dma for op in P.ops if op.dma is not None})
    sems = {}
    dsems = {}
    for e in Prog.ENGS:
        cm = nc.semaphore("s_" + e)
        sems[e] = cm.__enter__()
        ctxs.append(cm)
    for k in dma_keys:
        cm = nc.semaphore("d_" + k)
        dsems[k] = cm.__enter__()
        ctxs.append(cm)
    with nc.Block() as block:
        P.emit(nc, block, sems, dsems)
    return nc, P


def _fm(a2d):
    n = a2d.shape[1]
    return np.ascontiguousarray(a2d.reshape(NCH, 128, n).transpose(1, 0, 2))


def _bias_tables(rpb):
    out = np.full((16, 128, TBW), NEG, np.float32)
    qc = np.arange(64)
    wc0 = np.clip(qc - 8, 0, 48)
    kc = np.arange(64)
    colok = (kc[None, :] >= wc0[:, None]) & (kc[None, :] < wc0[:, None] + 16)
    dx = np.clip(kc[None, :] - qc[:, None] + 15, 0, 30)

    def fill(dst, r0, kbase, nrows, rs_of):
        for qi in range(2):
            qr = r0 + qi
            rs = rs_of(qr)
            for e in range(nrows):
                kr = kbase + e
                if rs <= kr < rs + 8:
                    dy = kr - qr + 7
                    vals = rpb[:, dy, :][:, dx]
                    blk = np.where(colok[None], vals, NEG)
                    dst[:, qi * 64:(qi + 1) * 64, e * 64:(e + 1) * 64] = blk

    rs_of = lambda r: int(np.clip(r - 4, 0, 24))
    fill(out[:, :, 0:640], 12, 8, 10, rs_of)
    fill(out[:, :, 640:1152], 0, 0, 8, rs_of)
    fill(out[:, :, 1152:1664], 2, 0, 8, rs_of)
    fill(out[:, :, 1664:2176], 28, 24, 8, rs_of)
    fill(out[:, :, 2176:2688], 30, 24, 8, rs_of)
    return np.ascontiguousarray(out.reshape(8, 2, 128, TBW).transpose(0, 2, 1, 3)).reshape(8, 128, 2 * TBW)


def _rope_tables():
    t = np.arange(TL)
    nf = 16
    inv = 10000.0 ** (-np.arange(nf) / nf)
    ang_r = (t // GRID_W)[:, None] * inv[None]
    ang_c = (t % GRID_W)[:, None] * inv[None]
    cr, sr = np.cos(ang_r).astype(np.float32), np.sin(ang_r).astype(np.float32)
    cc, sc = np.cos(ang_c).astype(np.float32), np.sin(ang_c).astype(np.float32)
    C = np.zeros((128, TL), np.float32)
    S = np.zeros((128, TL), np.float32)
    for p in range(128):
        n = p % 64
        m = n % 32
        cs, sn = (cr[:, m], sr[:, m]) if m < 16 else (cc[:, m - 16], sc[:, m - 16])
        C[p] = cs
        S[p] = sn if n < 32 else -sn
    return C, S


_PERM = np.concatenate([np.arange(0, 16), np.arange(32, 48), np.arange(16, 32), np.arange(48, 64)])


def prepare_inputs(x, c, ctx, c_ctx, ada_w, ada_b, ln_g, ln_b, ffn_w_in, ffn_w_out,
                   na_w_qkv, na_w_out, na_rpb, sc_w_in, sc_conv, sc_w_out):
    f = lambda a: np.asarray(a, np.float32)
    x, c, ctx, c_ctx = f(x), f(c), f(ctx), f(c_ctx)
    ada_w, ada_b, ln_g, ln_b = f(ada_w), f(ada_b), f(ln_g), f(ln_b)
    ffn_w_in, ffn_w_out = f(ffn_w_in), f(ffn_w_out)
    na_w_qkv, na_w_out, na_rpb = f(na_w_qkv), f(na_w_out), f(na_rpb)
    sc_w_in, sc_conv, sc_w_out = f(sc_w_in), f(sc_conv), f(sc_w_out)
    sh = {}
    a = ada_w.reshape(DEPTH, NCH, 128, 18, 512).transpose(0, 3, 2, 1, 4)
    sh["adaw"] = np.ascontiguousarray(a).reshape(DEPTH, 18, 128, NCH * 512)
    sh["adab"] = np.ascontiguousarray(ada_b.reshape(DEPTH, 72, 128).transpose(2, 0, 1))
    sh["lng"] = np.ascontiguousarray(ln_g.reshape(DEPTH, 3, NCH, 128).transpose(3, 0, 1, 2))
    sh["lnb"] = np.ascontiguousarray(ln_b.reshape(DEPTH, 3, NCH, 128).transpose(3, 0, 1, 2))
    winb = np.zeros((DEPTH, 2, NFB, 128, NCH, 1024), np.float32)
    woutb = np.zeros((DEPTH, 2, NFB, 128, 4, 1024), np.float32)
    wi5 = ffn_w_in.reshape(DEPTH, 2, NCH, 128, 2 * FF)
    wo5 = ffn_w_out.reshape(DEPTH, 2, FF // 128, 128, D)
    for j in range(NFB):
        nbk, o = BLK[j], BOFF[j] * 128
        a = wi5[..., o:o + nbk * 128]
        u = wi5[..., FF + o:FF + o + nbk * 128]
        blk = np.concatenate([a, u], axis=-1)
        winb[:, :, j, :, :, 0:2 * nbk * 128] = blk.transpose(0, 1, 3, 2, 4)
        woutb[:, :, j, :, 0:nbk, :] = wo5[:, :, BOFF[j]:BOFF[j] + nbk].transpose(0, 1, 3, 2, 4)
    sh["win"] = winb.reshape(DEPTH, 2, NFB, 128, NCH * 1024)
    sh["wout"] = woutb.reshape(DEPTH, 2, NFB, 128, 4 * 1024)
    cols = []
    for cch in range(8):
        qcols = np.concatenate([(2 * cch) * 64 + _PERM, (2 * cch + 1) * 64 + _PERM])
        cols.append(np.concatenate([qcols, 1024 + qcols, 2048 + cch * 128 + np.arange(128)]))
    cols = np.stack(cols)
    wq = na_w_qkv[:, :, cols]
    wq = wq.reshape(2, NCH, 128, 8, 384).transpose(0, 3, 2, 1, 4)
    sh["wqkv"] = np.ascontiguousarray(wq).reshape(2, 8, 128, NCH * 384)
    w2 = na_w_out.reshape(2, 4, 2, 128, D).transpose(0, 1, 3, 2, 4)
    sh["wo"] = np.ascontiguousarray(w2).reshape(2, 4, 128, 2 * 1024)
    sh["tbias"] = np.stack([_bias_tables(na_rpb[j]) for j in range(2)])
    ccols = np.stack([np.concatenate([cch * 128 + np.arange(128), 1024 + cch * 128 + np.arange(128),
                                      2048 + cch * 128 + np.arange(128)]) for cch in range(8)])
    wc = sc_w_in[:, :, ccols].reshape(2, NCH, 128, 8, 384).transpose(0, 3, 2, 1, 4)
    sh["wcin"] = np.ascontiguousarray(wc).reshape(2, 8, 128, NCH * 384)
    w3 = sc_w_out.reshape(2, 4, 2, 128, D).transpose(0, 1, 3, 2, 4)
    sh["wco"] = np.ascontiguousarray(w3).reshape(2, 4, 128, 2 * 1024)
    sh["convw"] = np.ascontiguousarray(sc_conv.reshape(2, 3, NCH, 128).transpose(3, 0, 1, 2))
    C, S = _rope_tables()
    sh["ropec"], sh["ropes"] = C, S
    sh["ident"] = np.eye(128, dtype=np.float32)
    in_maps = []
    for i in range(8):
        m = dict(sh)
        xs = []
        for bb in range(2):
            full = np.concatenate([x[2 * i + bb], ctx[2 * i + bb]], axis=0)
            xs.append(_fm(np.ascontiguousarray(full.T)))
        m["xin"] = np.stack(xs)
        cvv = np.stack([c[2 * i], c[2 * i + 1], c_ctx], axis=1)
        m["cv"] = _fm(cvv)
        in_maps.append(m)
    return in_maps


def assemble_output(results):
    outs = []
    for i in range(8):
        o = np.asarray(results[i]["out"])
        for bb in range(2):
            outs.append(o[bb].transpose(2, 1, 0).reshape(TL, D))
    return np.ascontiguousarray(np.stack(outs)).astype(np.float32)


_CACHE = {}


def kernel(x, c, ctx, c_ctx, ada_w, ada_b, ln_g, ln_b, ffn_w_in, ffn_w_out,
           na_w_qkv, na_w_out, na_rpb, sc_w_in, sc_conv, sc_w_out):
    in_maps = prepare_inputs(x, c, ctx, c_ctx, ada_w, ada_b, ln_g, ln_b, ffn_w_in, ffn_w_out,
                             na_w_qkv, na_w_out, na_rpb, sc_w_in, sc_conv, sc_w_out)
    if "nc" not in _CACHE:
        _CACHE["nc"] = build_program()[0]
    res = run_bass_kernel_spmd(_CACHE["nc"], in_maps, core_ids=list(range(8)))
    return assemble_output(res.results)
```

```python
import numpy as np
import concourse.bass as bass
import concourse.mybir as mybir
from concourse.bass_utils import run_bass_kernel_spmd

F32 = mybir.dt.float32
BF16 = mybir.dt.bfloat16
AF = mybir.ActivationFunctionType
ALU = mybir.AluOpType

D = 1024
NCH = 8
TL = 2048
TCX = 256
T = TL + TCX
FF = 2816
BLK = [2, 4, 4, 4, 4, 4]
BOFF = [0, 2, 6, 10, 14, 18]
NFB = len(BLK)
DEPTH = 4
ALPHA = float((2.0 * DEPTH) ** 0.25)
EPS = 1e-6
NEG = -30000.0
GRID_W = 64
TILES = [(0, 512), (512, 512), (1024, 512), (1536, 512), (2048, 256)]
TBW = 640 + 4 * 512
ARENA = 46592


class Op:
    __slots__ = ("eng", "fn", "deps", "dma", "idx", "waits", "sig", "sigval")


class Prog:
    ENGS = ["pe", "act", "dve", "pool", "sp"]

    def __init__(self):
        self.ops = []
        self.lastw = {}
        self.readers = {}
        self.pending_bar = {}
        self.last_of = {}
        self.dmas_since_bar = []
        self.ranges = {}
        self.stale = []
        self.stale_done = set()
        self.phase_no = 0

    def add(self, eng, fn, r=(), w=(), dma=None):
        op = Op()
        op.eng, op.fn, op.dma, op.idx = eng, fn, dma, len(self.ops)
        deps = {}
        for t in r:
            x = self.lastw.get(t)
            if x is not None:
                deps[x] = "raw"
        for t in w:
            x = self.lastw.get(t)
            if x is not None and x not in deps:
                deps[x] = "waw"
            for y in self.readers.get(t, ()):
                if y not in deps:
                    deps[y] = "war"
        if eng in self.pending_bar:
            for x in self.pending_bar.pop(eng):
                deps[x] = "bar" if deps.get(x) != "raw" else "raw"
        if self.stale:
            for t in list(r) + list(w):
                rg = self.ranges.get(t[0])
                if rg is None or (eng, t[0]) in self.stale_done:
                    continue
                self.stale_done.add((eng, t[0]))
                for (_, a, b_, opset) in self.stale:
                    if a < rg[1] and rg[0] < b_:
                        for x in opset:
                            if x not in deps:
                                deps[x] = "war"
        op.deps = deps
        for t in r:
            self.readers.setdefault(t, []).append(op.idx)
        for t in w:
            self.lastw[t] = op.idx
            self.readers[t] = []
        self.ops.append(op)
        if dma is None:
            self.last_of[eng] = op.idx
        else:
            self.dmas_since_bar.append(op.idx)
        return op.idx

    def new_phase(self):
        self.phase_no += 1
        byname = {}
        for t in list(self.lastw.keys()):
            if t[0] in self.ranges:
                byname.setdefault(t[0], set()).add(self.lastw.pop(t))
        for t in list(self.readers.keys()):
            if t[0] in self.ranges:
                byname.setdefault(t[0], set()).update(self.readers.pop(t))
        for nm, opset in byname.items():
            a, b_ = self.ranges[nm]
            self.stale.append((self.phase_no, a, b_, opset))
        self.stale = [x for x in self.stale if x[0] >= self.phase_no - 2]
        self.ranges = {}
        self.stale_done = set()

    def barrier(self):
        deps = set(self.last_of.values()) | set(self.dmas_since_bar)
        self.pending_bar = {e: set(deps) for e in self.ENGS}
        self.dmas_since_bar = []
        self.lastw = {}
        self.readers = {}

    def emit(self, nc, block, sems, dma_sems):
        ops = self.ops
        need = set()
        for op in ops:
            waits = []
            for d, kind in op.deps.items():
                dop = ops[d]
                if dop.dma is not None:
                    waits.append(d)
                elif dop.eng == op.eng:
                    if op.dma is not None:
                        waits.append(d)
                    elif op.eng == "pe":
                        continue
                    elif kind == "raw":
                        waits.append(d)
                else:
                    waits.append(d)
            op.waits = waits
            for d in waits:
                if ops[d].dma is None:
                    need.add(d)
        cnt = {e: 0 for e in self.ENGS}
        dcnt = {}
        for op in ops:
            if op.dma is not None:
                dcnt[op.dma] = dcnt.get(op.dma, 0) + 16
                op.sigval = dcnt[op.dma]
                op.sig = True
            elif op.idx in need:
                cnt[op.eng] += 1
                op.sigval = cnt[op.eng]
                op.sig = True
            else:
                op.sig = False
                op.sigval = 0
        self.final_counts = (cnt, dcnt)

        def run_engine(engname, eng):
            waited = {}
            for op in ops:
                if op.eng != engname:
                    continue
                for d in op.waits:
                    dop = ops[d]
                    key = ("d", dop.dma) if dop.dma is not None else ("e", dop.eng)
                    if waited.get(key, 0) < dop.sigval:
                        s = dma_sems[dop.dma] if dop.dma is not None else sems[dop.eng]
                        eng.wait_ge(s, dop.sigval)
                        waited[key] = dop.sigval
                ins = op.fn(eng)
                if op.sig:
                    if op.dma is not None:
                        ins.then_inc(dma_sems[op.dma], 16)
                    else:
                        ins.then_inc(sems[op.eng], 1)
            for e in self.ENGS:
                if cnt[e] > 0 and waited.get(("e", e), 0) < cnt[e] and e != engname:
                    pass
            if engname == "sp":
                for k, v in dcnt.items():
                    if k.startswith("out"):
                        eng.wait_ge(dma_sems[k], v)

        @block.tensor
        def _(e):
            run_engine("pe", e)

        @block.scalar
        def _(e):
            run_engine("act", e)

        @block.vector
        def _(e):
            run_engine("dve", e)

        @block.gpsimd
        def _(e):
            run_engine("pool", e)

        @block.sync
        def _(e):
            run_engine("sp", e)


def sublayer_tiles(l, s, what):
    if l <= 1:
        return 5
    if l == 2:
        if s == 0:
            return 5
        if s == 1:
            return 5 if what == "in" else 4
        return 4
    return 4


def build_program(stop=None, nb=2, debug=False):
    nc = bass.Bass("TRN2", target_bir_lowering=False)

    def dram(name, shape, kind="ExternalInput", dt=F32):
        return nc.dram_tensor(name, list(shape), dt, kind=kind).ap()

    xin = dram("xin", [2, 128, NCH, T])
    cv = dram("cv", [128, NCH, 3])
    adaw = dram("adaw", [DEPTH, 18, 128, NCH * 512])
    adab = dram("adab", [128, DEPTH, 72])
    lng = dram("lng", [128, DEPTH, 3, NCH])
    lnb = dram("lnb", [128, DEPTH, 3, NCH])
    win = dram("win", [DEPTH, 2, NFB, 128, NCH * 1024])
    wout = dram("wout", [DEPTH, 2, NFB, 128, 4 * 1024])
    wqkv = dram("wqkv", [2, 8, 128, NCH * 384])
    wo = dram("wo", [2, 4, 128, 2 * 1024])
    tbias = dram("tbias", [2, 8, 128, 2 * TBW])
    wcin = dram("wcin", [2, 8, 128, NCH * 384])
    wco = dram("wco", [2, 4, 128, 2 * 1024])
    convw = dram("convw", [128, 2, 3, NCH])
    ropec = dram("ropec", [128, TL])
    ropes = dram("ropes", [128, TL])
    ident = dram("ident", [128, 128])
    out = dram("out", [2, 128, NCH, TL], kind="ExternalOutput")
    if debug:
        dbg_qt = dram("dbg_qt", [128, T], kind="ExternalOutput", dt=BF16)
        dbg_kt = dram("dbg_kt", [128, T], kind="ExternalOutput", dt=BF16)
        dbg_vc = dram("dbg_vc", [128, 18 * 192], kind="ExternalOutput", dt=BF16)
        dbg_ot = dram("dbg_ot", [128, T], kind="ExternalOutput", dt=BF16)
        dbg_xm = dram("dbg_xm", [128, NCH, T], kind="ExternalOutput", dt=BF16)
        dbg_x = dram("dbg_x", [128, NCH, T], kind="ExternalOutput")
        dbg_mod = dram("dbg_mod", [128, DEPTH, 72, 3], kind="ExternalOutput")

    P = Prog()
    ctxs = []

    def sb(name, shape, dt):
        cm = nc.sbuf_tensor(name, list(shape), dt)
        t = cm.__enter__()
        ctxs.append(cm)
        return t

    X = sb("X", [128, NCH, T], F32)
    XM = sb("XM", [128, NCH, T], BF16)
    AR = sb("AR", [128, ARENA], BF16)
    MOD = sb("MOD", [128, DEPTH, 72, 3], F32)
    ADAB = sb("ADAB", [128, DEPTH, 72], F32)
    LNG = sb("LNG", [128, DEPTH, 3, NCH], F32)
    LNB = sb("LNB", [128, DEPTH, 3, NCH], F32)
    CONVW = sb("CONVW", [128, 2, 3, NCH], F32)
    IDB = sb("IDB", [128, 128], BF16)
    IDF = sb("IDF", [128, 128], F32)
    ONESD = sb("ONESD", [128, 128], BF16)
    MHALF = sb("MHALF", [128, 1], F32)
    SVS = [sb("SV%d" % i, [128, 2, 6, NCH], F32) for i in range(2)]
    SVI = sb("SVI", [128, 2, NCH], F32)
    cur = {"k": 0}
    SC = sb("SC", [128, NCH, 3], F32)
    CVT = sb("CVT", [128, NCH, 3], F32)
    cm = nc.psum_tensor("PS", [128, 8, 512], F32)
    PS = cm.__enter__()
    ctxs.append(cm)

    K_GT, K_GA, K_BA, K_GP, K_BP, K_TMP = range(6)

    class Carver:
        def __init__(self, off=0):
            self.off = off

        def bf(self, n, name=None):
            a = AR[:, self.off:self.off + n]
            if name is not None:
                P.ranges[name] = (self.off, self.off + n)
            self.off += n
            assert self.off <= ARENA, self.off
            return a

        def f32(self, n, name=None):
            if self.off % 2:
                self.off += 1
            a = AR[:, self.off:self.off + 2 * n].bitcast(F32)
            if name is not None:
                P.ranges[name] = (self.off, self.off + 2 * n)
            self.off += 2 * n
            assert self.off <= ARENA, self.off
            return a

    A_LOW = 0
    A_W = 19584
    A_S = 33152

    def bank(i):
        return PS[:, i, :]

    def pst(i):
        return ("ps", i)

    P.add("sp", lambda e: e.dma_start(out=CVT[:], in_=cv), w=[("CVT",)], dma="misc_cvt")
    P.add("sp", lambda e: e.dma_start(out=ADAB[:], in_=adab), w=[("ADAB",)], dma="misc_adab")
    P.add("sp", lambda e: e.dma_start(out=LNG[:], in_=lng), w=[("LNG",)], dma="misc_lng")
    P.add("sp", lambda e: e.dma_start(out=LNB[:], in_=lnb), w=[("LNB",)], dma="misc_lnb")
    P.add("sp", lambda e: e.dma_start(out=CONVW[:], in_=convw), w=[("CONVW",)], dma="misc_convw")
    P.add("pool", lambda e: e.dma_start(out=IDB[:], in_=ident), w=[("IDB",)], dma="misc2")
    P.add("sp", lambda e: e.dma_start(out=IDF[:], in_=ident), w=[("IDF",)], dma="misc_idf")
    P.add("dve", lambda e: e.memset(ONESD[:], 1.0 / 1024.0), w=[("ONESD",)])
    P.add("dve", lambda e: e.memset(MHALF[:], -0.5), w=[("MHALF",)])
    P.add("act", lambda e: e.activation(out=SC[:], in_=CVT[:], func=AF.Silu), r=[("CVT",)], w=[("SC",)])

    cz = Carver()
    AW = [cz.f32(NCH * 512, "aw%d" % i).rearrange("p (k n) -> p k n", k=NCH) for i in range(2)]
    n_ada = 0
    MT = [cz.f32(512, "mt%d" % i) for i in range(2)]
    layers_needed = DEPTH if stop is None else stop[0] + 2
    for l in range(min(DEPTH, layers_needed)):
        for jb in range(18):
            slot = n_ada % 2
            P.add("sp", lambda e, l=l, jb=jb, slot=slot: e.dma_start(
                out=AW[slot], in_=adaw[l, jb].rearrange("p (k n) -> p k n", k=NCH)),
                w=[("aw%d" % slot,)], dma="aw%d" % slot)
            bk = (2 * n_ada) % 8
            bk2 = bk + 1

            def mm(e, slot=slot, bk=bk):
                ins = None
                for k in range(NCH):
                    ins = e.matmul(PS[0:3, bk, :], lhsT=SC[:, k, :], rhs=AW[slot][:, k, :],
                                   start=(k == 0), stop=(k == NCH - 1))
                return ins
            P.add("pe", mm, r=[("aw%d" % slot,), ("SC",)], w=[pst(bk)])
            P.add("act", lambda e, slot=slot, bk=bk: e.activation(out=MT[slot][0:3, :], in_=PS[0:3, bk, :], func=AF.Copy),
                  w=[pst(bk), ("mt%d" % slot,)])

            def tr(e, slot=slot, bk2=bk2):
                ins = None
                for fcl in range(4):
                    ins = e.transpose(PS[:, bk2, fcl * 3:(fcl + 1) * 3], MT[slot][0:3, fcl * 128:(fcl + 1) * 128],
                                      IDF[0:3, 0:3])
                return ins
            P.add("pe", tr, r=[("mt%d" % slot,), ("IDF",)], w=[pst(bk2)])
            P.add("dve", lambda e, l=l, jb=jb, bk2=bk2: e.tensor_tensor(
                MOD[:, l, jb * 4:(jb + 1) * 4, :], PS[:, bk2, 0:12].rearrange("p (f c) -> p f c", c=3),
                ADAB[:, l, jb * 4:(jb + 1) * 4].unsqueeze(2).to_broadcast([128, 4, 3]), op=ALU.add),
                r=[("ADAB",)], w=[pst(bk2), ("MOD",)])
            n_ada += 1
    P.barrier()

    def modv(l, mv, col):
        return MOD[:, l, mv * 8:(mv + 1) * 8, col]

    def next_sub(l, s):
        if s < 2:
            return (l, s + 1)
        if l + 1 < DEPTH:
            return (l + 1, 0)
        return None

    def emit_sv(b, l, s, last):
        w = 0.5 if s != 1 else 1.0
        nxt = None if last else next_sub(l, s)
        cur["k"] += 1
        par = cur["k"] % 2
        SV = SVS[par]
        cur["sv"] = SV
        cur["tag"] = "SVR%d" % par
        tg = lambda x: ("%s_%d" % (x, par),)
        for ci, col in enumerate((b, 2)):
            P.add("dve", lambda e, ci=ci, col=col: e.tensor_scalar(
                SV[:, ci, K_GT, :], modv(l, 3 * s + 2, col), w, None, op0=ALU.mult), w=[tg("SV"), (cur["tag"],)])
            al = ALPHA if nxt is not None else 1.0
            P.add("dve", lambda e, ci=ci, al=al: e.tensor_scalar(
                SV[:, ci, K_GA, :], LNG[:, l, s, :], al, None, op0=ALU.mult), w=[tg("SV")])
            P.add("dve", lambda e, ci=ci, al=al: e.tensor_scalar(
                SV[:, ci, K_BA, :], LNB[:, l, s, :], al, None, op0=ALU.mult), w=[tg("SV")])
            if nxt is not None:
                nl, ns = nxt
                P.add("dve", lambda e, ci=ci, col=col, nl=nl, ns=ns: e.tensor_scalar(
                    SV[:, ci, K_TMP, :], modv(nl, 3 * ns + 1, col), 1.0, None, op0=ALU.add), w=[tg("SV")])
                P.add("dve", lambda e, ci=ci: e.tensor_tensor(
                    SV[:, ci, K_GP, :], LNG[:, l, s, :], SV[:, ci, K_TMP, :], op=ALU.mult),
                    r=[tg("SV")], w=[tg("SV2")])
                P.add("dve", lambda e, ci=ci: e.tensor_tensor(
                    SV[:, ci, K_BP, :], LNB[:, l, s, :], SV[:, ci, K_TMP, :], op=ALU.mult),
                    r=[tg("SV")], w=[tg("SV3")])
                P.add("dve", lambda e, ci=ci, col=col, nl=nl, ns=ns: e.tensor_tensor(
                    SV[:, ci, K_BP, :], SV[:, ci, K_BP, :], modv(nl, 3 * ns, col), op=ALU.add),
                    r=[tg("SV3")], w=[tg("SV4")])
        tag = cur["tag"]
        P.add("dve", lambda e: e.memset(SV[:, 0, K_TMP, 0:1], 0.0),
              r=[tg("SV"), tg("SV2"), tg("SV3"), tg("SV4")], w=[(tag,)])
        return (SV, tag)

    def svv(sv, tt, kind, c):
        ci = 1 if tt == 4 else 0
        return sv[0][:, ci, kind, c:c + 1]

    def xtok(tt):
        return [("X", c, tt) for c in range(NCH)]

    def xmtok(tt):
        return [("XM", c, tt) for c in range(NCH)]

    def emit_ln_front_a(tt, lns, mixer=False):
        t0, n = TILES[tt]
        ZB, SQ, MEAN, MSQ, VARE, RSTD = lns
        if mixer:
            P.add("act", lambda e: e.activation(out=ZB[:, :, :n], in_=X[:, :, t0:t0 + n], func=AF.Copy),
                  r=xtok(tt), w=[("ZB",)])
        else:
            P.add("dve", lambda e: e.tensor_copy(ZB[:, :, :n], X[:, :, t0:t0 + n]), r=xtok(tt), w=[("ZB",)])
        P.add("act", lambda e: e.activation(out=SQ[:, :, :n], in_=X[:, :, t0:t0 + n], func=AF.Square),
              r=xtok(tt), w=[("SQ",)])

    def emit_ln_front(tt, lns):
        emit_ln_front_a(tt, lns)
        emit_ln_front_b(tt, lns)

    def emit_ln_front_b(tt, lns, passes=True, split=5):
        t0, n = TILES[tt]
        ZB, SQ, MEAN, MSQ, VARE, RSTD = lns

        def st(e, src, bk):
            ins = None
            for c in range(NCH):
                ins = e.matmul(PS[:, bk, :n], lhsT=ONESD[:], rhs=src[:, c, :n], start=(c == 0), stop=(c == NCH - 1))
            return ins
        P.add("pe", lambda e: st(e, ZB, 4), r=[("ZB",), ("ONESD",)], w=[pst(4)])
        P.add("pe", lambda e: st(e, SQ, 5), r=[("SQ",), ("ONESD",)], w=[pst(5)])

        def a1(e):
            e.activation(out=MEAN[:, :n], in_=PS[:, 4, :n], func=AF.Copy)
            return e.activation(out=MSQ[:, :n], in_=PS[:, 4, :n], func=AF.Square)
        P.add("act", a1, w=[pst(4), ("MEAN",), ("MSQ",)])
        P.add("dve", lambda e: e.scalar_tensor_tensor(
            out=VARE[:, :n], in0=PS[:, 5, :n], scalar=EPS, in1=MSQ[:, :n], op0=ALU.add, op1=ALU.subtract),
            r=[("MSQ",)], w=[pst(5), ("VARE",)])

        P.add("act", lambda e: e.activation(out=VARE[:, :n], in_=VARE[:, :n], func=AF.Ln),
              r=[("VARE",)], w=[("SD",)])
        P.add("act", lambda e: e.activation(out=RSTD[:, :n], in_=VARE[:, :n], func=AF.Exp, scale=-0.5),
              r=[("SD",)], w=[("RSTD",), ("VARE",)])

        if not passes:
            return
        emit_ln_front_b2(tt, lns, split)

    def emit_ln_front_b2(tt, lns, split=5):
        t0, n = TILES[tt]
        ZB, SQ, MEAN, MSQ, VARE, RSTD = lns
        for eng, c0, c1 in (("dve", 0, split), ("pool", split, 8)):
            toks = [("X", c, tt) for c in range(c0, c1)]
            P.add(eng, lambda e, c0=c0, c1=c1: e.tensor_tensor(
                X[:, c0:c1, t0:t0 + n], X[:, c0:c1, t0:t0 + n],
                MEAN[:, None, :n].to_broadcast([128, c1 - c0, n]), op=ALU.subtract),
                r=[("MEAN",)] + toks, w=toks)
            P.add(eng, lambda e, c0=c0, c1=c1: e.tensor_tensor(
                X[:, c0:c1, t0:t0 + n], X[:, c0:c1, t0:t0 + n],
                RSTD[:, None, :n].to_broadcast([128, c1 - c0, n]), op=ALU.mult),
                r=[("RSTD",)] + toks, w=toks)

    def emit_ln_apply(tt, sv, has_next):
        t0, n = TILES[tt]
        svr = (sv[1],)
        for c in range(NCH):
            gp, bp, ga, ba = svv(sv, tt, K_GP, c), svv(sv, tt, K_BP, c), svv(sv, tt, K_GA, c), svv(sv, tt, K_BA, c)

            if has_next:
                P.add("act", lambda e, c=c, gp=gp, bp=bp: e.activation(
                    out=XM[:, c, t0:t0 + n], in_=X[:, c, t0:t0 + n], func=AF.Identity, scale=gp, bias=bp),
                    r=[svr, ("X", c, tt)], w=[("XM", c, tt)])
            P.add("pool", lambda e, c=c, ga=ga, ba=ba: e.tensor_scalar(
                X[:, c, t0:t0 + n], X[:, c, t0:t0 + n], ga, ba, op0=ALU.mult, op1=ALU.add),
                r=[svr], w=[("X", c, tt)])

    def carve_lns(cz):
        ZB = cz.bf(NCH * 512, "ZB").rearrange("p (c t) -> p c t", c=NCH)
        SQ = cz.bf(NCH * 512, "SQ").rearrange("p (c t) -> p c t", c=NCH)
        MEAN = cz.f32(512, "MEAN")
        MSQ = cz.f32(512, "MSQ")
        VARE = cz.f32(512, "VARE")
        RSTD = cz.f32(512, "RSTD")
        P.ranges["SD"] = P.ranges["VARE"]
        return (ZB, SQ, MEAN, MSQ, VARE, RSTD)

    def emit_rmw(sv, tt, dc, bk):
        t0, n = TILES[tt]
        gt = svv(sv, tt, K_GT, dc)
        P.add("dve", lambda e: e.scalar_tensor_tensor(
            out=X[:, dc, t0:t0 + n], in0=PS[:, bk, :n], scalar=gt, in1=X[:, dc, t0:t0 + n],
            op0=ALU.mult, op1=ALU.add), r=[(sv[1],)], w=[pst(bk), ("X", dc, tt)])

    def emit_ffn(b, l, s, last):
        fi = 0 if s == 0 else 1
        nt_in = sublayer_tiles(l, s, "in")
        nt_out = sublayer_tiles(l, s, "out")
        nt = min(nt_in, nt_out)
        P.new_phase()
        lns = carve_lns(Carver(A_LOW))
        WIN, WOUT = [None, None], [None, None]
        cz = Carver(A_W)
        WIN[0] = cz.bf(NCH * 1024, "win0").rearrange("p (k n) -> p k n", k=NCH)
        WOUT[0] = cz.bf(4 * 1024, "wout0").rearrange("p (k n) -> p k n", k=4)
        assert cz.off <= A_S
        cz = Carver(A_S)
        WIN[1] = cz.bf(NCH * 1024, "win1").rearrange("p (k n) -> p k n", k=NCH)
        WOUT[1] = cz.bf(4 * 1024, "wout1").rearrange("p (k n) -> p k n", k=4)
        cz = Carver(12288)
        G = [cz.bf(4 * 512, "G%d" % i).rearrange("p (k n) -> p k n", k=4) for i in range(2)]
        SIL = [cz.f32(512, "SIL%d" % i) for i in range(2)]
        assert cz.off <= A_W
        sv = emit_sv(b, l, s, last)

        def load(j):
            slot = j % 2
            nbk = BLK[j]
            P.add("pool", lambda e: e.dma_start(
                out=WIN[slot][:, :, 0:nbk * 256],
                in_=win[l, fi, j].rearrange("p (k n) -> p k n", k=NCH)[:, :, 0:nbk * 256]),
                w=[("win%d" % slot,)], dma="win%d" % slot)
            P.add("pool", lambda e: e.dma_start(
                out=WOUT[slot][:, 0:nbk, :], in_=wout[l, fi, j].rearrange("p (k n) -> p k n", k=4)[:, 0:nbk, :]),
                w=[("wout%d" % slot,)], dma="wout%d" % slot)

        def emit_h(j, tt, gs):
            slot = j % 2
            t0, n = TILES[tt]
            nbk = BLK[j]
            for i in range(nbk):
                for half in range(2):
                    bk = 2 * (i % 2) + half
                    co = half * nbk * 128 + i * 128

                    def mm(e, bk=bk, co=co):
                        ins = None
                        for k in range(NCH):
                            ins = e.matmul(PS[:, bk, :n], lhsT=WIN[slot][:, k, co:co + 128],
                                           rhs=XM[:, k, t0:t0 + n], start=(k == 0), stop=(k == NCH - 1))
                        return ins
                    P.add("pe", mm, r=[("win%d" % slot,)] + xmtok(tt), w=[pst(bk)])
                ip = i % 2
                P.add("act", lambda e, ip=ip: e.activation(out=SIL[ip][:, :n], in_=PS[:, 2 * ip, :n], func=AF.Silu),
                      w=[pst(2 * ip), ("SIL%d" % ip,)])
                P.add("dve", lambda e, i=i, ip=ip: e.tensor_tensor(
                    G[gs][:, i, :n], PS[:, 2 * ip + 1, :n], SIL[ip][:, :n], op=ALU.mult),
                    r=[("SIL%d" % ip,)], w=[pst(2 * ip + 1), ("G%d" % gs, i)])

        def emit_y(j, tt, gs):
            slot = j % 2
            t0, n = TILES[tt]
            for dc in range(NCH):
                bk = (6 + dc % 2) if j >= NFB - 1 else (4 + dc % 4)
                nbk = BLK[j]

                def mm(e, dc=dc, bk=bk):
                    ins = None
                    for i in range(nbk):
                        ins = e.matmul(PS[:, bk, :n], lhsT=WOUT[slot][:, i, dc * 128:(dc + 1) * 128],
                                       rhs=G[gs][:, i, :n], start=(i == 0), stop=(i == nbk - 1))
                    return ins
                P.add("pe", mm, r=[("wout%d" % slot,)] + [("G%d" % gs, i) for i in range(nbk)], w=[pst(bk)])
                emit_rmw(sv, tt, dc, bk)

        load(0)
        yield "ready"
        pending = None
        cnt = 0
        qb, qc = [], []
        for j in range(NFB):
            for tt in range(nt):
                gs = cnt % 2
                cnt += 1
                emit_h(j, tt, gs)
                fb = qb.pop(0) if qb else None
                if fb is not None:
                    emit_ln_front_b(fb, lns, passes=False)
                if pending is not None:
                    emit_y(*pending)
                    if fb is not None:
                        emit_ln_front_b2(fb, lns)
                    if qc:
                        emit_ln_apply(qc.pop(0), sv, not last)
                    if fb is not None:
                        qc.append(fb)
                    if pending[0] == NFB - 1:
                        emit_ln_front_a(pending[1], lns)
                        qb.append(pending[1])
                pending = (j, tt, gs)
                if tt == 0 and j + 1 < NFB:
                    load(j + 1)
        fb = qb.pop(0) if qb else None
        if fb is not None:
            emit_ln_front_b(fb, lns, passes=False)
        emit_y(*pending)
        if fb is not None:
            emit_ln_front_b2(fb, lns)
        if qc:
            emit_ln_apply(qc.pop(0), sv, not last)
        if fb is not None:
            qc.append(fb)
        emit_ln_front_a(pending[1], lns)
        yield "hook"
        emit_ln_front_b(pending[1], lns)
        while qc:
            emit_ln_apply(qc.pop(0), sv, not last)
        emit_ln_apply(pending[1], sv, not last)

    def emit_outproj(WOg, OT, tiles_out, lns, last, final_group, sv):
        prev = None
        for tt in range(tiles_out):
            t0, n = TILES[tt]
            for dc in range(NCH):
                bk = 6 + dc % 2

                def mm(e, dc=dc, bk=bk, t0=t0, n=n):
                    ins = None
                    for i in range(2):
                        ins = e.matmul(PS[:, bk, :n], lhsT=WOg[:, i, dc * 128:(dc + 1) * 128],
                                       rhs=OT[:, i, t0:t0 + n], start=(i == 0), stop=(i == 1))
                    return ins
                P.add("pe", mm, r=[("wo",), ("OT", 0, tt), ("OT", 1, tt)], w=[pst(bk)])
                emit_rmw(sv, tt, dc, bk)
            if final_group:
                if tt >= 1:
                    emit_ln_front_b(tt - 1, lns, split=4)
                if tt >= 2:
                    emit_ln_apply(tt - 2, sv, not last)
                if tt == tiles_out - 1:
                    yield "hook"
                emit_ln_front_a(tt, lns, mixer=True)
        if final_group:
            lt = tiles_out - 1
            emit_ln_front_b(lt, lns, split=4)
            if lt >= 1:
                emit_ln_apply(lt - 1, sv, not last)
            emit_ln_apply(lt, sv, not last)

    def emit_na(b, l, last):
        jn = l // 2
        nt_in = 5
        nt_out = sublayer_tiles(l, 1, "out")
        ctxq = (l == 0)
        P.new_phase()
        lns = carve_lns(Carver(A_LOW))
        cz = Carver(A_LOW)
        QT = cz.bf(2 * T, "QT").rearrange("p (h t) -> p h t", h=2)
        P.ranges["QTZ"] = P.ranges["QT"]
        KT = cz.bf(T, "KT")
        VC = cz.bf(18 * 192, "VC").rearrange("p (c n) -> p c n", c=18)
        P.ranges["VC1"] = P.ranges["VC"]
        PT = [cz.bf(8 * 128, "PT%d" % i) for i in range(4)]
        T1 = [cz.f32(512, "T1%d" % i) for i in range(2)]
        T2 = [cz.f32(512, "T2%d" % i) for i in range(2)]
        REC = [cz.f32(256, "REC%d" % i) for i in range(2)]
        assert cz.off <= A_W
        cz = Carver(A_W)
        WQ = [cz.bf(NCH * 384, "wq%d" % i).rearrange("p (k n) -> p k n", k=NCH) for i in range(2)]
        WOg = cz.bf(2 * 1024, "wo").rearrange("p (k n) -> p k n", k=2)
        TB = cz.bf(2 * TBW, "TB").rearrange("p (h n) -> p h n", h=2)
        assert cz.off <= A_S
        cz = Carver(A_S)
        OT = cz.bf(2 * T, "OT").rearrange("p (k n) -> p k n", k=2)
        RC = cz.f32(TL, "RC")
        RS = cz.f32(TL, "RS")
        sv = emit_sv(b, l, 1, last)
        yield "ready"
        P.add("sp", lambda e: e.dma_start(out=RC, in_=ropec), w=[("RC",)], dma="ropec")
        P.add("sp", lambda e: e.dma_start(out=RS, in_=ropes), w=[("RS",)], dma="ropes")
        P.add("pool", lambda e: e.memset(VC[:, :, 64:128], 1.0), w=[("VC1",)])
        P.add("pool", lambda e: e.memset(QT[64:128, 0, :], 0.0), w=[("QTZ",)])
        P.add("pool", lambda e: e.memset(QT[0:64, 1, :], 0.0), w=[("QTZ",)])

        def rope(tt, bk, dst, scale, which):
            t0, n = TILES[tt]
            a, bq = T1[which], T2[which]

            def d1(e):
                e.scalar_tensor_tensor(out=a[:, :n], in0=PS[:, bk, :n], scalar=scale, in1=RC[:, t0:t0 + n],
                                       op0=ALU.mult, op1=ALU.mult)
                ins = None
                for q in range(4):
                    isl = slice(32 * (q ^ 1), 32 * (q ^ 1) + 32)
                    osl = slice(32 * q, 32 * q + 32)
                    ins = e.scalar_tensor_tensor(out=bq[osl, :n], in0=PS[isl, bk, :n], scalar=scale,
                                                 in1=RS[isl, t0:t0 + n], op0=ALU.mult, op1=ALU.mult)
                return ins
            P.add("dve", d1, r=[("RC",), ("RS",)], w=[pst(bk), ("T1%d" % which,), ("T2%d" % which,)])
            if which == 1:
                P.add("pool", lambda e: e.tensor_tensor(dst[:, t0:t0 + n], a[:, :n], bq[:, :n], op=ALU.add),
                      r=[("T1%d" % which,), ("T2%d" % which,)], w=[("KT", tt)])
            else:
                def qadd(e):
                    e.tensor_tensor(QT[0:64, 0, t0:t0 + n], a[0:64, :n], bq[0:64, :n], op=ALU.add)
                    return e.tensor_tensor(QT[64:128, 1, t0:t0 + n], a[64:128, :n], bq[64:128, :n], op=ALU.add)
                P.add("pool", qadd, r=[("T1%d" % which,), ("T2%d" % which,)], w=[("QT", tt)])

        ucount = [0]

        def emit_st(u):
            kind = u["kind"]
            if kind == "ctx":
                nq, q0 = 256, TL
                tcs, boff = [], None
            else:
                p = kind
                nq, q0 = 128, 128 * p
                if 2 <= p <= 13:
                    tcs, boff = list(range(p - 2, p + 3)), 0
                elif p < 2:
                    tcs, boff = [0, 1, 2, 3], 640 + 512 * p
                else:
                    tcs, boff = [12, 13, 14, 15], 640 + 512 * (p - 12)
            allc = tcs + [16, 17]
            u["allc"], u["nq"], u["q0"] = allc, nq, q0
            w2 = 2 * nq
            per = 1024 // w2
            halves = [list(range(i, min(i + per, len(allc)))) for i in range(0, len(allc), per)]
            u["pts"] = {}
            for hi, idxs in enumerate(halves):
                buf = ucount[0] % 2
                ucount[0] += 1
                ptb = (u["n"] % 2) * 2 + hi
                stv = PS[:, 2 * buf:2 * buf + 2, :].rearrange("p b (s q) -> p (b s) q", q=w2)
                ptv = PT[ptb].rearrange("p (s q) -> p s q", q=w2)

                def mm(e, idxs=idxs, stv=stv):
                    ins = None
                    for s_, i in enumerate(idxs):
                        tc = allc[i]
                        hasb = i < len(tcs)
                        ins = e.matmul(stv[:, s_, :], lhsT=KT[:, tc * 128:(tc + 1) * 128],
                                       rhs=QT[:, :, q0:q0 + nq], start=True, stop=not hasb)
                        if hasb:
                            for hh in range(2):
                                ins = e.matmul(stv[:, s_, hh * nq:(hh + 1) * nq],
                                               lhsT=TB[:, hh, boff + i * 128:boff + (i + 1) * 128],
                                               rhs=IDB[:, :], start=False, stop=(hh == 1))
                    return ins
                banks = [pst(2 * buf), pst(2 * buf + 1)]
                P.add("pe", mm, r=[("QT", t) for t in range(5)] + [("KT", t) for t in range(5)]
                      + [("TB",), ("IDB",), ("QTZ",)], w=banks)
                ns = len(idxs)
                P.add("act", lambda e, ns=ns, stv=stv, ptv=ptv: e.activation(
                    out=ptv[:, 0:ns, :], in_=stv[:, 0:ns, :], func=AF.Exp), w=banks + [("PT%d" % ptb,)])
                for s_, i in enumerate(idxs):
                    u["pts"][i] = (ptb, ptv, s_)

        def emit_pv(u, ci):
            kind = u["kind"]
            allc, nq, q0 = u["allc"], u["nq"], u["q0"]
            tt = 4 if kind == "ctx" else kind // 4
            ptbs = sorted({v[0] for v in u["pts"].values()})
            for hh in range(2):
                hoff = 64 * hh
                osl = slice(64 * hh, 64 * hh + 64)
                rsl = slice(64 * (1 - hh), 64 * (1 - hh) + 64)
                bk = 4 + 2 * (u["n"] % 2) + hh

                def mm(e, hh=hh, hoff=hoff, bk=bk):
                    ins = None
                    for i, tc in enumerate(allc):
                        _, ptv, s_ = u["pts"][i]
                        ins = e.matmul(PS[:, bk, :nq], lhsT=VC[:, tc, hoff:hoff + 128],
                                       rhs=ptv[:, s_, hh * nq:(hh + 1) * nq],
                                       start=(i == 0), stop=(i == len(allc) - 1))
                    return ins
                P.add("pe", mm, r=[("PT%d" % x,) for x in ptbs] + [("VC1",)] + [("VC", t) for t in range(5)],
                      w=[pst(bk)])
                rb = REC[hh]

                P.add("dve", lambda e, osl=osl, rsl=rsl, bk=bk, rb=rb: e.reciprocal(rb[osl, :nq], PS[rsl, bk, :nq]),
                      w=[pst(bk), ("REC%d" % hh,)])
                P.add("dve", lambda e, osl=osl, bk=bk, rb=rb: e.tensor_tensor(
                    OT[osl, ci, q0:q0 + nq], PS[osl, bk, :nq], rb[osl, :nq], op=ALU.mult),
                    r=[("REC%d" % hh,)], w=[pst(bk), ("OT", ci, tt)])

        def load_wq(c):
            P.add("pool", lambda e: e.dma_start(
                out=WQ[c % 2], in_=wqkv[jn, c].rearrange("p (k n) -> p k n", k=NCH)),
                w=[("wq%d" % (c % 2),)], dma="wq%d" % (c % 2))
        load_wq(0)
        for c in range(8):
            slot = c % 2
            ci = c % 2
            if c + 1 < 8:
                load_wq(c + 1)
            P.add("pool", lambda e, c=c: e.dma_start(
                out=TB, in_=tbias[jn, c].rearrange("p (h n) -> p h n", h=2)), w=[("TB",)], dma="tb")
            if ci == 0:
                P.add("pool", lambda e, c=c: e.dma_start(
                    out=WOg, in_=wo[jn, c // 2].rearrange("p (k n) -> p k n", k=2)), w=[("wo",)], dma="wo")
            for tt in range(nt_in):
                t0, n = TILES[tt]
                for which in range(2):
                    def mm(e, which=which, t0=t0, n=n, slot=slot):
                        ins = None
                        for k in range(NCH):
                            ins = e.matmul(PS[:, which, :n], lhsT=WQ[slot][:, k, which * 128:(which + 1) * 128],
                                           rhs=XM[:, k, t0:t0 + n], start=(k == 0), stop=(k == NCH - 1))
                        return ins
                    P.add("pe", mm, r=[("wq%d" % slot,)] + xmtok(tt), w=[pst(which)])
                    dst = QT if which == 0 else KT
                    scale = 0.125 if which == 0 else 1.0
                    if tt < 4:
                        rope(tt, which, dst, scale, which)
                    elif which == 1:
                        P.add("act", lambda e, which=which, dst=dst, scale=scale, t0=t0, n=n: e.activation(
                            out=dst[:, t0:t0 + n], in_=PS[:, which, :n], func=AF.Copy, scale=scale),
                            w=[pst(which), ("KT", tt)])
                    else:
                        def qcp(e, t0=t0, n=n, scale=scale):
                            e.activation(out=QT[0:64, 0, t0:t0 + n], in_=PS[0:64, 0, :n], func=AF.Copy, scale=scale)
                            return e.activation(out=QT[64:128, 1, t0:t0 + n], in_=PS[64:128, 0, :n],
                                                func=AF.Copy, scale=scale)
                        P.add("act", qcp, w=[pst(0), ("QT", tt)])
                ntc = n // 128

                def mmv(e, t0=t0, ntc=ntc, slot=slot):
                    ins = None
                    for tcl in range(ntc):
                        for k in range(NCH):
                            ins = e.matmul(PS[:, 2, tcl * 128:(tcl + 1) * 128],
                                           lhsT=XM[:, k, t0 + tcl * 128:t0 + (tcl + 1) * 128],
                                           rhs=WQ[slot][:, k, 256:384], start=(k == 0), stop=(k == NCH - 1))
                    return ins
                P.add("pe", mmv, r=[("wq%d" % slot,)] + xmtok(tt), w=[pst(2)])
                tc0 = t0 // 128

                def ev(e, tc0=tc0, ntc=ntc):
                    src = PS[:, 2, 0:ntc * 128].rearrange("p (c n) -> p c n", n=128)
                    e.activation(out=VC[:, tc0:tc0 + ntc, 0:64], in_=src[:, :, 0:64], func=AF.Copy)
                    return e.activation(out=VC[:, tc0:tc0 + ntc, 128:192], in_=src[:, :, 64:128], func=AF.Copy)
                P.add("act", ev, w=[pst(2), ("VC", tt)])
            if debug and c == 0 and b == 0:
                P.add("sp", lambda e: e.dma_start(out=dbg_qt, in_=QT[:, 0, :]), r=[("QT", t) for t in range(5)], dma="dbg1")
                P.add("sp", lambda e: e.dma_start(out=dbg_kt, in_=KT), r=[("KT", t) for t in range(5)], dma="dbg2")
                P.add("sp", lambda e: e.dma_start(out=dbg_vc, in_=VC.rearrange("p c n -> p (c n)")),
                      r=[("VC", t) for t in range(5)] + [("VC1",)], dma="dbg3")
                P.add("sp", lambda e: e.dma_start(out=dbg_xm, in_=XM[:]),
                      r=[t for tt in range(5) for t in xmtok(tt)], dma="dbg4")
            units = [{"kind": p} for p in range(16)]
            if ctxq:
                units.append({"kind": "ctx"})
            prev = None
            for un, u in enumerate(units):
                u["n"] = un
                emit_st(u)
                if prev is not None:
                    emit_pv(prev, ci)
                prev = u
            emit_pv(prev, ci)
            if debug and c == 0 and b == 0:
                P.add("sp", lambda e: e.dma_start(out=dbg_ot, in_=OT[:, 0, :]),
                      r=[("OT", 0, t) for t in range(5)], dma="dbg5")
            if ci == 1:
                yield from emit_outproj(WOg, OT, nt_out, lns, last, c == 7, sv)

    def emit_conv(b, l, last):
        jn = l // 2
        nt = sublayer_tiles(l, 1, "out")
        P.new_phase()
        lns = carve_lns(Carver(A_LOW))
        cz = Carver(A_W)
        WC = [cz.bf(NCH * 384, "wc%d" % i).rearrange("p (k n) -> p k n", k=NCH) for i in range(2)]
        WOg = cz.bf(2 * 1024, "wo").rearrange("p (k n) -> p k n", k=2)
        cz = Carver(A_S)
        BB = cz.bf(T, "BB")
        OT = cz.bf(2 * T, "OT").rearrange("p (k n) -> p k n", k=2)
        CG = [cz.f32(512, "CG%d" % i) for i in range(2)]
        ACC = [cz.f32(512, "ACC%d" % i) for i in range(2)]
        cz = Carver(12288)
        U = cz.f32(T + 8, "U")
        assert cz.off <= A_W
        P.ranges["Upad"] = P.ranges["U"]
        sv = emit_sv(b, l, 1, last)
        P.add("pool", lambda e: e.memset(U[:, 0:1], 0.0), w=[("Upad",)])
        P.add("pool", lambda e: e.memset(U[:, 2049:2051], 0.0), w=[("Upad",)])
        P.add("pool", lambda e: e.memset(U[:, 2307:2308], 0.0), w=[("Upad",)])

        def upos(t):
            return t + 1 if t < TL else t + 3

        def load_wc(c):
            P.add("pool", lambda e: e.dma_start(
                out=WC[c % 2], in_=wcin[jn, c].rearrange("p (k n) -> p k n", k=NCH)),
                w=[("wc%d" % (c % 2),)], dma="wq%d" % (c % 2))
        load_wc(0)
        yield "ready"
        for c in range(8):
            slot = c % 2
            ci = c % 2
            if c + 1 < 8:
                load_wc(c + 1)
            if ci == 0:
                P.add("pool", lambda e, c=c: e.dma_start(
                    out=WOg, in_=wco[jn, c // 2].rearrange("p (k n) -> p k n", k=2)), w=[("wo",)], dma="wo")
            for tt in range(nt):
                t0, n = TILES[tt]
                for which in range(3):
                    def mm(e, which=which, t0=t0, n=n, slot=slot):
                        ins = None
                        for k in range(NCH):
                            ins = e.matmul(PS[:, which, :n], lhsT=WC[slot][:, k, which * 128:(which + 1) * 128],
                                           rhs=XM[:, k, t0:t0 + n], start=(k == 0), stop=(k == NCH - 1))
                        return ins
                    P.add("pe", mm, r=[("wc%d" % slot,)] + xmtok(tt), w=[pst(which)])
                P.add("act", lambda e, t0=t0, n=n: e.activation(out=BB[:, t0:t0 + n], in_=PS[:, 0, :n], func=AF.Copy),
                      w=[pst(0), ("BB", tt)])
                cg = CG[tt % 2]
                P.add("act", lambda e, n=n, cg=cg: e.activation(out=cg[:, :n], in_=PS[:, 1, :n], func=AF.Copy),
                      w=[pst(1), ("CG%d" % (tt % 2),)])
                u0 = upos(t0)
                P.add("dve", lambda e, n=n, cg=cg, u0=u0: e.tensor_tensor(
                    U[:, u0:u0 + n], PS[:, 2, :n], cg[:, :n], op=ALU.mult),
                    r=[("CG%d" % (tt % 2),), ("Upad",)], w=[pst(2), ("U", tt)])
            for tt in range(nt):
                t0, n = TILES[tt]
                u0 = upos(t0)
                acc = ACC[tt % 2]
                if tt == 4:
                    rd = [("U", 4)]
                else:
                    rd = [("U", x) for x in range(max(tt - 1, 0), min(tt + 1, 3) + 1)]

                atok = ("ACC%d" % (tt % 2),)
                P.add("dve", lambda e, n=n, u0=u0, acc=acc, c=c: e.tensor_scalar(
                    acc[:, :n], U[:, u0 - 1:u0 - 1 + n], CONVW[:, jn, 0, c:c + 1], None, op0=ALU.mult),
                    r=rd + [("Upad",)], w=[atok])
                for tap in (1, 2):
                    P.add("dve", lambda e, n=n, u0=u0, acc=acc, c=c, tap=tap: e.scalar_tensor_tensor(
                        out=acc[:, :n], in0=U[:, u0 - 1 + tap:u0 - 1 + tap + n], scalar=CONVW[:, jn, tap, c:c + 1],
                        in1=acc[:, :n], op0=ALU.mult, op1=ALU.add), r=rd + [("Upad",), atok], w=[atok])
                P.add("pool", lambda e, t0=t0, n=n, acc=acc, ci=ci: e.tensor_tensor(
                    OT[:, ci, t0:t0 + n], acc[:, :n], BB[:, t0:t0 + n], op=ALU.mult),
                    r=[("ACC%d" % (tt % 2),), ("BB", tt)], w=[("OT", ci, tt)])
            if ci == 1:
                yield from emit_outproj(WOg, OT, nt, lns, last, c == 7, sv)

    subs = [(l, s) for l in range(DEPTH) for s in range(3)]
    if stop is not None:
        subs = subs[:subs.index(tuple(stop)) + 1]

    for b in range(nb):
        for tt in range(5):
            t0, n = TILES[tt]
            P.add("sp", lambda e, b=b, t0=t0, n=n: e.dma_start(out=X[:, :, t0:t0 + n], in_=xin[b][:, :, t0:t0 + n]),
                  w=xtok(tt), dma="xin%d" % tt)
        for ci, col in enumerate((b, 2)):
            P.add("dve", lambda e, ci=ci, col=col: e.tensor_scalar(
                SVI[:, ci, :], modv(0, 1, col), 1.0, None, op0=ALU.add), w=[("SVI",)])
        for tt in range(5):
            t0, n = TILES[tt]
            ci = 1 if tt == 4 else 0
            col = 2 if tt == 4 else b
            for c in range(NCH):
                P.add("pool", lambda e, c=c, t0=t0, n=n, ci=ci, col=col: e.tensor_scalar(
                    XM[:, c, t0:t0 + n], X[:, c, t0:t0 + n], SVI[:, ci, c:c + 1], MOD[:, 0, c, col:col + 1],
                    op0=ALU.mult, op1=ALU.add), r=[("X", c, tt), ("SVI",)], w=[("XM", c, tt)])
                P.add("act", lambda e, c=c, t0=t0, n=n: e.activation(
                    out=X[:, c, t0:t0 + n], in_=X[:, c, t0:t0 + n], func=AF.Copy, scale=ALPHA),
                    w=[("X", c, tt)])
        gens = []
        for (l, s) in subs:
            last = (l, s) == subs[-1]
            if s != 1:
                gens.append(("ffn", emit_ffn(b, l, s, last)))
            elif l % 2 == 0:
                gens.append(("na", emit_na(b, l, last)))
            else:
                gens.append(("conv", emit_conv(b, l, last)))
        assert next(gens[0][1]) == "ready"
        for gi, (kind, g) in enumerate(gens):
            assert next(g) == "hook"
            nxt = gens[gi + 1] if gi + 1 < len(gens) else None
            hoist = nxt is not None and nxt[0] != "na"
            if hoist:
                assert next(nxt[1]) == "ready"
            for _ in g:
                raise AssertionError("unexpected extra yield")
            if nxt is not None and not hoist:
                assert next(nxt[1]) == "ready"
        for tt in range(4):
            t0, n = TILES[tt]
            P.add("sp", lambda e, b=b, t0=t0, n=n: e.dma_start(out=out[b][:, :, t0:t0 + n], in_=X[:, :, t0:t0 + n]),
                  r=xtok(tt), dma="out%d_%d" % (b, tt))
        if debug and b == 0:
            P.add("sp", lambda e: e.dma_start(out=dbg_x, in_=X[:]), dma="outdbg1")
            P.add("sp", lambda e: e.dma_start(out=dbg_mod, in_=MOD[:]), dma="outdbg2")

    dma_keys = sorted({op.dma for op in P.ops if op.dma is not None})
    sems = {}
    dsems = {}
    for e in Prog.ENGS:
        cm = nc.semaphore("s_" + e)
        sems[e] = cm.__enter__()
        ctxs.append(cm)
    for k in dma_keys:
        cm = nc.semaphore("d_" + k)
        dsems[k] = cm.__enter__()
        ctxs.append(cm)
    with nc.Block() as block:
        P.emit(nc, block, sems, dsems)
    return nc, P


def _fm(a2d):
    n = a2d.shape[1]
    return np.ascontiguousarray(a2d.reshape(NCH, 128, n).transpose(1, 0, 2))


def _bias_tables(rpb):
    out = np.full((16, 128, TBW), NEG, np.float32)
    qc = np.arange(64)
    wc0 = np.clip(qc - 8, 0, 48)
    kc = np.arange(64)
    colok = (kc[None, :] >= wc0[:, None]) & (kc[None, :] < wc0[:, None] + 16)
    dx = np.clip(kc[None, :] - qc[:, None] + 15, 0, 30)

    def fill(dst, r0, kbase, nrows, rs_of):
        for qi in range(2):
            qr = r0 + qi
            rs = rs_of(qr)
            for e in range(nrows):
                kr = kbase + e
                if rs <= kr < rs + 8:
                    dy = kr - qr + 7
                    vals = rpb[:, dy, :][:, dx]
                    blk = np.where(colok[None], vals, NEG)
                    dst[:, qi * 64:(qi + 1) * 64, e * 64:(e + 1) * 64] = blk

    rs_of = lambda r: int(np.clip(r - 4, 0, 24))
    fill(out[:, :, 0:640], 12, 8, 10, rs_of)
    fill(out[:, :, 640:1152], 0, 0, 8, rs_of)
    fill(out[:, :, 1152:1664], 2, 0, 8, rs_of)
    fill(out[:, :, 1664:2176], 28, 24, 8, rs_of)
    fill(out[:, :, 2176:2688], 30, 24, 8, rs_of)
    return np.ascontiguousarray(out.reshape(8, 2, 128, TBW).transpose(0, 2, 1, 3)).reshape(8, 128, 2 * TBW)


def _rope_tables():
    t = np.arange(TL)
    nf = 16
    inv = 10000.0 ** (-np.arange(nf) / nf)
    ang_r = (t // GRID_W)[:, None] * inv[None]
    ang_c = (t % GRID_W)[:, None] * inv[None]
    cr, sr = np.cos(ang_r).astype(np.float32), np.sin(ang_r).astype(np.float32)
    cc, sc = np.cos(ang_c).astype(np.float32), np.sin(ang_c).astype(np.float32)
    C = np.zeros((128, TL), np.float32)
    S = np.zeros((128, TL), np.float32)
    for p in range(128):
        n = p % 64
        m = n % 32
        cs, sn = (cr[:, m], sr[:, m]) if m < 16 else (cc[:, m - 16], sc[:, m - 16])
        C[p] = cs
        S[p] = sn if n < 32 else -sn
    return C, S


_PERM = np.concatenate([np.arange(0, 16), np.arange(32, 48), np.arange(16, 32), np.arange(48, 64)])


def prepare_inputs(x, c, ctx, c_ctx, ada_w, ada_b, ln_g, ln_b, ffn_w_in, ffn_w_out,
                   na_w_qkv, na_w_out, na_rpb, sc_w_in, sc_conv, sc_w_out):
    f = lambda a: np.asarray(a, np.float32)
    x, c, ctx, c_ctx = f(x), f(c), f(ctx), f(c_ctx)
    ada_w, ada_b, ln_g, ln_b = f(ada_w), f(ada_b), f(ln_g), f(ln_b)
    ffn_w_in, ffn_w_out = f(ffn_w_in), f(ffn_w_out)
    na_w_qkv, na_w_out, na_rpb = f(na_w_qkv), f(na_w_out), f(na_rpb)
    sc_w_in, sc_conv, sc_w_out = f(sc_w_in), f(sc_conv), f(sc_w_out)
    sh = {}
    a = ada_w.reshape(DEPTH, NCH, 128, 18, 512).transpose(0, 3, 2, 1, 4)
    sh["adaw"] = np.ascontiguousarray(a).reshape(DEPTH, 18, 128, NCH * 512)
    sh["adab"] = np.ascontiguousarray(ada_b.reshape(DEPTH, 72, 128).transpose(2, 0, 1))
    sh["lng"] = np.ascontiguousarray(ln_g.reshape(DEPTH, 3, NCH, 128).transpose(3, 0, 1, 2))
    sh["lnb"] = np.ascontiguousarray(ln_b.reshape(DEPTH, 3, NCH, 128).transpose(3, 0, 1, 2))
    winb = np.zeros((DEPTH, 2, NFB, 128, NCH, 1024), np.float32)
    woutb = np.zeros((DEPTH, 2, NFB, 128, 4, 1024), np.float32)
    wi5 = ffn_w_in.reshape(DEPTH, 2, NCH, 128, 2 * FF)
    wo5 = ffn_w_out.reshape(DEPTH, 2, FF // 128, 128, D)
    for j in range(NFB):
        nbk, o = BLK[j], BOFF[j] * 128
        a = wi5[..., o:o + nbk * 128]
        u = wi5[..., FF + o:FF + o + nbk * 128]
        blk = np.concatenate([a, u], axis=-1)
        winb[:, :, j, :, :, 0:2 * nbk * 128] = blk.transpose(0, 1, 3, 2, 4)
        woutb[:, :, j, :, 0:nbk, :] = wo5[:, :, BOFF[j]:BOFF[j] + nbk].transpose(0, 1, 3, 2, 4)
    sh["win"] = winb.reshape(DEPTH, 2, NFB, 128, NCH * 1024)
    sh["wout"] = woutb.reshape(DEPTH, 2, NFB, 128, 4 * 1024)
    cols = []
    for cch in range(8):
        qcols = np.concatenate([(2 * cch) * 64 + _PERM, (2 * cch + 1) * 64 + _PERM])
        cols.append(np.concatenate([qcols, 1024 + qcols, 2048 + cch * 128 + np.arange(128)]))
    cols = np.stack(cols)
    wq = na_w_qkv[:, :, cols]
    wq = wq.reshape(2, NCH, 128, 8, 384).transpose(0, 3, 2, 1, 4)
    sh["wqkv"] = np.ascontiguousarray(wq).reshape(2, 8, 128, NCH * 384)
    w2 = na_w_out.reshape(2, 4, 2, 128, D).transpose(0, 1, 3, 2, 4)
    sh["wo"] = np.ascontiguousarray(w2).reshape(2, 4, 128, 2 * 1024)
    sh["tbias"] = np.stack([_bias_tables(na_rpb[j]) for j in range(2)])
    ccols = np.stack([np.concatenate([cch * 128 + np.arange(128), 1024 + cch * 128 + np.arange(128),
                                      2048 + cch * 128 + np.arange(128)]) for cch in range(8)])
    wc = sc_w_in[:, :, ccols].reshape(2, NCH, 128, 8, 384).transpose(0, 3, 2, 1, 4)
    sh["wcin"] = np.ascontiguousarray(wc).reshape(2, 8, 128, NCH * 384)
    w3 = sc_w_out.reshape(2, 4, 2, 128, D).transpose(0, 1, 3, 2, 4)
    sh["wco"] = np.ascontiguousarray(w3).reshape(2, 4, 128, 2 * 1024)
    sh["convw"] = np.ascontiguousarray(sc_conv.reshape(2, 3, NCH, 128).transpose(3, 0, 1, 2))
    C, S = _rope_tables()
    sh["ropec"], sh["ropes"] = C, S
    sh["ident"] = np.eye(128, dtype=np.float32)
    in_maps = []
    for i in range(8):
        m = dict(sh)
        xs = []
        for bb in range(2):
            full = np.concatenate([x[2 * i + bb], ctx[2 * i + bb]], axis=0)
            xs.append(_fm(np.ascontiguousarray(full.T)))
        m["xin"] = np.stack(xs)
        cvv = np.stack([c[2 * i], c[2 * i + 1], c_ctx], axis=1)
        m["cv"] = _fm(cvv)
        in_maps.append(m)
    return in_maps


def assemble_output(results):
    outs = []
    for i in range(8):
        o = np.asarray(results[i]["out"])
        for bb in range(2):
            outs.append(o[bb].transpose(2, 1, 0).reshape(TL, D))
    return np.ascontiguousarray(np.stack(outs)).astype(np.float32)


_CACHE = {}


def kernel(x, c, ctx, c_ctx, ada_w, ada_b, ln_g, ln_b, ffn_w_in, ffn_w_out,
           na_w_qkv, na_w_out, na_rpb, sc_w_in, sc_conv, sc_w_out):
    in_maps = prepare_inputs(x, c, ctx, c_ctx, ada_w, ada_b, ln_g, ln_b, ffn_w_in, ffn_w_out,
                             na_w_qkv, na_w_out, na_rpb, sc_w_in, sc_conv, sc_w_out)
    if "nc" not in _CACHE:
        _CACHE["nc"] = build_program()[0]
    res = run_bass_kernel_spmd(_CACHE["nc"], in_maps, core_ids=list(range(8)))
    return assemble_output(res.results)
```

```python
import numpy as np
import concourse.bass as bass
import concourse.mybir as mybir
from concourse.bass_utils import run_bass_kernel_spmd

F32 = mybir.dt.float32
BF16 = mybir.dt.bfloat16
AF = mybir.ActivationFunctionType
ALU = mybir.AluOpType

D = 1024
NCH = 8
TL = 2048
TCX = 256
T = TL + TCX
FF = 2816
BLK = [2, 4, 4, 4, 4, 4]
BOFF = [0, 2, 6, 10, 14, 18]
NFB = len(BLK)
DEPTH = 4
ALPHA = float((2.0 * DEPTH) ** 0.25)
EPS = 1e-6
NEG = -30000.0
GRID_W = 64
TILES = [(0, 512), (512, 512), (1024, 512), (1536, 512), (2048, 256)]
TBW = 640 + 4 * 512
ARENA = 46592


class Op:
    __slots__ = ("eng", "fn", "deps", "dma", "idx", "waits", "sig", "sigval")


class Prog:
    ENGS = ["pe", "act", "dve", "pool", "sp"]

    def __init__(self):
        self.ops = []
        self.lastw = {}
        self.readers = {}
        self.pending_bar = {}
        self.last_of = {}
        self.dmas_since_bar = []
        self.ranges = {}
        self.stale = []
        self.stale_done = set()
        self.phase_no = 0

    def add(self, eng, fn, r=(), w=(), dma=None):
        op = Op()
        op.eng, op.fn, op.dma, op.idx = eng, fn, dma, len(self.ops)
        deps = {}
        for t in r:
            x = self.lastw.get(t)
            if x is not None:
                deps[x] = "raw"
        for t in w:
            x = self.lastw.get(t)
            if x is not None and x not in deps:
                deps[x] = "waw"
            for y in self.readers.get(t, ()):
                if y not in deps:
                    deps[y] = "war"
        if eng in self.pending_bar:
            for x in self.pending_bar.pop(eng):
                deps[x] = "bar" if deps.get(x) != "raw" else "raw"
        if self.stale:
            for t in list(r) + list(w):
                rg = self.ranges.get(t[0])
                if rg is None or (eng, t[0]) in self.stale_done:
                    continue
                self.stale_done.add((eng, t[0]))
                for (_, a, b_, opset) in self.stale:
                    if a < rg[1] and rg[0] < b_:
                        for x in opset:
                            if x not in deps:
                                deps[x] = "war"
        op.deps = deps
        for t in r:
            self.readers.setdefault(t, []).append(op.idx)
        for t in w:
            self.lastw[t] = op.idx
            self.readers[t] = []
        self.ops.append(op)
        if dma is None:
            self.last_of[eng] = op.idx
        else:
            self.dmas_since_bar.append(op.idx)
        return op.idx

    def new_phase(self):
        self.phase_no += 1
        byname = {}
        for t in list(self.lastw.keys()):
            if t[0] in self.ranges:
                byname.setdefault(t[0], set()).add(self.lastw.pop(t))
        for t in list(self.readers.keys()):
            if t[0] in self.ranges:
                byname.setdefault(t[0], set()).update(self.readers.pop(t))
        for nm, opset in byname.items():
            a, b_ = self.ranges[nm]
            self.stale.append((self.phase_no, a, b_, opset))
        self.stale = [x for x in self.stale if x[0] >= self.phase_no - 2]
        self.ranges = {}
        self.stale_done = set()

    def barrier(self):
        deps = set(self.last_of.values()) | set(self.dmas_since_bar)
        self.pending_bar = {e: set(deps) for e in self.ENGS}
        self.dmas_since_bar = []
        self.lastw = {}
        self.readers = {}

    def emit(self, nc, block, sems, dma_sems):
        ops = self.ops
        need = set()
        for op in ops:
            waits = []
            for d, kind in op.deps.items():
                dop = ops[d]
                if dop.dma is not None:
                    waits.append(d)
                elif dop.eng == op.eng:
                    if op.dma is not None:
                        waits.append(d)
                    elif op.eng == "pe":
                        continue
                    elif kind == "raw":
                        waits.append(d)
                else:
                    waits.append(d)
            op.waits = waits
            for d in waits:
                if ops[d].dma is None:
                    need.add(d)
        cnt = {e: 0 for e in self.ENGS}
        dcnt = {}
        for op in ops:
            if op.dma is not None:
                dcnt[op.dma] = dcnt.get(op.dma, 0) + 16
                op.sigval = dcnt[op.dma]
                op.sig = True
            elif op.idx in need:
                cnt[op.eng] += 1
                op.sigval = cnt[op.eng]
                op.sig = True
            else:
                op.sig = False
                op.sigval = 0
        self.final_counts = (cnt, dcnt)

        def run_engine(engname, eng):
            waited = {}
            for op in ops:
                if op.eng != engname:
                    continue
                for d in op.waits:
                    dop = ops[d]
                    key = ("d", dop.dma) if dop.dma is not None else ("e", dop.eng)
                    if waited.get(key, 0) < dop.sigval:
                        s = dma_sems[dop.dma] if dop.dma is not None else sems[dop.eng]
                        eng.wait_ge(s, dop.sigval)
                        waited[key] = dop.sigval
                ins = op.fn(eng)
                if op.sig:
                    if op.dma is not None:
                        ins.then_inc(dma_sems[op.dma], 16)
                    else:
                        ins.then_inc(sems[op.eng], 1)
            for e in self.ENGS:
                if cnt[e] > 0 and waited.get(("e", e), 0) < cnt[e] and e != engname:
                    pass
            if engname == "sp":
                for k, v in dcnt.items():
                    if k.startswith("out"):
                        eng.wait_ge(dma_sems[k], v)

        @block.tensor
        def _(e):
            run_engine("pe", e)

        @block.scalar
        def _(e):
            run_engine("act", e)

        @block.vector
        def _(e):
            run_engine("dve", e)

        @block.gpsimd
        def _(e):
            run_engine("pool", e)

        @block.sync
        def _(e):
            run_engine("sp", e)


def sublayer_tiles(l, s, what):
    if l <= 1:
        return 5
    if l == 2:
        if s == 0:
            return 5
        if s == 1:
            return 5 if what == "in" else 4
        return 4
    return 4


def build_program(stop=None, nb=2, debug=False):
    nc = bass.Bass("TRN2", target_bir_lowering=False)

    def dram(name, shape, kind="ExternalInput", dt=F32):
        return nc.dram_tensor(name, list(shape), dt, kind=kind).ap()

    xin = dram("xin", [2, 128, NCH, T])
    cv = dram("cv", [128, NCH, 3])
    adaw = dram("adaw", [DEPTH, 18, 128, NCH * 512])
    adab = dram("adab", [128, DEPTH, 72])
    lng = dram("lng", [128, DEPTH, 3, NCH])
    lnb = dram("lnb", [128, DEPTH, 3, NCH])
    win = dram("win", [DEPTH, 2, NFB, 128, NCH * 1024])
    wout = dram("wout", [DEPTH, 2, NFB, 128, 4 * 1024])
    wqkv = dram("wqkv", [2, 8, 128, NCH * 384])
    wo = dram("wo", [2, 4, 128, 2 * 1024])
    tbias = dram("tbias", [2, 8, 128, 2 * TBW])
    wcin = dram("wcin", [2, 8, 128, NCH * 384])
    wco = dram("wco", [2, 4, 128, 2 * 1024])
    convw = dram("convw", [128, 2, 3, NCH])
    ropec = dram("ropec", [128, TL])
    ropes = dram("ropes", [128, TL])
    ident = dram("ident", [128, 128])
    out = dram("out", [2, 128, NCH, TL], kind="ExternalOutput")
    if debug:
        dbg_qt = dram("dbg_qt", [128, T], kind="ExternalOutput", dt=BF16)
        dbg_kt = dram("dbg_kt", [128, T], kind="ExternalOutput", dt=BF16)
        dbg_vc = dram("dbg_vc", [128, 18 * 192], kind="ExternalOutput", dt=BF16)
        dbg_ot = dram("dbg_ot", [128, T], kind="ExternalOutput", dt=BF16)
        dbg_xm = dram("dbg_xm", [128, NCH, T], kind="ExternalOutput", dt=BF16)
        dbg_x = dram("dbg_x", [128, NCH, T], kind="ExternalOutput")
        dbg_mod = dram("dbg_mod", [128, DEPTH, 72, 3], kind="ExternalOutput")

    P = Prog()
    ctxs = []

    def sb(name, shape, dt):
        cm = nc.sbuf_tensor(name, list(shape), dt)
        t = cm.__enter__()
        ctxs.append(cm)
        return t

    X = sb("X", [128, NCH, T], F32)
    XM = sb("XM", [128, NCH, T], BF16)
    AR = sb("AR", [128, ARENA], BF16)
    MOD = sb("MOD", [128, DEPTH, 72, 3], F32)
    ADAB = sb("ADAB", [128, DEPTH, 72], F32)
    LNG = sb("LNG", [128, DEPTH, 3, NCH], F32)
    LNB = sb("LNB", [128, DEPTH, 3, NCH], F32)
    CONVW = sb("CONVW", [128, 2, 3, NCH], F32)
    IDB = sb("IDB", [128, 128], BF16)
    IDF = sb("IDF", [128, 128], F32)
    ONESD = sb("ONESD", [128, 128], BF16)
    MHALF = sb("MHALF", [128, 1], F32)
    SVS = [sb("SV%d" % i, [128, 2, 6, NCH], F32) for i in range(2)]
    SVI = sb("SVI", [128, 2, NCH], F32)
    cur = {"k": 0}
    SC = sb("SC", [128, NCH, 3], F32)
    CVT = sb("CVT", [128, NCH, 3], F32)
    cm = nc.psum_tensor("PS", [128, 8, 512], F32)
    PS = cm.__enter__()
    ctxs.append(cm)

    K_GT, K_GA, K_BA, K_GP, K_BP, K_TMP = range(6)

    class Carver:
        def __init__(self, off=0):
            self.off = off

        def bf(self, n, name=None):
            a = AR[:, self.off:self.off + n]
            if name is not None:
                P.ranges[name] = (self.off, self.off + n)
            self.off += n
            assert self.off <= ARENA, self.off
            return a

        def f32(self, n, name=None):
            if self.off % 2:
                self.off += 1
            a = AR[:, self.off:self.off + 2 * n].bitcast(F32)
            if name is not None:
                P.ranges[name] = (self.off, self.off + 2 * n)
            self.off += 2 * n
            assert self.off <= ARENA, self.off
            return a

    A_LOW = 0
    A_W = 19584
    A_S = 33152

    def bank(i):
        return PS[:, i, :]

    def pst(i):
        return ("ps", i)

    P.add("sp", lambda e: e.dma_start(out=CVT[:], in_=cv), w=[("CVT",)], dma="misc_cvt")
    P.add("sp", lambda e: e.dma_start(out=ADAB[:], in_=adab), w=[("ADAB",)], dma="misc_adab")
    P.add("sp", lambda e: e.dma_start(out=LNG[:], in_=lng), w=[("LNG",)], dma="misc_lng")
    P.add("sp", lambda e: e.dma_start(out=LNB[:], in_=lnb), w=[("LNB",)], dma="misc_lnb")
    P.add("sp", lambda e: e.dma_start(out=CONVW[:], in_=convw), w=[("CONVW",)], dma="misc_convw")
    P.add("pool", lambda e: e.dma_start(out=IDB[:], in_=ident), w=[("IDB",)], dma="misc2")
    P.add("sp", lambda e: e.dma_start(out=IDF[:], in_=ident), w=[("IDF",)], dma="misc_idf")
    P.add("dve", lambda e: e.memset(ONESD[:], 1.0 / 1024.0), w=[("ONESD",)])
    P.add("dve", lambda e: e.memset(MHALF[:], -0.5), w=[("MHALF",)])
    P.add("act", lambda e: e.activation(out=SC[:], in_=CVT[:], func=AF.Silu), r=[("CVT",)], w=[("SC",)])

    cz = Carver()
    AW = [cz.f32(NCH * 512, "aw%d" % i).rearrange("p (k n) -> p k n", k=NCH) for i in range(2)]
    n_ada = 0
    MT = [cz.f32(512, "mt%d" % i) for i in range(2)]
    layers_needed = DEPTH if stop is None else stop[0] + 2
    for l in range(min(DEPTH, layers_needed)):
        for jb in range(18):
            slot = n_ada % 2
            P.add("sp", lambda e, l=l, jb=jb, slot=slot: e.dma_start(
                out=AW[slot], in_=adaw[l, jb].rearrange("p (k n) -> p k n", k=NCH)),
                w=[("aw%d" % slot,)], dma="aw%d" % slot)
            bk = (2 * n_ada) % 8
            bk2 = bk + 1

            def mm(e, slot=slot, bk=bk):
                ins = None
                for k in range(NCH):
                    ins = e.matmul(PS[0:3, bk, :], lhsT=SC[:, k, :], rhs=AW[slot][:, k, :],
                                   start=(k == 0), stop=(k == NCH - 1))
                return ins
            P.add("pe", mm, r=[("aw%d" % slot,), ("SC",)], w=[pst(bk)])
            P.add("act", lambda e, slot=slot, bk=bk: e.activation(out=MT[slot][0:3, :], in_=PS[0:3, bk, :], func=AF.Copy),
                  w=[pst(bk), ("mt%d" % slot,)])

            def tr(e, slot=slot, bk2=bk2):
                ins = None
                for fcl in range(4):
                    ins = e.transpose(PS[:, bk2, fcl * 3:(fcl + 1) * 3], MT[slot][0:3, fcl * 128:(fcl + 1) * 128],
                                      IDF[0:3, 0:3])
                return ins
            P.add("pe", tr, r=[("mt%d" % slot,), ("IDF",)], w=[pst(bk2)])
            P.add("dve", lambda e, l=l, jb=jb, bk2=bk2: e.tensor_tensor(
                MOD[:, l, jb * 4:(jb + 1) * 4, :], PS[:, bk2, 0:12].rearrange("p (f c) -> p f c", c=3),
                ADAB[:, l, jb * 4:(jb + 1) * 4].unsqueeze(2).to_broadcast([128, 4, 3]), op=ALU.add),
                r=[("ADAB",)], w=[pst(bk2), ("MOD",)])
            n_ada += 1
    P.barrier()

    def modv(l, mv, col):
        return MOD[:, l, mv * 8:(mv + 1) * 8, col]

    def next_sub(l, s):
        if s < 2:
            return (l, s + 1)
        if l + 1 < DEPTH:
            return (l + 1, 0)
        return None

    def emit_sv(b, l, s, last):
        w = 0.5 if s != 1 else 1.0
        nxt = None if last else next_sub(l, s)
        cur["k"] += 1
        par = cur["k"] % 2
        SV = SVS[par]
        cur["sv"] = SV
        cur["tag"] = "SVR%d" % par
        tg = lambda x: ("%s_%d" % (x, par),)
        for ci, col in enumerate((b, 2)):
            P.add("dve", lambda e, ci=ci, col=col: e.tensor_scalar(
                SV[:, ci, K_GT, :], modv(l, 3 * s + 2, col), w, None, op0=ALU.mult), w=[tg("SV"), (cur["tag"],)])
            al = ALPHA if nxt is not None else 1.0
            P.add("dve", lambda e, ci=ci, al=al: e.tensor_scalar(
                SV[:, ci, K_GA, :], LNG[:, l, s, :], al, None, op0=ALU.mult), w=[tg("SV")])
            P.add("dve", lambda e, ci=ci, al=al: e.tensor_scalar(
                SV[:, ci, K_BA, :], LNB[:, l, s, :], al, None, op0=ALU.mult), w=[tg("SV")])
            if nxt is not None:
                nl, ns = nxt
                P.add("dve", lambda e, ci=ci, col=col, nl=nl, ns=ns: e.tensor_scalar(
                    SV[:, ci, K_TMP, :], modv(nl, 3 * ns + 1, col), 1.0, None, op0=ALU.add), w=[tg("SV")])
                P.add("dve", lambda e, ci=ci: e.tensor_tensor(
                    SV[:, ci, K_GP, :], LNG[:, l, s, :], SV[:, ci, K_TMP, :], op=ALU.mult),
                    r=[tg("SV")], w=[tg("SV2")])
                P.add("dve", lambda e, ci=ci: e.tensor_tensor(
                    SV[:, ci, K_BP, :], LNB[:, l, s, :], SV[:, ci, K_TMP, :], op=ALU.mult),
                    r=[tg("SV")], w=[tg("SV3")])
                P.add("dve", lambda e, ci=ci, col=col, nl=nl, ns=ns: e.tensor_tensor(
                    SV[:, ci, K_BP, :], SV[:, ci, K_BP, :], modv(nl, 3 * ns, col), op=ALU.add),
                    r=[tg("SV3")], w=[tg("SV4")])
        tag = cur["tag"]
        P.add("dve", lambda e: e.memset(SV[:, 0, K_TMP, 0:1], 0.0),
              r=[tg("SV"), tg("SV2"), tg("SV3"), tg("SV4")], w=[(tag,)])
        return (SV, tag)

    def svv(sv, tt, kind, c):
        ci = 1 if tt == 4 else 0
        return sv[0][:, ci, kind, c:c + 1]

    def xtok(tt):
        return [("X", c, tt) for c in range(NCH)]

    def xmtok(tt):
        return [("XM", c, tt) for c in range(NCH)]

    def emit_ln_front_a(tt, lns, mixer=False):
        t0, n = TILES[tt]
        ZB, SQ, MEAN, MSQ, VARE, RSTD = lns
        if mixer:
            P.add("act", lambda e: e.activation(out=ZB[:, :, :n], in_=X[:, :, t0:t0 + n], func=AF.Copy),
                  r=xtok(tt), w=[("ZB",)])
        else:
            P.add("dve", lambda e: e.tensor_copy(ZB[:, :, :n], X[:, :, t0:t0 + n]), r=xtok(tt), w=[("ZB",)])
        P.add("act", lambda e: e.activation(out=SQ[:, :, :n], in_=X[:, :, t0:t0 + n], func=AF.Square),
              r=xtok(tt), w=[("SQ",)])

    def emit_ln_front(tt, lns):
        emit_ln_front_a(tt, lns)
        emit_ln_front_b(tt, lns)

    def emit_ln_front_b(tt, lns, passes=True, split=5):
        t0, n = TILES[tt]
        ZB, SQ, MEAN, MSQ, VARE, RSTD = lns

        def st(e, src, bk):
            ins = None
            for c in range(NCH):
                ins = e.matmul(PS[:, bk, :n], lhsT=ONESD[:], rhs=src[:, c, :n], start=(c == 0), stop=(c == NCH - 1))
            return ins
        P.add("pe", lambda e: st(e, ZB, 4), r=[("ZB",), ("ONESD",)], w=[pst(4)])
        P.add("pe", lambda e: st(e, SQ, 5), r=[("SQ",), ("ONESD",)], w=[pst(5)])

        def a1(e):
            e.activation(out=MEAN[:, :n], in_=PS[:, 4, :n], func=AF.Copy)
            return e.activation(out=MSQ[:, :n], in_=PS[:, 4, :n], func=AF.Square)
        P.add("act", a1, w=[pst(4), ("MEAN",), ("MSQ",)])
        P.add("dve", lambda e: e.scalar_tensor_tensor(
            out=VARE[:, :n], in0=PS[:, 5, :n], scalar=EPS, in1=MSQ[:, :n], op0=ALU.add, op1=ALU.subtract),
            r=[("MSQ",)], w=[pst(5), ("VARE",)])

        P.add("act", lambda e: e.activation(out=VARE[:, :n], in_=VARE[:, :n], func=AF.Ln),
              r=[("VARE",)], w=[("SD",)])
        P.add("act", lambda e: e.activation(out=RSTD[:, :n], in_=VARE[:, :n], func=AF.Exp, scale=-0.5),
              r=[("SD",)], w=[("RSTD",), ("VARE",)])

        if not passes:
            return
        emit_ln_front_b2(tt, lns, split)

    def emit_ln_front_b2(tt, lns, split=5):
        t0, n = TILES[tt]
        ZB, SQ, MEAN, MSQ, VARE, RSTD = lns
        for eng, c0, c1 in (("dve", 0, split), ("pool", split, 8)):
            toks = [("X", c, tt) for c in range(c0, c1)]
            P.add(eng, lambda e, c0=c0, c1=c1: e.tensor_tensor(
                X[:, c0:c1, t0:t0 + n], X[:, c0:c1, t0:t0 + n],
                MEAN[:, None, :n].to_broadcast([128, c1 - c0, n]), op=ALU.subtract),
                r=[("MEAN",)] + toks, w=toks)
            P.add(eng, lambda e, c0=c0, c1=c1: e.tensor_tensor(
                X[:, c0:c1, t0:t0 + n], X[:, c0:c1, t0:t0 + n],
                RSTD[:, None, :n].to_broadcast([128, c1 - c0, n]), op=ALU.mult),
                r=[("RSTD",)] + toks, w=toks)

    def emit_ln_apply(tt, sv, has_next):
        t0, n = TILES[tt]
        svr = (sv[1],)
        for c in range(NCH):
            gp, bp, ga, ba = svv(sv, tt, K_GP, c), svv(sv, tt, K_BP, c), svv(sv, tt, K_GA, c), svv(sv, tt, K_BA, c)

            if has_next:
                P.add("act", lambda e, c=c, gp=gp, bp=bp: e.activation(
                    out=XM[:, c, t0:t0 + n], in_=X[:, c, t0:t0 + n], func=AF.Identity, scale=gp, bias=bp),
                    r=[svr, ("X", c, tt)], w=[("XM", c, tt)])
            P.add("pool", lambda e, c=c, ga=ga, ba=ba: e.tensor_scalar(
                X[:, c, t0:t0 + n], X[:, c, t0:t0 + n], ga, ba, op0=ALU.mult, op1=ALU.add),
                r=[svr, ("X", c, tt)], w=[("X", c, tt)])

    def carve_lns(cz):
        ZB = cz.bf(NCH * 512, "ZB").rearrange("p (c t) -> p c t", c=NCH)
        SQ = cz.bf(NCH * 512, "SQ").rearrange("p (c t) -> p c t", c=NCH)
        MEAN = cz.f32(512, "MEAN")
        MSQ = cz.f32(512, "MSQ")
        VARE = cz.f32(512, "VARE")
        RSTD = cz.f32(512, "RSTD")
        P.ranges["SD"] = P.ranges["VARE"]
        return (ZB, SQ, MEAN, MSQ, VARE, RSTD)

    def emit_rmw(sv, tt, dc, bk):
        t0, n = TILES[tt]
        gt = svv(sv, tt, K_GT, dc)
        P.add("dve", lambda e: e.scalar_tensor_tensor(
            out=X[:, dc, t0:t0 + n], in0=PS[:, bk, :n], scalar=gt, in1=X[:, dc, t0:t0 + n],
            op0=ALU.mult, op1=ALU.add), r=[(sv[1],), ("X", dc, tt)], w=[pst(bk), ("X", dc, tt)])

    def emit_ffn(b, l, s, last):
        fi = 0 if s == 0 else 1
        nt_in = sublayer_tiles(l, s, "in")
        nt_out = sublayer_tiles(l, s, "out")
        nt = min(nt_in, nt_out)
        P.new_phase()
        lns = carve_lns(Carver(A_LOW))
        WIN, WOUT = [None, None], [None, None]
        cz = Carver(A_W)
        WIN[0] = cz.bf(NCH * 1024, "win0").rearrange("p (k n) -> p k n", k=NCH)
        WOUT[0] = cz.bf(4 * 1024, "wout0").rearrange("p (k n) -> p k n", k=4)
        assert cz.off <= A_S
        cz = Carver(A_S)
        WIN[1] = cz.bf(NCH * 1024, "win1").rearrange("p (k n) -> p k n", k=NCH)
        WOUT[1] = cz.bf(4 * 1024, "wout1").rearrange("p (k n) -> p k n", k=4)
        cz = Carver(12288)
        G = [cz.bf(4 * 512, "G%d" % i).rearrange("p (k n) -> p k n", k=4) for i in range(2)]
        SIL = [cz.f32(512, "SIL%d" % i) for i in range(2)]
        assert cz.off <= A_W
        sv = emit_sv(b, l, s, last)

        def load(j):
            slot = j % 2
            nbk = BLK[j]
            P.add("pool", lambda e: e.dma_start(
                out=WIN[slot][:, :, 0:nbk * 256],
                in_=win[l, fi, j].rearrange("p (k n) -> p k n", k=NCH)[:, :, 0:nbk * 256]),
                w=[("win%d" % slot,)], dma="win%d" % slot)
            P.add("pool", lambda e: e.dma_start(
                out=WOUT[slot][:, 0:nbk, :], in_=wout[l, fi, j].rearrange("p (k n) -> p k n", k=4)[:, 0:nbk, :]),
                w=[("wout%d" % slot,)], dma="wout%d" % slot)

        def emit_h(j, tt, gs):
            slot = j % 2
            t0, n = TILES[tt]
            nbk = BLK[j]
            for i in range(nbk):
                for half in range(2):
                    bk = 2 * (i % 2) + half
                    co = half * nbk * 128 + i * 128

                    def mm(e, bk=bk, co=co):
                        ins = None
                        for k in range(NCH):
                            ins = e.matmul(PS[:, bk, :n], lhsT=WIN[slot][:, k, co:co + 128],
                                           rhs=XM[:, k, t0:t0 + n], start=(k == 0), stop=(k == NCH - 1))
                        return ins
                    P.add("pe", mm, r=[("win%d" % slot,)] + xmtok(tt), w=[pst(bk)])
                ip = i % 2
                P.add("act", lambda e, ip=ip: e.activation(out=SIL[ip][:, :n], in_=PS[:, 2 * ip, :n], func=AF.Silu),
                      w=[pst(2 * ip), ("SIL%d" % ip,)])
                P.add("dve", lambda e, i=i, ip=ip: e.tensor_tensor(
                    G[gs][:, i, :n], PS[:, 2 * ip + 1, :n], SIL[ip][:, :n], op=ALU.mult),
                    r=[("SIL%d" % ip,)], w=[pst(2 * ip + 1), ("G%d" % gs, i)])

        def emit_y(j, tt, gs):
            slot = j % 2
            t0, n = TILES[tt]
            for dc in range(NCH):
                bk = (6 + dc % 2) if j >= NFB - 1 else (4 + dc % 4)
                nbk = BLK[j]

                def mm(e, dc=dc, bk=bk):
                    ins = None
                    for i in range(nbk):
                        ins = e.matmul(PS[:, bk, :n], lhsT=WOUT[slot][:, i, dc * 128:(dc + 1) * 128],
                                       rhs=G[gs][:, i, :n], start=(i == 0), stop=(i == nbk - 1))
                    return ins
                P.add("pe", mm, r=[("wout%d" % slot,)] + [("G%d" % gs, i) for i in range(nbk)], w=[pst(bk)])
                emit_rmw(sv, tt, dc, bk)

        load(0)
        yield "ready"
        pending = None
        cnt = 0
        qb, qc = [], []
        for j in range(NFB):
            for tt in range(nt):
                gs = cnt % 2
                cnt += 1
                emit_h(j, tt, gs)
                fb = qb.pop(0) if qb else None
                if fb is not None:
                    emit_ln_front_b(fb, lns, passes=False)
                if pending is not None:
                    emit_y(*pending)
                    if fb is not None:
                        emit_ln_front_b2(fb, lns)
                    if qc:
                        emit_ln_apply(qc.pop(0), sv, not last)
                    if fb is not None:
                        qc.append(fb)
                    if pending[0] == NFB - 1:
                        emit_ln_front_a(pending[1], lns)
                        qb.append(pending[1])
                pending = (j, tt, gs)
                if tt == 0 and j + 1 < NFB:
                    load(j + 1)
        fb = qb.pop(0) if qb else None
        if fb is not None:
            emit_ln_front_b(fb, lns, passes=False)
        emit_y(*pending)
        if fb is not None:
            emit_ln_front_b2(fb, lns)
        if qc:
            emit_ln_apply(qc.pop(0), sv, not last)
        if fb is not None:
            qc.append(fb)
        emit_ln_front_a(pending[1], lns)
        yield "hook"
        emit_ln_front_b(pending[1], lns)
        while qc:
            emit_ln_apply(qc.pop(0), sv, not last)
        emit_ln_apply(pending[1], sv, not last)

    def emit_outproj(WOg, OT, tiles_out, lns, last, final_group, sv):
        prev = None
        for tt in range(tiles_out):
            t0, n = TILES[tt]
            for dc in range(NCH):
                bk = 6 + dc % 2

                def mm(e, dc=dc, bk=bk, t0=t0, n=n):
                    ins = None
                    for i in range(2):
                        ins = e.matmul(PS[:, bk, :n], lhsT=WOg[:, i, dc * 128:(dc + 1) * 128],
                                       rhs=OT[:, i, t0:t0 + n], start=(i == 0), stop=(i == 1))
                    return ins
                P.add("pe", mm, r=[("wo",), ("OT", 0, tt), ("OT", 1, tt)], w=[pst(bk)])
                emit_rmw(sv, tt, dc, bk)
            if final_group:
                if tt >= 1:
                    emit_ln_front_b(tt - 1, lns, split=4)
                if tt >= 2:
                    emit_ln_apply(tt - 2, sv, not last)
                if tt == tiles_out - 1:
                    yield "hook"
                emit_ln_front_a(tt, lns, mixer=True)
        if final_group:
            lt = tiles_out - 1
            emit_ln_front_b(lt, lns, split=4)
            if lt >= 1:
                emit_ln_apply(lt - 1, sv, not last)
            emit_ln_apply(lt, sv, not last)

    def emit_na(b, l, last):
        jn = l // 2
        nt_in = 5
        nt_out = sublayer_tiles(l, 1, "out")
        ctxq = (l == 0)
        P.new_phase()
        lns = carve_lns(Carver(A_LOW))
        cz = Carver(A_LOW)
        QT = cz.bf(2 * T, "QT").rearrange("p (h t) -> p h t", h=2)
        P.ranges["QTZ"] = P.ranges["QT"]
        KT = cz.bf(T, "KT")
        VC = cz.bf(18 * 192, "VC").rearrange("p (c n) -> p c n", c=18)
        P.ranges["VC1"] = P.ranges["VC"]
        PT = [cz.bf(8 * 128, "PT%d" % i) for i in range(4)]
        T1 = [cz.f32(512, "T1%d" % i) for i in range(2)]
        T2 = [cz.f32(512, "T2%d" % i) for i in range(2)]
        REC = [cz.f32(256, "REC%d" % i) for i in range(2)]
        assert cz.off <= A_W
        cz = Carver(A_W)
        WQ = [cz.bf(NCH * 384, "wq%d" % i).rearrange("p (k n) -> p k n", k=NCH) for i in range(2)]
        WOg = cz.bf(2 * 1024, "wo").rearrange("p (k n) -> p k n", k=2)
        TB = cz.bf(2 * TBW, "TB").rearrange("p (h n) -> p h n", h=2)
        assert cz.off <= A_S
        cz = Carver(A_S)
        OT = cz.bf(2 * T, "OT").rearrange("p (k n) -> p k n", k=2)
        RC = cz.f32(TL, "RC")
        RS = cz.f32(TL, "RS")
        sv = emit_sv(b, l, 1, last)
        yield "ready"
        P.add("sp", lambda e: e.dma_start(out=RC, in_=ropec), w=[("RC",)], dma="ropec")
        P.add("sp", lambda e: e.dma_start(out=RS, in_=ropes), w=[("RS",)], dma="ropes")
        P.add("pool", lambda e: e.memset(VC[:, :, 64:128], 1.0), w=[("VC1",)])
        P.add("pool", lambda e: e.memset(QT[64:128, 0, :], 0.0), w=[("QTZ",)])
        P.add("pool", lambda e: e.memset(QT[0:64, 1, :], 0.0), w=[("QTZ",)])

        def rope(tt, bk, dst, scale, which):
            t0, n = TILES[tt]
            a, bq = T1[which], T2[which]

            def d1(e):
                e.scalar_tensor_tensor(out=a[:, :n], in0=PS[:, bk, :n], scalar=scale, in1=RC[:, t0:t0 + n],
                                       op0=ALU.mult, op1=ALU.mult)
                ins = None
                for q in range(4):
                    isl = slice(32 * (q ^ 1), 32 * (q ^ 1) + 32)
                    osl = slice(32 * q, 32 * q + 32)
                    ins = e.scalar_tensor_tensor(out=bq[osl, :n], in0=PS[isl, bk, :n], scalar=scale,
                                                 in1=RS[isl, t0:t0 + n], op0=ALU.mult, op1=ALU.mult)
                return ins
            P.add("dve", d1, r=[("RC",), ("RS",)], w=[pst(bk), ("T1%d" % which,), ("T2%d" % which,)])
            if which == 1:
                P.add("pool", lambda e: e.tensor_tensor(dst[:, t0:t0 + n], a[:, :n], bq[:, :n], op=ALU.add),
                      r=[("T1%d" % which,), ("T2%d" % which,)], w=[("KT", tt)])
            else:
                def qadd(e):
                    e.tensor_tensor(QT[0:64, 0, t0:t0 + n], a[0:64, :n], bq[0:64, :n], op=ALU.add)
                    return e.tensor_tensor(QT[64:128, 1, t0:t0 + n], a[64:128, :n], bq[64:128, :n], op=ALU.add)
                P.add("pool", qadd, r=[("T1%d" % which,), ("T2%d" % which,)], w=[("QT", tt)])

        ucount = [0]

        def emit_st(u):
            kind = u["kind"]
            if kind == "ctx":
                nq, q0 = 256, TL
                tcs, boff = [], None
            else:
                p = kind
                nq, q0 = 128, 128 * p
                if 2 <= p <= 13:
                    tcs, boff = list(range(p - 2, p + 3)), 0
                elif p < 2:
                    tcs, boff = [0, 1, 2, 3], 640 + 512 * p
                else:
                    tcs, boff = [12, 13, 14, 15], 640 + 512 * (p - 12)
            allc = tcs + [16, 17]
            u["allc"], u["nq"], u["q0"] = allc, nq, q0
            w2 = 2 * nq
            per = 1024 // w2
            halves = [list(range(i, min(i + per, len(allc)))) for i in range(0, len(allc), per)]
            u["pts"] = {}
            for hi, idxs in enumerate(halves):
                buf = ucount[0] % 2
                ucount[0] += 1
                ptb = (u["n"] % 2) * 2 + hi
                stv = PS[:, 2 * buf:2 * buf + 2, :].rearrange("p b (s q) -> p (b s) q", q=w2)
                ptv = PT[ptb].rearrange("p (s q) -> p s q", q=w2)

                def mm(e, idxs=idxs, stv=stv):
                    ins = None
                    for s_, i in enumerate(idxs):
                        tc = allc[i]
                        hasb = i < len(tcs)
                        ins = e.matmul(stv[:, s_, :], lhsT=KT[:, tc * 128:(tc + 1) * 128],
                                       rhs=QT[:, :, q0:q0 + nq], start=True, stop=not hasb)
                        if hasb:
                            for hh in range(2):
                                ins = e.matmul(stv[:, s_, hh * nq:(hh + 1) * nq],
                                               lhsT=TB[:, hh, boff + i * 128:boff + (i + 1) * 128],
                                               rhs=IDB[:, :], start=False, stop=(hh == 1))
                    return ins
                banks = [pst(2 * buf), pst(2 * buf + 1)]
                P.add("pe", mm, r=[("QT", t) for t in range(5)] + [("KT", t) for t in range(5)]
                      + [("TB",), ("IDB",), ("QTZ",)], w=banks)
                ns = len(idxs)
                P.add("act", lambda e, ns=ns, stv=stv, ptv=ptv: e.activation(
                    out=ptv[:, 0:ns, :], in_=stv[:, 0:ns, :], func=AF.Exp), w=banks + [("PT%d" % ptb,)])
                for s_, i in enumerate(idxs):
                    u["pts"][i] = (ptb, ptv, s_)

        def emit_pv(u, ci):
            kind = u["kind"]
            allc, nq, q0 = u["allc"], u["nq"], u["q0"]
            tt = 4 if kind == "ctx" else kind // 4
            ptbs = sorted({v[0] for v in u["pts"].values()})
            for hh in range(2):
                hoff = 64 * hh
                osl = slice(64 * hh, 64 * hh + 64)
                rsl = slice(64 * (1 - hh), 64 * (1 - hh) + 64)
                bk = 4 + 2 * (u["n"] % 2) + hh

                def mm(e, hh=hh, hoff=hoff, bk=bk):
                    ins = None
                    for i, tc in enumerate(allc):
                        _, ptv, s_ = u["pts"][i]
                        ins = e.matmul(PS[:, bk, :nq], lhsT=VC[:, tc, hoff:hoff + 128],
                                       rhs=ptv[:, s_, hh * nq:(hh + 1) * nq],
                                       start=(i == 0), stop=(i == len(allc) - 1))
                    return ins
                P.add("pe", mm, r=[("PT%d" % x,) for x in ptbs] + [("VC1",)] + [("VC", t) for t in range(5)],
                      w=[pst(bk)])
                rb = REC[hh]

                P.add("dve", lambda e, osl=osl, rsl=rsl, bk=bk, rb=rb: e.reciprocal(rb[osl, :nq], PS[rsl, bk, :nq]),
                      w=[pst(bk), ("REC%d" % hh,)])
                P.add("dve", lambda e, osl=osl, bk=bk, rb=rb: e.tensor_tensor(
                    OT[osl, ci, q0:q0 + nq], PS[osl, bk, :nq], rb[osl, :nq], op=ALU.mult),
                    r=[("REC%d" % hh,)], w=[pst(bk), ("OT", ci, tt)])

        def load_wq(c):
            P.add("pool", lambda e: e.dma_start(
                out=WQ[c % 2], in_=wqkv[jn, c].rearrange("p (k n) -> p k n", k=NCH)),
                w=[("wq%d" % (c % 2),)], dma="wq%d" % (c % 2))
        load_wq(0)
        for c in range(8):
            slot = c % 2
            ci = c % 2
            if c + 1 < 8:
                load_wq(c + 1)
            P.add("pool", lambda e, c=c: e.dma_start(
                out=TB, in_=tbias[jn, c].rearrange("p (h n) -> p h n", h=2)), w=[("TB",)], dma="tb")
            if ci == 0:
                P.add("pool", lambda e, c=c: e.dma_start(
                    out=WOg, in_=wo[jn, c // 2].rearrange("p (k n) -> p k n", k=2)), w=[("wo",)], dma="wo")
            for tt in range(nt_in):
                t0, n = TILES[tt]
                for which in range(2):
                    def mm(e, which=which, t0=t0, n=n, slot=slot):
                        ins = None
                        for k in range(NCH):
                            ins = e.matmul(PS[:, which, :n], lhsT=WQ[slot][:, k, which * 128:(which + 1) * 128],
                                           rhs=XM[:, k, t0:t0 + n], start=(k == 0), stop=(k == NCH - 1))
                        return ins
                    P.add("pe", mm, r=[("wq%d" % slot,)] + xmtok(tt), w=[pst(which)])
                    dst = QT if which == 0 else KT
                    scale = 0.125 if which == 0 else 1.0
                    if tt < 4:
                        rope(tt, which, dst, scale, which)
                    elif which == 1:
                        P.add("act", lambda e, which=which, dst=dst, scale=scale, t0=t0, n=n: e.activation(
                            out=dst[:, t0:t0 + n], in_=PS[:, which, :n], func=AF.Copy, scale=scale),
                            w=[pst(which), ("KT", tt)])
                    else:
                        def qcp(e, t0=t0, n=n, scale=scale):
                            e.activation(out=QT[0:64, 0, t0:t0 + n], in_=PS[0:64, 0, :n], func=AF.Copy, scale=scale)
                            return e.activation(out=QT[64:128, 1, t0:t0 + n], in_=PS[64:128, 0, :n],
                                                func=AF.Copy, scale=scale)
                        P.add("act", qcp, w=[pst(0), ("QT", tt)])
                ntc = n // 128

                def mmv(e, t0=t0, ntc=ntc, slot=slot):
                    ins = None
                    for tcl in range(ntc):
                        for k in range(NCH):
                            ins = e.matmul(PS[:, 2, tcl * 128:(tcl + 1) * 128],
                                           lhsT=XM[:, k, t0 + tcl * 128:t0 + (tcl + 1) * 128],
                                           rhs=WQ[slot][:, k, 256:384], start=(k == 0), stop=(k == NCH - 1))
                    return ins
                P.add("pe", mmv, r=[("wq%d" % slot,)] + xmtok(tt), w=[pst(2)])
                tc0 = t0 // 128

                def ev(e, tc0=tc0, ntc=ntc):
                    src = PS[:, 2, 0:ntc * 128].rearrange("p (c n) -> p c n", n=128)
                    e.activation(out=VC[:, tc0:tc0 + ntc, 0:64], in_=src[:, :, 0:64], func=AF.Copy)
                    return e.activation(out=VC[:, tc0:tc0 + ntc, 128:192], in_=src[:, :, 64:128], func=AF.Copy)
                P.add("act", ev, w=[pst(2), ("VC", tt)])
            if debug and c == 0 and b == 0:
                P.add("sp", lambda e: e.dma_start(out=dbg_qt, in_=QT[:, 0, :]), r=[("QT", t) for t in range(5)], dma="dbg1")
                P.add("sp", lambda e: e.dma_start(out=dbg_kt, in_=KT), r=[("KT", t) for t in range(5)], dma="dbg2")
                P.add("sp", lambda e: e.dma_start(out=dbg_vc, in_=VC.rearrange("p c n -> p (c n)")),
                      r=[("VC", t) for t in range(5)] + [("VC1",)], dma="dbg3")
                P.add("sp", lambda e: e.dma_start(out=dbg_xm, in_=XM[:]),
                      r=[t for tt in range(5) for t in xmtok(tt)], dma="dbg4")
            units = [{"kind": p} for p in range(16)]
            if ctxq:
                units.append({"kind": "ctx"})
            prev = None
            for un, u in enumerate(units):
                u["n"] = un
                emit_st(u)
                if prev is not None:
                    emit_pv(prev, ci)
                prev = u
            emit_pv(prev, ci)
            if debug and c == 0 and b == 0:
                P.add("sp", lambda e: e.dma_start(out=dbg_ot, in_=OT[:, 0, :]),
                      r=[("OT", 0, t) for t in range(5)], dma="dbg5")
            if ci == 1:
                yield from emit_outproj(WOg, OT, nt_out, lns, last, c == 7, sv)

    def emit_conv(b, l, last):
        jn = l // 2
        nt = sublayer_tiles(l, 1, "out")
        P.new_phase()
        lns = carve_lns(Carver(A_LOW))
        cz = Carver(A_W)
        WC = [cz.bf(NCH * 384, "wc%d" % i).rearrange("p (k n) -> p k n", k=NCH) for i in range(2)]
        WOg = cz.bf(2 * 1024, "wo").rearrange("p (k n) -> p k n", k=2)
        cz = Carver(A_S)
        BB = cz.bf(T, "BB")
        OT = cz.bf(2 * T, "OT").rearrange("p (k n) -> p k n", k=2)
        CG = [cz.f32(512, "CG%d" % i) for i in range(2)]
        ACC = [cz.f32(512, "ACC%d" % i) for i in range(2)]
        cz = Carver(12288)
        U = cz.f32(T + 8, "U")
        assert cz.off <= A_W
        P.ranges["Upad"] = P.ranges["U"]
        sv = emit_sv(b, l, 1, last)
        P.add("pool", lambda e: e.memset(U[:, 0:1], 0.0), w=[("Upad",)])
        P.add("pool", lambda e: e.memset(U[:, 2049:2051], 0.0), w=[("Upad",)])
        P.add("pool", lambda e: e.memset(U[:, 2307:2308], 0.0), w=[("Upad",)])

        def upos(t):
            return t + 1 if t < TL else t + 3

        def load_wc(c):
            P.add("pool", lambda e: e.dma_start(
                out=WC[c % 2], in_=wcin[jn, c].rearrange("p (k n) -> p k n", k=NCH)),
                w=[("wc%d" % (c % 2),)], dma="wq%d" % (c % 2))
        load_wc(0)
        yield "ready"
        for c in range(8):
            slot = c % 2
            ci = c % 2
            if c + 1 < 8:
                load_wc(c + 1)
            if ci == 0:
                P.add("pool", lambda e, c=c: e.dma_start(
                    out=WOg, in_=wco[jn, c // 2].rearrange("p (k n) -> p k n", k=2)), w=[("wo",)], dma="wo")
            for tt in range(nt):
                t0, n = TILES[tt]
                for which in range(3):
                    def mm(e, which=which, t0=t0, n=n, slot=slot):
                        ins = None
                        for k in range(NCH):
                            ins = e.matmul(PS[:, which, :n], lhsT=WC[slot][:, k, which * 128:(which + 1) * 128],
                                           rhs=XM[:, k, t0:t0 + n], start=(k == 0), stop=(k == NCH - 1))
                        return ins
                    P.add("pe", mm, r=[("wc%d" % slot,)] + xmtok(tt), w=[pst(which)])
                P.add("act", lambda e, t0=t0, n=n: e.activation(out=BB[:, t0:t0 + n], in_=PS[:, 0, :n], func=AF.Copy),
                      w=[pst(0), ("BB", tt)])
                cg = CG[tt % 2]
                P.add("act", lambda e, n=n, cg=cg: e.activation(out=cg[:, :n], in_=PS[:, 1, :n], func=AF.Copy),
                      w=[pst(1), ("CG%d" % (tt % 2),)])
                u0 = upos(t0)
                P.add("dve", lambda e, n=n, cg=cg, u0=u0: e.tensor_tensor(
                    U[:, u0:u0 + n], PS[:, 2, :n], cg[:, :n], op=ALU.mult),
                    r=[("CG%d" % (tt % 2),), ("Upad",)], w=[pst(2), ("U", tt)])
            for tt in range(nt):
                t0, n = TILES[tt]
                u0 = upos(t0)
                acc = ACC[tt % 2]
                if tt == 4:
                    rd = [("U", 4)]
                else:
                    rd = [("U", x) for x in range(max(tt - 1, 0), min(tt + 1, 3) + 1)]

                atok = ("ACC%d" % (tt % 2),)
                P.add("dve", lambda e, n=n, u0=u0, acc=acc, c=c: e.tensor_scalar(
                    acc[:, :n], U[:, u0 - 1:u0 - 1 + n], CONVW[:, jn, 0, c:c + 1], None, op0=ALU.mult),
                    r=rd + [("Upad",)], w=[atok])
                for tap in (1, 2):
                    P.add("dve", lambda e, n=n, u0=u0, acc=acc, c=c, tap=tap: e.scalar_tensor_tensor(
                        out=acc[:, :n], in0=U[:, u0 - 1 + tap:u0 - 1 + tap + n], scalar=CONVW[:, jn, tap, c:c + 1],
                        in1=acc[:, :n], op0=ALU.mult, op1=ALU.add), r=rd + [("Upad",), atok], w=[atok])
                P.add("pool", lambda e, t0=t0, n=n, acc=acc, ci=ci: e.tensor_tensor(
                    OT[:, ci, t0:t0 + n], acc[:, :n], BB[:, t0:t0 + n], op=ALU.mult),
                    r=[("ACC%d" % (tt % 2),), ("BB", tt)], w=[("OT", ci, tt)])
            if ci == 1:
                yield from emit_outproj(WOg, OT, nt, lns, last, c == 7, sv)

    subs = [(l, s) for l in range(DEPTH) for s in range(3)]
    if stop is not None:
        subs = subs[:subs.index(tuple(stop)) + 1]

    for b in range(nb):
        for tt in range(5):
            t0, n = TILES[tt]
            P.add("sp", lambda e, b=b, t0=t0, n=n: e.dma_start(out=X[:, :, t0:t0 + n], in_=xin[b][:, :, t0:t0 + n]),
                  w=xtok(tt), dma="xin%d" % tt)
        for ci, col in enumerate((b, 2)):
            P.add("dve", lambda e, ci=ci, col=col: e.tensor_scalar(
                SVI[:, ci, :], modv(0, 1, col), 1.0, None, op0=ALU.add), w=[("SVI",)])
        for tt in range(5):
            t0, n = TILES[tt]
            ci = 1 if tt == 4 else 0
            col = 2 if tt == 4 else b
            for c in range(NCH):
                P.add("pool", lambda e, c=c, t0=t0, n=n, ci=ci, col=col: e.tensor_scalar(
                    XM[:, c, t0:t0 + n], X[:, c, t0:t0 + n], SVI[:, ci, c:c + 1], MOD[:, 0, c, col:col + 1],
                    op0=ALU.mult, op1=ALU.add), r=[("X", c, tt), ("SVI",)], w=[("XM", c, tt)])
                P.add("act", lambda e, c=c, t0=t0, n=n: e.activation(
                    out=X[:, c, t0:t0 + n], in_=X[:, c, t0:t0 + n], func=AF.Copy, scale=ALPHA),
                    w=[("X", c, tt)])
        gens = []
        for (l, s) in subs:
            last = (l, s) == subs[-1]
            if s != 1:
                gens.append(("ffn", emit_ffn(b, l, s, last)))
            elif l % 2 == 0:
                gens.append(("na", emit_na(b, l, last)))
            else:
                gens.append(("conv", emit_conv(b, l, last)))
        assert next(gens[0][1]) == "ready"
        for gi, (kind, g) in enumerate(gens):
            assert next(g) == "hook"
            nxt = gens[gi + 1] if gi + 1 < len(gens) else None
            hoist = nxt is not None and nxt[0] != "na"
            if hoist:
                assert next(nxt[1]) == "ready"
            for _ in g:
                raise AssertionError("unexpected extra yield")
            if nxt is not None and not hoist:
                assert next(nxt[1]) == "ready"
        for tt in range(4):
            t0, n = TILES[tt]
            P.add("sp", lambda e, b=b, t0=t0, n=n: e.dma_start(out=out[b][:, :, t0:t0 + n], in_=X[:, :, t0:t0 + n]),
                  r=xtok(tt), dma="out%d_%d" % (b, tt))
        if debug and b == 0:
            P.add("sp", lambda e: e.dma_start(out=dbg_x, in_=X[:]), dma="outdbg1")
            P.add("sp", lambda e: e.dma_start(out=dbg_mod, in_=MOD[:]), dma="outdbg2")

    dma_keys = sorted({op.dma for op in P.ops if op.dma is not None})
    sems = {}
    dsems = {}
    for e in Prog.ENGS:
        cm = nc.semaphore("s_" + e)
        sems[e] = cm.__enter__()
        ctxs.append(cm)
    for k in dma_keys:
        cm = nc.semaphore("d_" + k)
        dsems[k] = cm.__enter__()
        ctxs.append(cm)
    with nc.Block() as block:
        P.emit(nc, block, sems, dsems)
    return nc, P


def _fm(a2d):
    n = a2d.shape[1]
    return np.ascontiguousarray(a2d.reshape(NCH, 128, n).transpose(1, 0, 2))


def _bias_tables(rpb):
    out = np.full((16, 128, TBW), NEG, np.float32)
    qc = np.arange(64)
    wc0 = np.clip(qc - 8, 0, 48)
    kc = np.arange(64)
    colok = (kc[None, :] >= wc0[:, None]) & (kc[None, :] < wc0[:, None] + 16)
    dx = np.clip(kc[None, :] - qc[:, None] + 15, 0, 30)

    def fill(dst, r0, kbase, nrows, rs_of):
        for qi in range(2):
            qr = r0 + qi
            rs = rs_of(qr)
            for e in range(nrows):
                kr = kbase + e
                if rs <= kr < rs + 8:
                    dy = kr - qr + 7
                    vals = rpb[:, dy, :][:, dx]
                    blk = np.where(colok[None], vals, NEG)
                    dst[:, qi * 64:(qi + 1) * 64, e * 64:(e + 1) * 64] = blk

    rs_of = lambda r: int(np.clip(r - 4, 0, 24))
    fill(out[:, :, 0:640], 12, 8, 10, rs_of)
    fill(out[:, :, 640:1152], 0, 0, 8, rs_of)
    fill(out[:, :, 1152:1664], 2, 0, 8, rs_of)
    fill(out[:, :, 1664:2176], 28, 24, 8, rs_of)
    fill(out[:, :, 2176:2688], 30, 24, 8, rs_of)
    return np.ascontiguousarray(out.reshape(8, 2, 128, TBW).transpose(0, 2, 1, 3)).reshape(8, 128, 2 * TBW)


def _rope_tables():
    t = np.arange(TL)
    nf = 16
    inv = 10000.0 ** (-np.arange(nf) / nf)
    ang_r = (t // GRID_W)[:, None] * inv[None]
    ang_c = (t % GRID_W)[:, None] * inv[None]
    cr, sr = np.cos(ang_r).astype(np.float32), np.sin(ang_r).astype(np.float32)
    cc, sc = np.cos(ang_c).astype(np.float32), np.sin(ang_c).astype(np.float32)
    C = np.zeros((128, TL), np.float32)
    S = np.zeros((128, TL), np.float32)
    for p in range(128):
        n = p % 64
        m = n % 32
        cs, sn = (cr[:, m], sr[:, m]) if m < 16 else (cc[:, m - 16], sc[:, m - 16])
        C[p] = cs
        S[p] = sn if n < 32 else -sn
    return C, S


_PERM = np.concatenate([np.arange(0, 16), np.arange(32, 48), np.arange(16, 32), np.arange(48, 64)])


def prepare_inputs(x, c, ctx, c_ctx, ada_w, ada_b, ln_g, ln_b, ffn_w_in, ffn_w_out,
                   na_w_qkv, na_w_out, na_rpb, sc_w_in, sc_conv, sc_w_out):
    f = lambda a: np.asarray(a, np.float32)
    x, c, ctx, c_ctx = f(x), f(c), f(ctx), f(c_ctx)
    ada_w, ada_b, ln_g, ln_b = f(ada_w), f(ada_b), f(ln_g), f(ln_b)
    ffn_w_in, ffn_w_out = f(ffn_w_in), f(ffn_w_out)
    na_w_qkv, na_w_out, na_rpb = f(na_w_qkv), f(na_w_out), f(na_rpb)
    sc_w_in, sc_conv, sc_w_out = f(sc_w_in), f(sc_conv), f(sc_w_out)
    sh = {}
    a = ada_w.reshape(DEPTH, NCH, 128, 18, 512).transpose(0, 3, 2, 1, 4)
    sh["adaw"] = np.ascontiguousarray(a).reshape(DEPTH, 18, 128, NCH * 512)
    sh["adab"] = np.ascontiguousarray(ada_b.reshape(DEPTH, 72, 128).transpose(2, 0, 1))
    sh["lng"] = np.ascontiguousarray(ln_g.reshape(DEPTH, 3, NCH, 128).transpose(3, 0, 1, 2))
    sh["lnb"] = np.ascontiguousarray(ln_b.reshape(DEPTH, 3, NCH, 128).transpose(3, 0, 1, 2))
    winb = np.zeros((DEPTH, 2, NFB, 128, NCH, 1024), np.float32)
    woutb = np.zeros((DEPTH, 2, NFB, 128, 4, 1024), np.float32)
    wi5 = ffn_w_in.reshape(DEPTH, 2, NCH, 128, 2 * FF)
    wo5 = ffn_w_out.reshape(DEPTH, 2, FF // 128, 128, D)
    for j in range(NFB):
        nbk, o = BLK[j], BOFF[j] * 128
        a = wi5[..., o:o + nbk * 128]
        u = wi5[..., FF + o:FF + o + nbk * 128]
        blk = np.concatenate([a, u], axis=-1)
        winb[:, :, j, :, :, 0:2 * nbk * 128] = blk.transpose(0, 1, 3, 2, 4)
        woutb[:, :, j, :, 0:nbk, :] = wo5[:, :, BOFF[j]:BOFF[j] + nbk].transpose(0, 1, 3, 2, 4)
    sh["win"] = winb.reshape(DEPTH, 2, NFB, 128, NCH * 1024)
    sh["wout"] = woutb.reshape(DEPTH, 2, NFB, 128, 4 * 1024)
    cols = []
    for cch in range(8):
        qcols = np.concatenate([(2 * cch) * 64 + _PERM, (2 * cch + 1) * 64 + _PERM])
        cols.append(np.concatenate([qcols, 1024 + qcols, 2048 + cch * 128 + np.arange(128)]))
    cols = np.stack(cols)
    wq = na_w_qkv[:, :, cols]
    wq = wq.reshape(2, NCH, 128, 8, 384).transpose(0, 3, 2, 1, 4)
    sh["wqkv"] = np.ascontiguousarray(wq).reshape(2, 8, 128, NCH * 384)
    w2 = na_w_out.reshape(2, 4, 2, 128, D).transpose(0, 1, 3, 2, 4)
    sh["wo"] = np.ascontiguousarray(w2).reshape(2, 4, 128, 2 * 1024)
    sh["tbias"] = np.stack([_bias_tables(na_rpb[j]) for j in range(2)])
    ccols = np.stack([np.concatenate([cch * 128 + np.arange(128), 1024 + cch * 128 + np.arange(128),
                                      2048 + cch * 128 + np.arange(128)]) for cch in range(8)])
    wc = sc_w_in[:, :, ccols].reshape(2, NCH, 128, 8, 384).transpose(0, 3, 2, 1, 4)
    sh["wcin"] = np.ascontiguousarray(wc).reshape(2, 8, 128, NCH * 384)
    w3 = sc_w_out.reshape(2, 4, 2, 128, D).transpose(0, 1, 3, 2, 4)
    sh["wco"] = np.ascontiguousarray(w3).reshape(2, 4, 128, 2 * 1024)
    sh["convw"] = np.ascontiguousarray(sc_conv.reshape(2, 3, NCH, 128).transpose(3, 0, 1, 2))
    C, S = _rope_tables()
    sh["ropec"], sh["ropes"] = C, S
    sh["ident"] = np.eye(128, dtype=np.float32)
    in_maps = []
    for i in range(8):
        m = dict(sh)
        xs = []
        for bb in range(2):
            full = np.concatenate([x[2 * i + bb], ctx[2 * i + bb]], axis=0)
            xs.append(_fm(np.ascontiguousarray(full.T)))
        m["xin"] = np.stack(xs)
        cvv = np.stack([c[2 * i], c[2 * i + 1], c_ctx], axis=1)
        m["cv"] = _fm(cvv)
        in_maps.append(m)
    return in_maps


def assemble_output(results):
    outs = []
    for i in range(8):
        o = np.asarray(results[i]["out"])
        for bb in range(2):
            outs.append(o[bb].transpose(2, 1, 0).reshape(TL, D))
    return np.ascontiguousarray(np.stack(outs)).astype(np.float32)


_CACHE = {}


def kernel(x, c, ctx, c_ctx, ada_w, ada_b, ln_g, ln_b, ffn_w_in, ffn_w_out,
           na_w_qkv, na_w_out, na_rpb, sc_w_in, sc_conv, sc_w_out):
    in_maps = prepare_inputs(x, c, ctx, c_ctx, ada_w, ada_b, ln_g, ln_b, ffn_w_in, ffn_w_out,
                             na_w_qkv, na_w_out, na_rpb, sc_w_in, sc_conv, sc_w_out)
    if "nc" not in _CACHE:
        _CACHE["nc"] = build_program()[0]
    res = run_bass_kernel_spmd(_CACHE["nc"], in_maps, core_ids=list(range(8)))
    return assemble_output(res.results)
```
